# Optimizing a Trainium2 kernel written in Bass

```python
import jax
import jax.numpy as jnp
from jax import lax
import numpy as np

D_MODEL = 2048
BATCH = 2
SEQ = 16384
DEPTH = 4

CTX_LEN = 256
GRID_W = 64
HEAD_SIZE = 64
RWKV_WIDTH = 1024
N_HEADS = RWKV_WIDTH // HEAD_SIZE
POOL_WIDTH = D_MODEL - RWKV_WIDTH
MIX_WIDTH = RWKV_WIDTH + POOL_WIDTH
POOL_WINDOWS = (2, 4, 8, 16)
POOL_GROUP = POOL_WIDTH // len(POOL_WINDOWS)
DECAY_LORA = 96
ICLR_LORA = 96
GATE_LORA = 256
CONV_WIDTH = 3
D_FF = -(-(8 * D_MODEL) // (3 * 256)) * 256
N_DIRS = 2
RMS_EPS = 1e-6
GN_EPS = 64e-5

OFF_R = 0
OFF_K = RWKV_WIDTH
OFF_V = 2 * RWKV_WIDTH
OFF_W = 3 * RWKV_WIDTH
OFF_A = OFF_W + DECAY_LORA
OFF_G = OFF_A + ICLR_LORA
OFF_POOL = OFF_G + GATE_LORA
IN_WIDTH = OFF_POOL + POOL_WIDTH

kernel_name = "hymba_rwkv7_poolformer_dit_trunk"


def rms_norm(x, g):
    xf = x.astype(jnp.float32)
    y = xf * lax.rsqrt(jnp.mean(xf * xf, axis=-1, keepdims=True) + RMS_EPS)
    return (y * g.astype(jnp.float32)).astype(x.dtype)


def modulated_norm(x, g, shift, scale):
    return rms_norm(x, g) * (1 + scale) + shift


def ada_modulation(cond, w, b):
    m = jax.nn.silu(cond) @ w + b
    return jnp.split(m, 6, axis=-1)


def swiglu(h, w_gate, w_up, w_down):
    return (jax.nn.silu(h @ w_gate) * (h @ w_up)) @ w_down


def short_conv(u, w):
    up = jnp.pad(u, ((0, 0), (1, 1), (0, 0)))
    return up[:, :-2] * w[0] + up[:, 1:-1] * w[1] + up[:, 2:] * w[2]


def wkv7_scan(r, w, k, v, a_vec, b_vec, s0, reverse):
    def step(s, inp):
        r_t, w_t, k_t, v_t, a_t, b_t = inp
        sa = jnp.einsum('bhvk,bhk->bhv', s, a_t)
        s = s * w_t[:, :, None, :] + sa[..., None] * b_t[:, :, None, :] + v_t[..., None] * k_t[:, :, None, :]
        y = jnp.einsum('bhvk,bhk->bhv', s, r_t)
        return s, y
    xs = tuple(jnp.swapaxes(z, 0, 1) for z in (r, w, k, v, a_vec, b_vec))
    s_fin, ys = lax.scan(step, s0, xs, reverse=reverse)
    return jnp.swapaxes(ys, 0, 1), s_fin


def rwkv_time_mix(u, conv_w, decay_bias, decay_up, iclr_bias, iclr_up, gate_up,
                  k_k, k_a, r_k, gn_w, gn_b, s0_fwd, s0_bwd, need_output):
    f32 = jnp.float32
    b, t, _ = u.shape
    heads = lambda z: z.reshape(b, t, N_HEADS, HEAD_SIZE)
    hvec = lambda p: p.astype(f32).reshape(N_HEADS, HEAD_SIZE)
    rkv = short_conv(u[..., :OFF_W], conv_w).astype(f32)
    r = heads(rkv[..., OFF_R:OFF_K])
    k = heads(rkv[..., OFF_K:OFF_V])
    v = heads(rkv[..., OFF_V:OFF_W])
    w_lo = jnp.tanh(u[..., OFF_W:OFF_A].astype(f32))
    a_lo = u[..., OFF_A:OFF_G].astype(f32)
    kk = k * hvec(k_k)
    kk = kk * lax.rsqrt(jnp.maximum(jnp.sum(kk * kk, axis=-1, keepdims=True), 1e-24))
    ys, ks, states = [], [], []
    for d, s0, rev in ((0, s0_fwd, False), (1, s0_bwd, True)):
        w_log = -jax.nn.softplus(-(decay_bias[d].astype(f32) + w_lo @ decay_up[d].astype(f32))) - 0.5
        decay = heads(jnp.exp(-jnp.exp(w_log)))
        a = heads(jax.nn.sigmoid(iclr_bias[d].astype(f32) + a_lo @ iclr_up[d].astype(f32)))
        k_d = k * (1 + (a - 1) * hvec(k_a))
        y_d, s_d = wkv7_scan(r, decay, k_d, v, -kk, kk * a, s0, rev)
        ys.append(y_d)
        ks.append(k_d)
        states.append(s_d)
    if not need_output:
        return None, states[0], states[1]
    y = ys[0] + ys[1]
    mu = jnp.mean(y, axis=-1, keepdims=True)
    var = jnp.mean(jnp.square(y - mu), axis=-1, keepdims=True)
    y = ((y - mu) * lax.rsqrt(var + GN_EPS)) * hvec(gn_w) + hvec(gn_b)
    bonus = jnp.sum(r * (ks[0] + ks[1]) * hvec(r_k), axis=-1, keepdims=True) * v
    g = jax.nn.sigmoid(u[..., OFF_G:OFF_POOL].astype(f32)) @ gate_up.astype(f32)
    out = (y + bonus).reshape(b, t, RWKV_WIDTH) * g
    return out.astype(u.dtype), states[0], states[1]


def window_bounds(n, window):
    i = jnp.arange(n)
    half = window // 2
    return jnp.clip(i - half, 0, n), jnp.clip(i + half, 0, n)


def box_mean_1d(x, window):
    t = x.shape[1]
    cs = jnp.pad(jnp.cumsum(x, axis=1), ((0, 0), (1, 0), (0, 0)))
    lo, hi = window_bounds(t, window)
    cnt = (hi - lo).astype(x.dtype)
    return (cs[:, hi] - cs[:, lo]) / cnt[None, :, None]


def box_mean_2d(x, window):
    rows, cols = x.shape[1], x.shape[2]
    sat = jnp.pad(jnp.cumsum(jnp.cumsum(x, axis=1), axis=2), ((0, 0), (1, 0), (1, 0), (0, 0)))
    r_lo, r_hi = window_bounds(rows, window)
    c_lo, c_hi = window_bounds(cols, window)
    top = sat[:, r_lo]
    bot = sat[:, r_hi]
    total = bot[:, :, c_hi] - bot[:, :, c_lo] - top[:, :, c_hi] + top[:, :, c_lo]
    cnt = ((r_hi - r_lo)[:, None] * (c_hi - c_lo)[None, :]).astype(x.dtype)
    return total / cnt[None, :, :, None]


def pool_mix(u, pool_w, pool_scale, on_grid):
    b, t, _ = u.shape
    uf = u.astype(jnp.float32)
    outs = []
    for gi, win in enumerate(POOL_WINDOWS):
        ug = uf[..., gi * POOL_GROUP:(gi + 1) * POOL_GROUP]
        if on_grid:
            rows = t // GRID_W
            m = box_mean_2d(ug.reshape(b, rows, GRID_W, POOL_GROUP), win).reshape(b, t, POOL_GROUP)
        else:
            m = box_mean_1d(ug, win)
        outs.append((m - ug) @ pool_w[gi].astype(jnp.float32))
    return (jnp.concatenate(outs, axis=-1) * pool_scale.astype(jnp.float32)).astype(u.dtype)


def setup_inputs(seed: int = 0) -> dict:
    key = jax.random.key(seed)
    ks = jax.random.split(key, 32)
    f32 = jnp.float32
    nrm = lambda k, shape, s: s * jax.random.normal(k, shape, f32)
    L, RW = DEPTH, RWKV_WIDTH
    conv_center = jnp.zeros((CONV_WIDTH, 1), f32).at[CONV_WIDTH // 2].set(1.0)
    return {
        'x': nrm(ks[0], (BATCH, SEQ, D_MODEL), 1.0),
        'c': nrm(ks[1], (BATCH, D_MODEL), 1.0),
        'ctx': nrm(ks[2], (BATCH, CTX_LEN, D_MODEL), 1.0),
        'c_ctx': nrm(ks[3], (D_MODEL,), 1.0),
        'ada_w': nrm(ks[4], (L, D_MODEL, 6 * D_MODEL), 0.5 * D_MODEL ** -0.5),
        'ada_b': nrm(ks[5], (L, 6 * D_MODEL), 0.02),
        'norm_mix': 1.0 + nrm(ks[6], (L, D_MODEL), 0.02),
        'norm_ffn': 1.0 + nrm(ks[7], (L, D_MODEL), 0.02),
        'w_in': nrm(ks[8], (L, D_MODEL, IN_WIDTH), D_MODEL ** -0.5),
        'conv_rkv': conv_center[None] + nrm(ks[9], (L, CONV_WIDTH, 3 * RW), 0.3),
        'decay_bias': jax.random.uniform(ks[10], (L, N_DIRS, RW), f32, -4.0, -0.5),
        'decay_up': nrm(ks[11], (L, N_DIRS, DECAY_LORA, RW), 0.1 * DECAY_LORA ** -0.5),
        'iclr_bias': nrm(ks[12], (L, N_DIRS, RW), 0.1),
        'iclr_up': nrm(ks[13], (L, N_DIRS, ICLR_LORA, RW), 0.3 * ICLR_LORA ** -0.5),
        'gate_up': nrm(ks[14], (L, GATE_LORA, RW), GATE_LORA ** -0.5),
        'k_k': 0.85 + nrm(ks[15], (L, RW), 0.05),
        'k_a': 1.0 + nrm(ks[16], (L, RW), 0.05),
        'r_k': nrm(ks[17], (L, RW), 0.1),
        'gn_w': 1.0 + nrm(ks[18], (L, RW), 0.05),
        'gn_b': nrm(ks[19], (L, RW), 0.02),
        'pool_w': nrm(ks[20], (L, len(POOL_WINDOWS), POOL_GROUP, POOL_GROUP), POOL_GROUP ** -0.5),
        'pool_scale': 1.0 + nrm(ks[21], (L, POOL_WIDTH), 0.1),
        'w_out': nrm(ks[22], (L, MIX_WIDTH, D_MODEL), MIX_WIDTH ** -0.5),
        'ffn_gate': nrm(ks[23], (L, D_MODEL, D_FF), D_MODEL ** -0.5),
        'ffn_up': nrm(ks[24], (L, D_MODEL, D_FF), D_MODEL ** -0.5),
        'ffn_down': nrm(ks[25], (L, D_FF, D_MODEL), D_FF ** -0.5),
        'final_norm': 1.0 + nrm(ks[26], (D_MODEL,), 0.02),
    }


def reference(x, c, ctx, c_ctx, ada_w, ada_b, norm_mix, norm_ffn, w_in, conv_rkv,
              decay_bias, decay_up, iclr_bias, iclr_up, gate_up, k_k, k_a, r_k,
              gn_w, gn_b, pool_w, pool_scale, w_out, ffn_gate, ffn_up, ffn_down, final_norm):
    zero_state = jnp.zeros((x.shape[0], N_HEADS, HEAD_SIZE, HEAD_SIZE), jnp.float32)
    ctx_h = ctx
    for l in range(DEPTH):
        last = l == DEPTH - 1
        rw = (conv_rkv[l], decay_bias[l], decay_up[l], iclr_bias[l], iclr_up[l], gate_up[l],
              k_k[l], k_a[l], r_k[l], gn_w[l], gn_b[l])
        sh1, sc1, gt1, sh2, sc2, gt2 = ada_modulation(c[:, None, :], ada_w[l], ada_b[l])
        csh1, csc1, cgt1, csh2, csc2, cgt2 = ada_modulation(c_ctx, ada_w[l], ada_b[l])

        u_ctx = modulated_norm(ctx_h, norm_mix[l], csh1, csc1) @ w_in[l]
        y_ctx, s_fwd, s_bwd = rwkv_time_mix(u_ctx[..., :OFF_POOL], *rw, zero_state, zero_state, not last)

        u = modulated_norm(x, norm_mix[l], sh1, sc1) @ w_in[l]
        y_lat, _, _ = rwkv_time_mix(u[..., :OFF_POOL], *rw, s_fwd, s_bwd, True)
        p_lat = pool_mix(u[..., OFF_POOL:], pool_w[l], pool_scale[l], True)
        x = x + gt1 * (jnp.concatenate([y_lat, p_lat], axis=-1) @ w_out[l])
        x = x + gt2 * swiglu(modulated_norm(x, norm_ffn[l], sh2, sc2), ffn_gate[l], ffn_up[l], ffn_down[l])

        if not last:
            p_ctx = pool_mix(u_ctx[..., OFF_POOL:], pool_w[l], pool_scale[l], False)
            ctx_h = ctx_h + cgt1 * (jnp.concatenate([y_ctx, p_ctx], axis=-1) @ w_out[l])
            ctx_h = ctx_h + cgt2 * swiglu(modulated_norm(ctx_h, norm_ffn[l], csh2, csc2),
                                          ffn_gate[l], ffn_up[l], ffn_down[l])
    return rms_norm(x, final_norm)
```

```python
import numpy as np
from contextlib import ExitStack
import concourse.bass as bass
import concourse.mybir as mybir
from concourse.bass_utils import run_bass_kernel_spmd

F32 = mybir.dt.float32
F32R = mybir.dt.float32r
AF = mybir.ActivationFunctionType
ALU = mybir.AluOpType

D = 2048; KC = 16; NTL = 4160; TT = 416; NTI = 10; CW = 104; NCH = 40
UC = 17920; CTX0 = 256; LAT0 = 1024
DFF = 5632; FC = 44
C0 = float(np.exp(-0.5))
GROUPS = [[0, 1, 2, 3], [4, 5, 6, 7]]
UCH = [(0, 128), (128, 128), (256, 128), (384, 128), (512, 128), (640, 128),
       (768, 96), (864, 96), (960, 128), (1088, 128), (1216, 128), (1344, 128)]


class Sem:
    def __init__(s, h, dma):
        s.h = h; s.dma = dma; s.total = 0


class T:
    def __init__(s, t, name):
        s.t = t; s.name = name; s.w = []; s.r = []; s.dsem = None

    def __getitem__(s, k):
        return s.t[k]


class Eng:
    def __init__(s, h, sem, name):
        s.h = h; s.sem = sem; s.cnt = 0; s.seen = {}; s.name = name


class B:
    def __init__(s):
        nc = bass.Bass("TRN2", target_bir_lowering=False)
        s.nc = nc
        s.ges = ExitStack()
        s.nsem = 0
        s.dsems = []
        s.pe = Eng(nc.tensor, s.newsem(False), "pe")
        s.act = Eng(nc.scalar, s.newsem(False), "act")
        s.dve = Eng(nc.vector, s.newsem(False), "dve")
        s.pool = Eng(nc.gpsimd, s.newsem(False), "pool")
        s.sp = Eng(nc.sync, s.newsem(False), "sp")
        s.engs = [s.pe, s.act, s.dve, s.pool, s.sp]
        s.banks = [T(s.ges.enter_context(nc.psum_tensor("bank%d" % i, [128, 512], F32)), "bank%d" % i)
                   for i in range(8)]
        s.bi = 0
        s.flip = 0

    def newsem(s, dma):
        s.nsem += 1
        sm = Sem(s.ges.enter_context(s.nc.semaphore("sm%d" % s.nsem)), dma)
        if dma:
            s.dsems.append(sm)
        return sm

    def bank(s):
        b = s.banks[s.bi]; s.bi = (s.bi + 1) % 8
        return b

    def sb(s, es, name, shape, dt=F32):
        s.nsb = getattr(s, "nsb", 0) + 1
        return T(es.enter_context(s.nc.sbuf_tensor("sb%d_%s" % (s.nsb, name), list(shape), dt)), name)

    def dram(s, name, shape, kind="Internal", dt=F32):
        return T(s.nc.dram_tensor(name, list(shape), dt, kind=kind).ap(), name)

    def _wait(s, eng, toks):
        need = {}
        for (sm, v) in toks:
            if sm.dma:
                v = sm.total
            if sm is eng.sem and eng is s.pe:
                continue
            if need.get(sm, 0) < v:
                need[sm] = v
        for sm, v in need.items():
            if eng.seen.get(sm, 0) >= v:
                continue
            eng.h.wait_ge(sm.h, v)
            eng.seen[sm] = v

    def _deps(s, reads, writes):
        toks = []
        for b in reads:
            toks += b.w
        for b in writes:
            toks += b.w; toks += b.r
        return toks

    def _upd(s, tok, reads, writes):
        for b in writes:
            b.w = [tok]; b.r = []
        for b in reads:
            if b not in writes:
                b.r = [t for t in b.r if t[0] is not tok[0]] + [tok]

    def op(s, eng, fn, reads=(), writes=()):
        s._wait(eng, s._deps(reads, writes))
        ins = fn(eng.h)
        eng.cnt += 1
        ins.then_inc(eng.sem.h, 1)
        s._upd((eng.sem, eng.cnt), reads, writes)
        return ins

    def dma(s, eng, out, in_, reads, writes, owner=None, fn=None, inc=16):
        s._wait(eng, s._deps(reads, writes))
        ow = owner or writes[0]
        if ow.dsem is None:
            cache = s.__dict__.setdefault('semcache', {})
            if ow.name not in cache:
                cache[ow.name] = s.newsem(True)
            ow.dsem = cache[ow.name]
        sm = ow.dsem
        sm.total += inc
        if fn is None:
            ins = eng.h.dma_start(out=out, in_=in_)
        else:
            ins = fn(eng.h)
        ins.then_inc(sm.h, inc)
        s._upd((sm, sm.total), reads, writes)

    def load(s, out, in_, src, dst, owner=None, r=False):
        if r:
            s.dma(s.pool, out.bitcast(F32R), in_, [src], [dst], owner)
            return
        s.dma(s.sp, out, in_, [src], [dst], owner)

    def store(s, out, in_, src, dst, owner=None):
        s.dma(s.pool, out, in_, [src], [dst], owner)

    def barrier(s):
        for e in s.engs:
            toks = [(o.sem, o.cnt) for o in s.engs if o is not e and o.cnt > 0]
            toks += [(sm, sm.total) for sm in s.dsems if sm.total > 0]
            s._wait(e, toks)

    def mm(s, out, lhsT, rhs, start, stop, reads, writes, r=False):
        if r:
            lhsT = lhsT.bitcast(mybir.dt.float32r); rhs = rhs.bitcast(mybir.dt.float32r)
        return s.op(s.pe, lambda h: h.matmul(out, lhsT=lhsT, rhs=rhs, start=start, stop=stop), reads, writes)

    def V(s, fn, reads, writes):
        return s.op(s.dve, fn, reads, writes)

    def A(s, fn, reads, writes):
        return s.op(s.act, fn, reads, writes)

    def G(s, fn, reads, writes):
        return s.op(s.pool, fn, reads, writes)

    def cp(s, out, in_, reads, writes, eng=None):
        if eng is None:
            s.flip ^= 1
            eng = s.act if s.flip else s.dve
        if eng is s.act:
            return s.A(lambda h: h.activation(out=out, in_=in_, func=AF.Copy), reads, writes)
        return s.op(eng, lambda h: h.tensor_copy(out=out, in_=in_), reads, writes)


def rsq(k, out, in_, scale, bias, reads, wT):
    k.A(lambda h: h.activation(out=out, in_=in_, func=AF.Sqrt, scale=scale, bias=bias), reads, [wT])
    k.V(lambda h: h.reciprocal(out=out, in_=out), [wT], [wT])


def bc(ap, shape):
    return ap.broadcast_to(list(shape))


def build(depth=4, dbg=None):
    k = B()
    nc = k.nc
    ges = k.ges
    L = depth

    def din(name, shape, dt=F32):
        return k.dram(name, shape, "ExternalInput", dt)

    xT_in = din("xT", [D, NTL]); cT_in = din("cT", [128, 16, 2])
    adaw_in = din("adaw", [L, D, 3072]); adab_in = din("adab", [128, L, 24])
    nrm_in = din("nrm", [128, 2, L, 16]); fin_in = din("fin", [128, 16])
    win_in = din("win", [L, 128, 16, 1472]); convw_in = din("convw", [128, L, 6, 3])
    dbias_in = din("dbias", [128, L, 2, 2]); ibias_in = din("ibias", [128, L, 2, 2])
    dup_in = din("dup", [96, L, 2, 256]); iup_in = din("iup", [96, L, 2, 256])
    gup_in = din("gup", [128, L, 2, 256]); hv_in = din("hv", [128, L, 5, 2])
    poolw_in = din("poolw", [128, L, 2, 256]); pscale_in = din("pscale", [128, L, 8])
    wout_in = din("wout", [L, 128, 10, D])
    fg_in = din("fg", [L, FC, 128, 16, 128]); fu_in = din("fu", [L, FC, 128, 16, 128]); fd_in = din("fd", [L, 16, 128, FC, 128])
    rc2_in = din("rc2", [128, 5, 2, 4, 64]); rc1_in = din("rc1", [128, 2, 256])
    out_d = k.dram("out", [D, NTL], "ExternalOutput")

    XT = k.dram("XT", [D, NTL]); HT = k.dram("HT", [NCH, D, CW]); HG = k.dram("HG", [NCH, 4 * D, CW])
    UT = k.dram("UT", [1472, UC]); YY = k.dram("YY", [4, 256, UC])
    PT = k.dram("PT", [NCH, 4 * D, CW]); DT = k.dram("DT", [NCH, D, CW])
    MODP = k.dram("MODP", [128, 192]); MODG = k.dram("MODG", [512, 192])
    dbg_t = {}
    if dbg:
        for nm, shp in dbg.items():
            dbg_t[nm] = k.dram("dbg_" + nm, shp, "ExternalOutput")

    cst = T(None, "cstgrp")

    def cload(name, src, shape):
        t = k.sb(ges, name, shape)
        k.load(t[:], src[:], src, t, owner=cst)
        return t

    convw = cload("convw", convw_in, [128, L, 6, 3]); dbias = cload("dbias", dbias_in, [128, L, 2, 2])
    ibias = cload("ibias", ibias_in, [128, L, 2, 2]); hv = cload("hv", hv_in, [128, L, 5, 2])
    pscale = cload("pscale", pscale_in, [128, L, 8]); nrm = cload("nrm", nrm_in, [128, 2, L, 16])
    fin = cload("fin", fin_in, [128, 16]); rc2 = cload("rc2", rc2_in, [128, 5, 2, 4, 64])
    rc1 = cload("rc1", rc1_in, [128, 2, 256]); adab = cload("adab", adab_in, [128, L, 24])
    cT = cload("cT", cT_in, [128, 16, 2])

    ONES = k.sb(ges, "ONES", [128, 128]); IDENT = k.sb(ges, "IDENT", [128, 128]); BONES = k.sb(ges, "BONES", [128, 128])
    MF2 = k.sb(ges, "MF2", [128, 1, 256]); MB2 = k.sb(ges, "MB2", [128, 1, 256])
    MNF = k.sb(ges, "MNF", [128, 1, 128]); MNB = k.sb(ges, "MNB", [128, 1, 128])
    RST = k.sb(ges, "RST", [128, 256]); ZERO = k.sb(ges, "ZERO", [128, 512])
    DER = k.sb(ges, "DER", [128, L, 6, 16, 2])

    k.G(lambda h: h.memset(ONES[:], 1.0), [], [ONES])
    k.G(lambda h: h.memset(ZERO[:], 0.0), [], [ZERO])
    k.G(lambda h: h.memset(IDENT[:], 1.0), [], [IDENT])
    k.G(lambda h: h.affine_select(out=IDENT[:], in_=IDENT[:], pattern=[[1, 128]], base=0, channel_multiplier=-1,
                                  compare_op=ALU.is_equal, fill=0.0), [IDENT], [IDENT])
    k.G(lambda h: h.memset(BONES[:], 0.0), [], [BONES])
    k.G(lambda h: h.memset(BONES[0:64, 0:64], 1.0), [BONES], [BONES])
    k.G(lambda h: h.memset(BONES[64:128, 64:128], 1.0), [BONES], [BONES])

    def mask(t, ap, nrep, step, cm, cmp):
        k.G(lambda h: h.memset(ap, 1.0), [t], [t])
        k.G(lambda h: h.affine_select(out=ap, in_=ap, pattern=[[0, nrep], [step, 128]], base=0, channel_multiplier=cm,
                                      compare_op=cmp, fill=0.0), [t], [t])
    mask(MF2, MF2[:, :, 0:128], 1, 1, -1, ALU.is_gt)
    mask(MF2, MF2[:, :, 128:256], 1, 1, -1, ALU.is_ge)
    mask(MB2, MB2[:, :, 0:128], 1, -1, 1, ALU.is_gt)
    mask(MB2, MB2[:, :, 128:256], 1, -1, 1, ALU.is_ge)
    mask(MNF, MNF[:], 1, -1, 1, ALU.is_gt)
    mask(MNB, MNB[:], 1, 1, -1, ALU.is_gt)
    k.G(lambda h: h.memset(RST[:], 1.0), [], [RST])
    rst4 = RST[:].rearrange("p (c t) -> p c t", t=64)
    k.G(lambda h: h.memset(rst4[:, :, 0:1], 0.0), [RST], [RST])


    def coll(kind, op, src, dst, sap=None, dap=None):
        sap = src[:] if sap is None else sap
        dap = dst[:] if dap is None else dap
        k.dma(k.pool, None, None, [src], [dst], fn=lambda h: h.collective_compute(
            kind, op, replica_groups=GROUPS, ins=[sap], outs=[dap]), inc=1)

    def cdma(load, DR, row0, nrows, sb_fn, T_sb, lt0, n, r=False):
        lt = lt0
        while lt < lt0 + n:
            j = lt // CW; a = lt % CW; ln = min(CW - a, lt0 + n - lt)
            dap = DR[j, row0:row0 + nrows, a:a + ln].rearrange("(kc p) t -> p kc t", p=128)
            sap = sb_fn(lt - lt0, ln)
            if load:
                k.load(sap, dap, DR, T_sb, r=r)
            else:
                k.store(dap, sap, T_sb, DR)
            lt += ln

    with ExitStack() as es:
        sc = k.sb(es, "sc", [128, 16, 2])
        k.A(lambda h: h.activation(out=sc[:], in_=cT[:], func=AF.Silu), [cT], [sc])
        aw = [k.sb(es, "aw%d" % i, [128, 16, 512]) for i in range(2)]
        modp = k.sb(es, "modp", [128, L, 24, 2])
        bk = k.bank()
        n = 0
        for l in range(L):
            for i4 in range(6):
                t = aw[n % 2]; n += 1
                k.load(t[:], adaw_in[l, :, i4 * 512:(i4 + 1) * 512].rearrange("(kc p) n -> p kc n", p=128), adaw_in, t)
                for jj in range(4):
                    i = i4 * 4 + jj
                    col = (l * 24 + i) * 2
                    for kc in range(16):
                        k.mm(bk[:, col:col + 2], t[:, kc, jj * 128:(jj + 1) * 128], sc[:, kc, :], kc == 0, kc == 15, [t, sc], [bk])
        bv = bk[:, 0:L * 48].rearrange("p (l i m) -> p l i m", l=L, m=2)
        k.V(lambda h: h.tensor_tensor(out=modp[:], in0=bv, in1=bc(adab[:].unsqueeze(3), [128, L, 24, 2]), op=ALU.add), [bk, adab], [modp])
        k.store(MODP[:, 0:L * 48], modp[:].rearrange("p l i m -> p (l i m)"), modp, MODP)
        coll("AllGather", ALU.bypass, MODP, MODG)
        MOD = k.sb(es, "MOD", [128, 4, 192])
        k.load(MOD[:], MODG[:].rearrange("(r p) f -> p r f", p=128), MODG, MOD)
        MODv = MOD[:, :, 0:L * 48].rearrange("p r (l i m) -> p r l i m", l=L, m=2)
        for w in range(6):
            j0 = 16 * w
            while j0 < 16 * w + 16:
                r = j0 // 24; j1 = min(16 * w + 16, (r + 1) * 24)
                k.V(lambda h, r=r, j0=j0, j1=j1, w=w: h.tensor_copy(
                    out=DER[:, :, w, j0 - 16 * w:j1 - 16 * w, :], in_=MODv[:, r, :, j0 - 24 * r:j1 - 24 * r, :]), [MOD], [DER])
                j0 = j1
        for (w, wn) in ((1, 0), (4, 1)):
            k.V(lambda h, w=w: h.tensor_scalar(out=DER[:, :, w], in0=DER[:, :, w], scalar1=1.0, scalar2=None, op0=ALU.add), [DER], [DER])
            k.V(lambda h, w=w, wn=wn: h.tensor_tensor(out=DER[:, :, w], in0=DER[:, :, w],
                                                     in1=bc(nrm[:, wn].unsqueeze(3), [128, L, 16, 2]), op=ALU.mult), [DER, nrm], [DER])
    k.barrier()

    def modulate(tile, T_, lt0, ncols, l, wg, wsh, reads, r=False):
        segs = []
        if lt0 < 64:
            segs.append((0, 64 - lt0, 1))
            segs.append((64 - lt0, ncols, 0))
        else:
            segs.append((0, ncols, 0))
        for (a, b_, m) in segs:
            n_ = b_ - a
            if wg is not None:
                k.V(lambda h, a=a, b_=b_, m=m, n_=n_: h.tensor_tensor(out=(tile[:, :, a:b_].bitcast(F32R) if r else tile[:, :, a:b_]), in0=tile[:, :, a:b_],
                    in1=bc(DER[:, l, wg, :, m:m + 1], [128, 16, n_]), op=ALU.mult), [T_, DER] + reads, [T_])
            if wsh is not None:
                k.V(lambda h, a=a, b_=b_, m=m, n_=n_: h.tensor_tensor(out=(tile[:, :, a:b_].bitcast(F32R) if r else tile[:, :, a:b_]), in0=tile[:, :, a:b_],
                    in1=bc(DER[:, l, wsh, :, m:m + 1], [128, 16, n_]), op=ALU.add), [T_, DER] + reads, [T_])

    def rms_tile(es_t, xt, sq, rs, eps=1e-6):
        k.A(lambda h: h.activation(out=sq[:], in_=xt[:], func=AF.Square), [xt], [sq])
        bk = k.bank()
        for kc in range(16):
            k.mm(bk[:, 0:TT], ONES[:], sq[:, kc, :], kc == 0, kc == 15, [ONES, sq], [bk])
        rsq(k, rs[:], bk[:, 0:TT], 1.0 / D, eps, [bk], rs)

    xrows = lambda Tn, c0, c1: Tn[:, c0:c1].rearrange("(kc p) t -> p kc t", p=128)

    def ucol(r, lt):
        return CTX0 + 64 * r + lt if lt < 64 else LAT0 + 4096 * r + (lt - 64)

    for (c0, c1) in ((0, 256), (512, 1024), (17408, 17920)):
        for r0 in range(0, 1472, 128):
            r1 = min(1472, r0 + 128)
            for cc in range(c0, c1, 512):
                ce = min(c1, cc + 512)
                k.store(UT[r0:r1, cc:ce], ZERO[0:r1 - r0, 0:ce - cc], ZERO, UT)

    XS = xT_in
    for l in range(L):
        with ExitStack() as es:
            xts = [k.sb(es, "p1x%d" % i, [128, 16, TT]) for i in range(2)]
            sq = k.sb(es, "p1sq", [128, 16, TT]); rs = k.sb(es, "p1rs", [128, TT])
            for i in range(NTI):
                xt = xts[i % 2]
                k.load(xt[:], xrows(XS, i * TT, (i + 1) * TT), XS, xt)
                rms_tile(es, xt, sq, rs)
                k.V(lambda h: h.tensor_tensor(out=xt[:], in0=xt[:], in1=bc(rs[:].unsqueeze(1), [128, 16, TT]), op=ALU.mult), [xt, rs], [xt])
                modulate(xt, xt, i * TT, TT, l, 1, 0, [])
                cdma(False, HT, 0, D, lambda off, ln, xt=xt: xt[:, :, off:off + ln], xt, i * TT, TT)
            for j in range(NCH):
                coll("AllGather", ALU.bypass, HT, HG, HT[j], HG[j])
        k.barrier()
        with ExitStack() as es:
            wsl = k.sb(es, "wsl", [128, 16, 1472])
            k.load(wsl[:], win_in[l], win_in, wsl, r=True)
            hgs = [k.sb(es, "hg%d" % i, [128, 16, TT]) for i in range(2)]
            ust = k.sb(es, "ust", [128, 12, TT])
            n = 0
            for r in range(4):
                for i in range(NTI):
                    hg = hgs[n % 2]; n += 1
                    cdma(True, HG, r * D, D, lambda off, ln, hg=hg: hg[:, :, off:off + ln], hg, i * TT, TT, r=True)
                    for j, (c0, M) in enumerate(UCH):
                        bk = k.bank()
                        for kc in range(16):
                            k.mm(bk[0:M, 0:TT], wsl[:, kc, c0:c0 + M], hg[:, kc, :], kc == 0, kc == 15, [wsl, hg], [bk], r=True)
                        fn = AF.Tanh if j == 6 else (AF.Sigmoid if j in (8, 9) else None)
                        if fn is not None:
                            k.A(lambda h, fn=fn, M=M, j=j, bk=bk: h.activation(out=ust[0:M, j, :], in_=bk[0:M, 0:TT], func=fn), [bk], [ust])
                        else:
                            k.cp(ust[0:M, j, :], bk[0:M, 0:TT], [bk], [ust])
                    lt0 = i * TT
                    segs = [(0, 64, ucol(r, 0)), (64, TT, ucol(r, 64))] if i == 0 else [(0, TT, ucol(r, lt0))]
                    for (a, b_, uc0) in segs:
                        n_ = b_ - a
                        k.store(UT[0:768, uc0:uc0 + n_].rearrange("(j p) t -> p j t", p=128), ust[:, 0:6, a:b_], ust, UT)
                        k.store(UT[768:864, uc0:uc0 + n_], ust[0:96, 6, a:b_], ust, UT)
                        k.store(UT[864:960, uc0:uc0 + n_], ust[0:96, 7, a:b_], ust, UT)
                        k.store(UT[960:1472, uc0:uc0 + n_].rearrange("(j p) t -> p j t", p=128), ust[:, 8:12, a:b_], ust, UT)
        k.barrier()
        if dbg and "UT" in dbg and l == 0:
            k.dma(k.pool, dbg_t["UT"][:], UT[:], [UT], [dbg_t["UT"]])
            k.barrier()

        with ExitStack() as es:
            dup = k.sb(es, "dup", [96, 2, 256]); iup = k.sb(es, "iup", [96, 2, 256])
            k.load(dup[:], dup_in[:, l], dup_in, dup); k.load(iup[:], iup_in[:, l], iup_in, iup)
            u6s = [k.sb(es, "u6_%d" % i, [128, 6, 258]) for i in range(2)]
            los = [k.sb(es, "lo_%d" % i, [96, 2, 256]) for i in range(2)]
            sbt = lambda nm, shp=(128, 2, 256): k.sb(es, nm, list(shp))
            rkv = sbt("rkv", (128, 6, 256)); tmp6 = sbt("tmp6", (128, 6, 256))
            kk = sbt("kk"); kksq = sbt("kksq"); kkn = sbt("kkn"); rn = sbt("rn")
            sg = sbt("sg"); cs = sbt("cs"); ex = sbt("ex"); sbb = sbt("sbb"); sbx = sbt("sbx")
            eg = sbt("eg"); eng = sbt("eng"); ega = sbt("ega"); av = sbt("av"); tB = sbt("tB"); t1 = sbt("t1"); kd = sbt("kd")
            bon = sbt("bon"); yst = sbt("yst")
            AR = [sbt("AR%d" % q, (128, 4, 256)) for q in range(2)]
            Bt = [sbt("Bt%d" % q, (128, 4, 128)) for q in range(2)]; Kt = [sbt("Kt%d" % q, (128, 4, 128)) for q in range(2)]
            Bht = [sbt("Bht%d" % q, (128, 4, 128)) for q in range(2)]; Kht = [sbt("Kht%d" % q, (128, 4, 128)) for q in range(2)]
            Vt = [sbt("Vt%d" % q, (128, 4, 128)) for q in range(2)]
            for tl in AR + Bt + Kt + Bht + Kht + Vt:
                k.G(lambda h, tl=tl: h.memset(tl[:], 0.0), [], [tl])
            MA = sbt("MA", (128, 4, 256)); MB = sbt("MB", (128, 4, 256))
            Pq = [sbt("Pq%d" % i, (128, 4, 128)) for i in range(2)]; PTq = [sbt("PTq%d" % i, (128, 4, 128)) for i in range(2)]
            Xq = [sbt("Xq%d" % i, (128, 4, 256)) for i in range(2)]
            VTk = sbt("VTk", (128, 4, 128)); BhT = sbt("BhT", (128, 4, 128)); KhT = sbt("KhT", (128, 4, 128))
            RTp = sbt("RTp", (128, 4, 128)); YL = sbt("YL", (128, 4, 128)); MCT = sbt("MCT", (128, 4, 128)); NCt = sbt("NCt", (128, 4, 128))
            Z = {}
            for q in range(2):
                for d in range(2):
                    Z[(q, d)] = [sbt("Z%d%d%d" % (q, d, i), (128, 128)) for i in range(2)]
                    z0 = Z[(q, d)][0]
                    k.G(lambda h, z=z0: h.memset(z[:], 0.0), [], [z0])
            zi = {(q, d): 0 for q in range(2) for d in range(2)}
            npass = [0]

            def bdw(dst, dstslice, fn_in, reads, eng=None):
                pass

            def scan_pass(c0, d):
                n = npass[0]; npass[0] += 1
                u6 = u6s[n % 2]; lo = los[n % 2]
                k.load(u6[:], UT[0:768, c0 - 1:c0 + 257].rearrange("(j p) t -> p j t", p=128), UT, u6)
                k.load(lo[:, 0, :], UT[768:864, c0:c0 + 256], UT, lo)
                k.load(lo[:, 1, :], UT[864:960, c0:c0 + 256], UT, lo)
                cw = lambda tap: bc(convw[:, l, :, tap:tap + 1], [128, 6, 256])
                k.V(lambda h: h.tensor_tensor(out=rkv[:], in0=u6[:, :, 0:256], in1=cw(0), op=ALU.mult), [u6, convw], [rkv])
                k.G(lambda h: h.tensor_tensor(out=tmp6[:], in0=u6[:, :, 1:257], in1=cw(1), op=ALU.mult), [u6, convw], [tmp6])
                k.V(lambda h: h.tensor_tensor(out=rkv[:], in0=rkv[:], in1=tmp6[:], op=ALU.add), [rkv, tmp6], [rkv])
                k.G(lambda h: h.tensor_tensor(out=tmp6[:], in0=u6[:, :, 2:258], in1=cw(2), op=ALU.mult), [u6, convw], [tmp6])
                k.V(lambda h: h.tensor_tensor(out=rkv[:], in0=rkv[:], in1=tmp6[:], op=ALU.add), [rkv, tmp6], [rkv])
                R_ = rkv[:, 0:2, :]; K_ = rkv[:, 2:4, :]; V_ = rkv[:, 4:6, :]
                hvb = lambda w: bc(hv[:, l, w, :].unsqueeze(2), [128, 2, 256])
                k.V(lambda h: h.tensor_tensor(out=kk[:], in0=K_, in1=hvb(0), op=ALU.mult), [rkv, hv], [kk])
                k.G(lambda h: h.tensor_tensor(out=kksq[:], in0=kk[:], in1=kk[:], op=ALU.mult), [kk], [kksq])
                bk = k.bank()
                for q in range(2):
                    k.mm(bk[:, q * 256:(q + 1) * 256], BONES[:], kksq[:, q, :], True, True, [BONES, kksq], [bk])
                k.V(lambda h: h.tensor_scalar(out=rn[:].rearrange("p q t -> p (q t)"), in0=bk[:, 0:512], scalar1=1e-24, scalar2=None,
                                              op0=ALU.max), [bk], [rn])
                rsq(k, rn[:], rn[:], 1.0, 0.0, [rn], rn)
                k.V(lambda h: h.tensor_tensor(out=kkn[:], in0=kk[:], in1=rn[:], op=ALU.mult), [kk, rn], [kkn])
                k.V(lambda h: h.tensor_scalar(out=kksq[:], in0=kkn[:], scalar1=-1.0, scalar2=None, op0=ALU.mult), [kkn, kksq], [kksq])
                bk = k.bank(); bk2 = k.bank()
                for q in range(2):
                    k.mm(bk[:, q * 256:(q + 1) * 256], dup[:, d, q * 128:(q + 1) * 128], lo[:, 0, :], True, True, [dup, lo], [bk])
                    k.mm(bk2[:, q * 256:(q + 1) * 256], iup[:, d, q * 128:(q + 1) * 128], lo[:, 1, :], True, True, [iup, lo], [bk2])
                for q in range(2):
                    k.A(lambda h, q=q: h.activation(out=sg[:, q, :], in_=bk[:, q * 256:(q + 1) * 256], func=AF.Sigmoid,
                                                    bias=dbias[:, l, d, q:q + 1], scale=1.0), [bk, dbias], [sg])
                    k.A(lambda h, q=q: h.activation(out=av[:, q, :], in_=bk2[:, q * 256:(q + 1) * 256], func=AF.Sigmoid,
                                                    bias=ibias[:, l, d, q:q + 1], scale=1.0), [bk2, ibias], [av])
                for q in range(2):
                    k.V(lambda h, q=q: h.tensor_tensor_scan(out=cs[:, q, :], data0=RST[:], data1=sg[:, q, :], initial=0.0,
                                                            op0=ALU.mult, op1=ALU.add), [RST, sg], [cs])
                k.V(lambda h: h.tensor_tensor(out=ex[:], in0=cs[:], in1=sg[:], op=ALU.subtract), [cs, sg], [ex])
                if d == 0:
                    Sin, Sex, Tin, Tex = cs, ex, cs, ex
                    gcol = 63
                else:
                    cs5 = cs[:].rearrange("p q (c t) -> p q c t", t=64)
                    k.V(lambda h: h.tensor_tensor(out=sbb[:].rearrange("p q (c t) -> p q c t", t=64),
                                                  in0=bc(cs5[:, :, :, 63:64], [128, 2, 4, 64]),
                                                  in1=ex[:].rearrange("p q (c t) -> p q c t", t=64), op=ALU.subtract), [cs, ex], [sbb])
                    k.V(lambda h: h.tensor_tensor(out=sbx[:], in0=sbb[:], in1=sg[:], op=ALU.subtract), [sbb, sg], [sbx])
                    Sin, Sex, Tin, Tex = sbb[:], sbx[:], sbb, sbx
                    gcol = 0
                SinA = Sin[:] if d == 0 else Sin
                SexA = Sex[:] if d == 0 else Sex
                k.A(lambda h: h.activation(out=eg[:], in_=SinA, func=AF.Exp, scale=-C0), [Tin], [eg])
                k.A(lambda h: h.activation(out=eng[:], in_=SinA, func=AF.Exp, scale=C0), [Tin], [eng])
                k.A(lambda h: h.activation(out=ega[:], in_=SexA, func=AF.Exp, scale=-C0), [Tex], [ega])
                k.V(lambda h: h.tensor_tensor(out=t1[:], in0=av[:], in1=hvb(1), op=ALU.mult), [av, hv], [t1])
                k.V(lambda h: h.tensor_tensor(out=t1[:], in0=t1[:], in1=hvb(1), op=ALU.subtract), [t1, hv], [t1])
                k.V(lambda h: h.tensor_scalar(out=t1[:], in0=t1[:], scalar1=1.0, scalar2=None, op0=ALU.add), [t1], [t1])
                k.V(lambda h: h.tensor_tensor(out=kd[:], in0=t1[:], in1=K_, op=ALU.mult), [t1, rkv], [kd])
                k.G(lambda h: h.tensor_tensor(out=tB[:], in0=av[:], in1=eng[:], op=ALU.mult), [av, eng], [tB])
                k.G(lambda h: h.tensor_tensor(out=t1[:], in0=kd[:], in1=hvb(2), op=ALU.mult), [kd, hv, t1], [t1])
                k.G(lambda h: h.tensor_tensor(out=t1[:], in0=t1[:], in1=R_, op=ALU.mult), [t1, rkv], [t1])
                bk = k.bank()
                for q in range(2):
                    k.mm(bk[:, q * 256:(q + 1) * 256], BONES[:], t1[:, q, :], True, True, [BONES, t1], [bk])
                k.V(lambda h: h.tensor_tensor(out=bon[:].rearrange("p q t -> p (q t)"), in0=bk[:, 0:512],
                                              in1=rkv[:, 4:6, :].rearrange("p q t -> p (q t)"), op=ALU.mult), [bk, rkv], [bon])
                k.store(YY[2 + d, :, c0:c0 + 256].rearrange("(q p) t -> p q t", p=128), bon[:], bon, YY)
                egv = eg[:].rearrange("p q (c t) -> p q c t", t=64)
                ei = 0
                for q in range(2):
                    for hh in range(2):
                        ps = slice(hh * 64, hh * 64 + 64)
                        v4 = lambda tl, lo_=0: tl[ps, :, lo_ + hh * 64:lo_ + hh * 64 + 64]
                        s4 = lambda tl: tl[ps, q, :].rearrange("p (c t) -> p c t", t=64)
                        r4 = lambda j: rkv[ps, j * 2 + q, :].rearrange("p (c t) -> p c t", t=64)
                        gam = bc(egv[ps, q, :, gcol:gcol + 1], [64, 4, 64])
                        E = [k.V, k.G]
                        def em(fn, reads, writes):
                            nonlocal ei
                            ei += 1
                            E[ei % 2](fn, reads, writes)
                        em(lambda h: h.tensor_tensor(out=v4(AR[q]), in0=s4(kksq), in1=s4(ega), op=ALU.mult), [kksq, ega], [AR[q]])
                        em(lambda h: h.tensor_tensor(out=v4(AR[q], 128), in0=r4(0), in1=s4(eg), op=ALU.mult), [rkv, eg], [AR[q]])
                        em(lambda h: h.tensor_tensor(out=v4(Bt[q]), in0=s4(kkn), in1=s4(tB), op=ALU.mult), [kkn, tB], [Bt[q]])
                        em(lambda h: h.tensor_tensor(out=v4(Kt[q]), in0=s4(kd), in1=s4(eng), op=ALU.mult), [kd, eng], [Kt[q]])
                        em(lambda h: h.tensor_tensor(out=v4(Bht[q]), in0=v4(Bt[q]), in1=gam, op=ALU.mult), [Bt[q], eg], [Bht[q]])
                        em(lambda h: h.tensor_tensor(out=v4(Kht[q]), in0=v4(Kt[q]), in1=gam, op=ALU.mult), [Kt[q], eg], [Kht[q]])
                        em(lambda h: h.tensor_copy(out=v4(Vt[q]), in_=r4(2)), [rkv], [Vt[q]])
                M2 = MF2 if d == 0 else MB2
                MN = MNF if d == 0 else MNB
                order = [0, 1, 2, 3] if d == 0 else [3, 2, 1, 0]
                for q in range(2):
                    for (lt, dst) in ((Bt[q], MA), (Kt[q], MB)):
                        for hf in range(2):
                            bk = k.bank()
                            for cc in range(2):
                                c = hf * 2 + cc
                                k.mm(bk[:, cc * 256:(cc + 1) * 256], lt[:, c, :], AR[q][:, c, :], True, True, [lt, AR[q]], [bk])
                            k.V(lambda h, bk=bk, hf=hf, dst=dst: h.tensor_tensor(out=dst[:, hf * 2:hf * 2 + 2, :],
                                in0=bk[:, 0:512].rearrange("p (c n) -> p c n", n=256), in1=bc(M2[:], [128, 2, 256]), op=ALU.mult), [bk, M2], [dst])
                    P0 = Pq[0]
                    bk = k.bank()
                    for c in range(4):
                        k.mm(bk[:, c * 128:(c + 1) * 128], AR[q][:, c, 0:128], Bt[q][:, c, :], True, True, [AR[q], Bt[q]], [bk])
                    k.V(lambda h, bk=bk: h.tensor_tensor(out=P0[:], in0=bk[:, 0:512].rearrange("p (c n) -> p c n", n=128), in1=bc(MN[:], [128, 4, 128]), op=ALU.mult), [bk, MN], [P0])
                    X = Xq[0]
                    def tr(src_ap_fn, srcT, dst_ap, dstT):
                        bk = k.bank()
                        for c in range(4):
                            k.mm(bk[:, c * 128:(c + 1) * 128], src_ap_fn(c), IDENT[:], True, True, [srcT, IDENT], [bk])
                        k.cp(dst_ap, bk[:, 0:512].rearrange("p (c n) -> p c n", n=128), [bk], [dstT])
                    tr(lambda c: AR[q][:, c, 0:128], AR[q], X[:, :, 0:128], X)
                    tr(lambda c: Vt[q][:, c, :], Vt[q], VTk[:], VTk)
                    tr(lambda c: Bht[q][:, c, :], Bht[q], BhT[:], BhT)
                    tr(lambda c: Kht[q][:, c, :], Kht[q], KhT[:], KhT)
                    bk = k.bank()
                    for c in range(4):
                        k.mm(bk[:, c * 128:(c + 1) * 128], MB[:, c, 0:128], VTk[:, c, :], True, True, [MB, VTk], [bk])
                    k.cp(X[:, :, 128:256], bk[:, 0:512].rearrange("p (c n) -> p c n", n=128), [bk], [X])
                    xi = 0; pi = 0
                    Pc = P0; PTc_ap = lambda c: MA[:, c, 0:128]; PTc_T = MA
                    for it in range(6):
                        Xn = Xq[1 - xi]
                        for hf in range(2):
                            bk = k.bank()
                            for cc in range(2):
                                c = hf * 2 + cc
                                k.mm(bk[:, cc * 256:(cc + 1) * 256], PTc_ap(c), Xq[xi][:, c, :], True, True, [PTc_T, Xq[xi]], [bk])
                            k.V(lambda h, bk=bk, hf=hf, Xn=Xn, Xo=Xq[xi]: h.tensor_tensor(out=Xn[:, hf * 2:hf * 2 + 2, :],
                                in0=bk[:, 0:512].rearrange("p (c n) -> p c n", n=256), in1=Xo[:, hf * 2:hf * 2 + 2, :], op=ALU.add), [bk, Xq[xi]], [Xn])
                        xi = 1 - xi
                        if it < 5:
                            Pn = Pq[1 - pi]; PTn = PTq[1 - pi]
                            bk = k.bank(); bk2 = k.bank()
                            for c in range(4):
                                k.mm(bk[:, c * 128:(c + 1) * 128], PTc_ap(c), Pc[:, c, :], True, True, [PTc_T, Pc], [bk])
                                k.mm(bk2[:, c * 128:(c + 1) * 128], Pc[:, c, :], PTc_ap(c), True, True, [PTc_T, Pc], [bk2])
                            k.cp(Pn[:], bk[:, 0:512].rearrange("p (c n) -> p c n", n=128), [bk], [Pn])
                            k.cp(PTn[:], bk2[:, 0:512].rearrange("p (c n) -> p c n", n=128), [bk2], [PTn])
                            pi = 1 - pi
                            Pc = Pn; PTc_T = PTn; PTc_ap = (lambda c, PTn=PTn: PTn[:, c, :])
                    X = Xq[xi]
                    bk = k.bank(); bk2 = k.bank()
                    for c in range(4):
                        k.mm(bk[:, c * 128:(c + 1) * 128], X[:, c, 0:128], MA[:, c, 128:256], True, True, [X, MA], [bk])
                    for c in range(4):
                        k.mm(bk2[:, c * 128:(c + 1) * 128], X[:, c, 128:256], MA[:, c, 128:256], True, False, [X, MA], [bk2])
                        k.mm(bk2[:, c * 128:(c + 1) * 128], VTk[:, c, :], MB[:, c, 128:256], False, True, [VTk, MB], [bk2])
                    k.V(lambda h, bk=bk: h.tensor_tensor(out=RTp[:], in0=bk[:, 0:512].rearrange("p (c n) -> p c n", n=128),
                                                         in1=AR[q][:, :, 128:256], op=ALU.add), [bk, AR[q]], [RTp])
                    k.cp(YL[:], bk2[:, 0:512].rearrange("p (c n) -> p c n", n=128), [bk2], [YL])
                    bk = k.bank(); bk2 = k.bank()
                    for c in range(4):
                        k.mm(bk[:, c * 128:(c + 1) * 128], X[:, c, 0:128], BhT[:, c, :], True, True, [X, BhT], [bk])
                    for c in range(4):
                        k.mm(bk2[:, c * 128:(c + 1) * 128], BhT[:, c, :], X[:, c, 128:256], True, False, [X, BhT], [bk2])
                        k.mm(bk2[:, c * 128:(c + 1) * 128], KhT[:, c, :], VTk[:, c, :], False, True, [KhT, VTk], [bk2])
                    for c in range(4):
                        k.V(lambda h, c=c, bk=bk: h.scalar_tensor_tensor(out=MCT[:, c, :], in0=IDENT[:], scalar=eg[:, q, c * 64 + gcol:c * 64 + gcol + 1],
                            in1=bk[:, c * 128:(c + 1) * 128], op0=ALU.mult, op1=ALU.add), [IDENT, eg, bk], [MCT])
                    k.cp(NCt[:], bk2[:, 0:512].rearrange("p (c n) -> p c n", n=128), [bk2], [NCt])
                    for c in order:
                        zc = Z[(q, d)][zi[(q, d)]]; zn = Z[(q, d)][1 - zi[(q, d)]]
                        bk = k.bank(); bk2 = k.bank()
                        k.mm(bk[:, 0:128], zc[:], RTp[:, c, :], True, True, [zc, RTp], [bk])
                        k.mm(bk2[:, 0:128], MCT[:, c, :], zc[:], True, True, [zc, MCT], [bk2])
                        for hh in range(2):
                            ps = slice(hh * 64, hh * 64 + 64)
                            k.V(lambda h, ps=ps, hh=hh, c=c, bk=bk: h.tensor_tensor(out=yst[ps, q, c * 64:(c + 1) * 64],
                                in0=bk[ps, hh * 64:hh * 64 + 64], in1=YL[ps, c, hh * 64:hh * 64 + 64], op=ALU.add), [bk, YL], [yst])
                        k.V(lambda h, c=c, bk2=bk2, zn=zn: h.tensor_tensor(out=zn[:], in0=bk2[:, 0:128], in1=NCt[:, c, :], op=ALU.add), [bk2, NCt], [zn])
                        zi[(q, d)] = 1 - zi[(q, d)]
                k.store(YY[d, :, c0:c0 + 256].rearrange("(q p) t -> p q t", p=128), yst[:], yst, YY)

            fseq = [CTX0] + [LAT0 + 256 * i for i in range(64)]
            bseq = [CTX0] + [LAT0 + 256 * i for i in reversed(range(64))]
            for i in range(65):
                scan_pass(fseq[i], 0)
                scan_pass(bseq[i], 1)
        k.barrier()
        if dbg and "YY" in dbg and l == 0:
            k.dma(k.pool, dbg_t["YY"][:], YY[:], [YY], [dbg_t["YY"]])
            k.barrier()

        with ExitStack() as es:
            gup = k.sb(es, "gup", [128, 2, 256]); poolw = k.sb(es, "poolw", [128, 2, 256]); wo = k.sb(es, "wo", [128, 10, D])
            k.load(gup[:], gup_in[:, l], gup_in, gup); k.load(poolw[:], poolw_in[:, l], poolw_in, poolw)
            k.load(wo[:], wout_in[l], wout_in, wo, r=True)
            yy = k.sb(es, "yy", [128, 4, 2, 256]); glo = k.sb(es, "glo", [128, 2, 256])
            up = k.sb(es, "up", [128, 2, 20, 80]); pa = k.sb(es, "pa", [128, 2, 20, 80]); pb = k.sb(es, "pb", [128, 2, 20, 80])
            va = k.sb(es, "va", [128, 2, 20, 64]); vb = k.sb(es, "vb", [128, 2, 20, 64])
            cu = k.sb(es, "cu", [128, 2, 288]); ca = k.sb(es, "ca", [128, 2, 288]); cb = k.sb(es, "cb", [128, 2, 288])
            yv = k.sb(es, "yv", [128, 2, 256]); dv = k.sb(es, "dv", [128, 2, 256]); dsq = k.sb(es, "dsq", [128, 2, 256])
            mu = k.sb(es, "mu", [128, 2, 256]); dd = mu; tmpd = dsq
            mq = k.sb(es, "mq", [128, 10, 256]); ost = k.sb(es, "ost", [128, 4, 256])
            for tl in (up, pa, pb, va, vb, cu, ca, cb):
                k.G(lambda h, tl=tl: h.memset(tl[:], 0.0), [], [tl])
            hvb = lambda w: bc(hv[:, l, w, :].unsqueeze(2), [128, 2, 256])
            GRP = [(0, slice(0, 64)), (0, slice(64, 128)), (1, slice(0, 64)), (1, slice(64, 128))]

            def blk3c(c0, kind, bi):
                k.load(yy[:], YY[:, :, c0:c0 + 256].rearrange("w (q p) t -> p w q t", p=128), YY, yy)
                k.load(glo[:], UT[960:1216, c0:c0 + 256].rearrange("(q p) t -> p q t", p=128), UT, glo)
                k.V(lambda h: h.tensor_tensor(out=yv[:], in0=yy[:, 0], in1=yy[:, 1], op=ALU.add), [yy], [yv])
                bk = k.bank()
                for q in range(2):
                    k.mm(bk[:, q * 256:(q + 1) * 256], BONES[:], yv[:, q, :], True, True, [BONES, yv], [bk])
                k.V(lambda h: h.scalar_tensor_tensor(out=dv[:].rearrange("p q t -> p (q t)"), in0=bk[:, 0:512], scalar=-1.0 / 64,
                    in1=yv[:].rearrange("p q t -> p (q t)"), op0=ALU.mult, op1=ALU.add), [bk, yv], [dv])
                k.G(lambda h: h.tensor_tensor(out=dsq[:], in0=dv[:], in1=dv[:], op=ALU.mult), [dv], [dsq])
                bk = k.bank()
                for q in range(2):
                    k.mm(bk[:, q * 256:(q + 1) * 256], BONES[:], dsq[:, q, :], True, True, [BONES, dsq], [bk])
                rsq(k, mu[:].rearrange("p q t -> p (q t)"), bk[:, 0:512], 1.0 / 64, 64e-5, [bk], mu)
                k.V(lambda h: h.tensor_tensor(out=dv[:], in0=dv[:], in1=mu[:], op=ALU.mult), [dv, mu], [dv])
                k.V(lambda h: h.tensor_tensor(out=dv[:], in0=dv[:], in1=hvb(3), op=ALU.mult), [dv, hv], [dv])
                k.V(lambda h: h.tensor_tensor(out=dv[:], in0=dv[:], in1=hvb(4), op=ALU.add), [dv, hv], [dv])
                k.V(lambda h: h.tensor_tensor(out=dv[:], in0=dv[:], in1=yy[:, 2], op=ALU.add), [dv, yy], [dv])
                k.V(lambda h: h.tensor_tensor(out=dv[:], in0=dv[:], in1=yy[:, 3], op=ALU.add), [dv, yy], [dv])
                bk = k.bank()
                for q in range(2):
                    for kc in range(2):
                        k.mm(bk[:, q * 256:(q + 1) * 256], gup[:, kc, q * 128:(q + 1) * 128], glo[:, kc, :], kc == 0, kc == 1, [gup, glo], [bk])
                k.V(lambda h: h.tensor_tensor(out=mq[:, 0:2, :].rearrange("p q t -> p (q t)").bitcast(F32R), in0=bk[:, 0:512],
                                              in1=dv[:].rearrange("p q t -> p (q t)"), op=ALU.mult), [bk, dv], [mq])
                if kind == "lat":
                    r0 = bi * 4
                    k.load(up[:, 0, :, 8:72], UT[1216:1344, c0 - 512:c0 + 768].rearrange("p (r c) -> p r c", c=64), UT, up)
                    k.load(up[:, 1, :, 8:72], UT[1344:1472, c0 - 512:c0 + 768].rearrange("p (r c) -> p r c", c=64), UT, up)
                    k.V(lambda h: h.tensor_tensor(out=pb[:, :, :, 1:80], in0=up[:, :, :, 0:79], in1=up[:, :, :, 1:80], op=ALU.add), [up], [pb])
                    k.V(lambda h: h.tensor_tensor(out=pa[64:128, 0, :, 2:79], in0=pb[64:128, 0, :, 1:78], in1=pb[64:128, 0, :, 3:80], op=ALU.add), [pb], [pa])
                    k.V(lambda h: h.tensor_tensor(out=pa[:, 1, :, 2:79], in0=pb[:, 1, :, 1:78], in1=pb[:, 1, :, 3:80], op=ALU.add), [pb], [pa])
                    k.V(lambda h: h.tensor_tensor(out=pb[:, 1, :, 4:77], in0=pa[:, 1, :, 2:75], in1=pa[:, 1, :, 6:79], op=ALU.add), [pa], [pb])
                    k.V(lambda h: h.tensor_tensor(out=pa[64:128, 1, :, 8:73], in0=pb[64:128, 1, :, 4:69], in1=pb[64:128, 1, :, 12:77], op=ALU.add), [pb], [pa])
                    hs = [pb, pa, pb, pa]
                    for gi, (pc, ps) in enumerate(GRP):
                        k.G(lambda h, gi=gi, pc=pc, ps=ps: h.tensor_tensor(out=va[ps, pc, 1:20, :], in0=hs[gi][ps, pc, 0:19, 8:72],
                                                                          in1=hs[gi][ps, pc, 1:20, 8:72], op=ALU.add), [pa, pb], [va])
                    k.G(lambda h: h.tensor_tensor(out=vb[64:128, 0, 2:19, :], in0=va[64:128, 0, 1:18, :], in1=va[64:128, 0, 3:20, :], op=ALU.add), [va], [vb])
                    k.G(lambda h: h.tensor_tensor(out=vb[:, 1, 2:19, :], in0=va[:, 1, 1:18, :], in1=va[:, 1, 3:20, :], op=ALU.add), [va], [vb])
                    k.G(lambda h: h.tensor_tensor(out=va[:, 1, 4:17, :], in0=vb[:, 1, 2:15, :], in1=vb[:, 1, 6:19, :], op=ALU.add), [vb], [va])
                    k.G(lambda h: h.tensor_tensor(out=vb[64:128, 1, 8:13, :], in0=va[64:128, 1, 4:9, :], in1=va[64:128, 1, 12:17, :], op=ALU.add), [va], [vb])
                    vs = [va, vb, va, vb]
                    var = 0 if bi == 0 else (1 if bi == 1 else (3 if bi == 62 else (4 if bi == 63 else 2)))
                    for gi, (pc, ps) in enumerate(GRP):
                        d4 = lambda tl: tl[ps, pc, :].rearrange("p (r c) -> p r c", c=64)
                        k.V(lambda h: h.tensor_tensor(out=d4(tmpd), in0=vs[gi][ps, pc, 8:12, :], in1=rc2[ps, var, pc], op=ALU.mult), [va, vb, rc2], [tmpd])
                        k.V(lambda h: h.tensor_tensor(out=d4(dd), in0=d4(tmpd), in1=up[ps, pc, 8:12, 8:72], op=ALU.subtract), [tmpd, up], [dd])
                else:
                    k.load(cu[:, 0, 16:272], UT[1216:1344, c0:c0 + 256], UT, cu)
                    k.load(cu[:, 1, 16:272], UT[1344:1472, c0:c0 + 256], UT, cu)
                    k.V(lambda h: h.tensor_tensor(out=cb[:, :, 1:288], in0=cu[:, :, 0:287], in1=cu[:, :, 1:288], op=ALU.add), [cu], [cb])
                    k.V(lambda h: h.tensor_tensor(out=ca[64:128, 0, 2:287], in0=cb[64:128, 0, 1:286], in1=cb[64:128, 0, 3:288], op=ALU.add), [cb], [ca])
                    k.V(lambda h: h.tensor_tensor(out=ca[:, 1, 2:287], in0=cb[:, 1, 1:286], in1=cb[:, 1, 3:288], op=ALU.add), [cb], [ca])
                    k.V(lambda h: h.tensor_tensor(out=cb[:, 1, 4:285], in0=ca[:, 1, 2:283], in1=ca[:, 1, 6:287], op=ALU.add), [ca], [cb])
                    k.V(lambda h: h.tensor_tensor(out=ca[64:128, 1, 8:281], in0=cb[64:128, 1, 4:277], in1=cb[64:128, 1, 12:285], op=ALU.add), [cb], [ca])
                    hs = [cb, ca, cb, ca]
                    for gi, (pc, ps) in enumerate(GRP):
                        k.V(lambda h: h.tensor_tensor(out=tmpd[ps, pc, :], in0=hs[gi][ps, pc, 16:272], in1=rc1[ps, pc, :], op=ALU.mult), [ca, cb, rc1], [tmpd])
                        k.V(lambda h: h.tensor_tensor(out=dd[ps, pc, :], in0=tmpd[ps, pc, :], in1=cu[ps, pc, 16:272], op=ALU.subtract), [tmpd, cu], [dd])
                for gi, (pc, ps) in enumerate(GRP):
                    for nn in range(2):
                        bk = k.bank()
                        k.mm(bk[:, 0:256], poolw[ps, pc, nn * 128:(nn + 1) * 128], dd[ps, pc, :], True, True, [poolw, dd], [bk])
                        k.A(lambda h, bk=bk, gi=gi, nn=nn: h.activation(out=mq[:, 2 + gi * 2 + nn, :].bitcast(F32R), in_=bk[:, 0:256], func=AF.Copy,
                                                                       scale=pscale[:, l, gi * 2 + nn:gi * 2 + nn + 1]), [bk, pscale], [mq])
                for n4 in range(4):
                    for nn in range(4):
                        n_ = n4 * 4 + nn
                        bk = k.bank()
                        for kc in range(10):
                            k.mm(bk[:, 0:256], wo[:, kc, n_ * 128:(n_ + 1) * 128], mq[:, kc, :], kc == 0, kc == 9, [wo, mq], [bk], r=True)
                        k.cp(ost[:, nn, :], bk[:, 0:256], [bk], [ost])
                    if kind == "lat":
                        r = bi // 16; lc = 64 + (bi % 16) * 256
                        cdma(False, PT, r * D + n4 * 512, 512, lambda off, ln: ost[:, :, off:off + ln], ost, lc, 256)
                    else:
                        for r in range(4):
                            cdma(False, PT, r * D + n4 * 512, 512, lambda off, ln, r=r: ost[:, :, r * 64 + off:r * 64 + off + ln], ost, 0, 64)

            blk3c(CTX0, "ctx", 0)
            for bi in range(64):
                blk3c(LAT0 + 256 * bi, "lat", bi)
            for j in range(NCH):
                coll("ReduceScatter", ALU.add, PT, DT, PT[j], DT[j])
        k.barrier()
        FT = 260
        with ExitStack() as es:
            xt = k.sb(es, "f_x", [128, 16, FT]); h2 = k.sb(es, "f_h", [128, 16, FT]); sq = k.sb(es, "f_sq", [128, 16, FT])
            rs = k.sb(es, "f_rs", [128, FT]); ff = k.sb(es, "f_f", [128, FC, FT]); sl = k.sb(es, "f_sl", [128, FT])
            wgs = [k.sb(es, "f_wg%d" % i, [128, 16, 128]) for i in range(2)]; wus = [k.sb(es, "f_wu%d" % i, [128, 16, 128]) for i in range(2)]
            wds = [k.sb(es, "f_wd%d" % i, [128, FC, 128]) for i in range(2)]
            last = (l == L - 1)
            for i in range(NTL // FT):
                lt0 = i * FT
                k.load(xt[:], xrows(XS, lt0, lt0 + FT), XS, xt)
                cdma(True, DT, 0, D, lambda off, ln: sq[:, :, off:off + ln], sq, lt0, FT)
                modulate(sq, sq, lt0, FT, l, 2, None, [])
                k.V(lambda h: h.tensor_tensor(out=xt[:], in0=xt[:], in1=sq[:], op=ALU.add), [xt, sq], [xt])
                k.A(lambda h: h.activation(out=sq[:], in_=xt[:], func=AF.Square), [xt], [sq])
                bk = k.bank()
                for kc in range(16):
                    k.mm(bk[:, 0:FT], ONES[:], sq[:, kc, :], kc == 0, kc == 15, [ONES, sq], [bk])
                rsq(k, rs[:], bk[:, 0:FT], 1.0 / D, 1e-6, [bk], rs)
                k.V(lambda h: h.tensor_tensor(out=h2[:].bitcast(F32R), in0=xt[:], in1=bc(rs[:].unsqueeze(1), [128, 16, FT]), op=ALU.mult), [xt, rs], [h2])
                modulate(h2, h2, lt0, FT, l, 4, 3, [], r=True)
                for j in range(FC):
                    wg = wgs[j % 2]; wu = wus[j % 2]
                    k.load(wg[:], fg_in[l, j], fg_in, wg, r=True)
                    k.load(wu[:], fu_in[l, j], fu_in, wu, r=True)
                    bk = k.bank(); bk2 = k.bank()
                    for kc in range(16):
                        k.mm(bk[:, 0:FT], wg[:, kc, :], h2[:, kc, :], kc == 0, kc == 15, [wg, h2], [bk], r=True)
                    for kc in range(16):
                        k.mm(bk2[:, 0:FT], wu[:, kc, :], h2[:, kc, :], kc == 0, kc == 15, [wu, h2], [bk2], r=True)
                    k.A(lambda h, bk=bk: h.activation(out=sl[:], in_=bk[:, 0:FT], func=AF.Silu), [bk], [sl])
                    k.V(lambda h, bk2=bk2, j=j: h.tensor_tensor(out=ff[:, j, :].bitcast(F32R), in0=bk2[:, 0:FT], in1=sl[:], op=ALU.mult), [bk2, sl], [ff])
                for n_ in range(16):
                    wd = wds[n_ % 2]
                    k.load(wd[:], fd_in[l, n_], fd_in, wd, r=True)
                    bk = k.bank()
                    for fc in range(FC):
                        k.mm(bk[:, 0:FT], wd[:, fc, :], ff[:, fc, :], fc == 0, fc == FC - 1, [wd, ff], [bk], r=True)
                    segs = [(0, 64, 1), (64, FT, 0)] if lt0 == 0 else [(0, FT, 0)]
                    for (a, b_, m) in segs:
                        k.V(lambda h, a=a, b_=b_, m=m, n_=n_, bk=bk: h.scalar_tensor_tensor(out=xt[:, n_, a:b_], in0=bk[:, a:b_],
                            scalar=DER[:, l, 5, n_, m:m + 1], in1=xt[:, n_, a:b_], op0=ALU.mult, op1=ALU.add), [bk, DER, xt], [xt])
                if not last:
                    k.store(xrows(XT, lt0, lt0 + FT), xt[:], xt, XT)
                else:
                    k.A(lambda h: h.activation(out=sq[:], in_=xt[:], func=AF.Square), [xt], [sq])
                    bk = k.bank()
                    for kc in range(16):
                        k.mm(bk[:, 0:FT], ONES[:], sq[:, kc, :], kc == 0, kc == 15, [ONES, sq], [bk])
                    rsq(k, rs[:], bk[:, 0:FT], 1.0 / D, 1e-6, [bk], rs)
                    k.V(lambda h: h.tensor_tensor(out=xt[:], in0=xt[:], in1=bc(rs[:].unsqueeze(1), [128, 16, FT]), op=ALU.mult), [xt, rs], [xt])
                    k.V(lambda h: h.tensor_tensor(out=xt[:], in0=xt[:], in1=bc(fin[:].unsqueeze(2), [128, 16, FT]), op=ALU.mult), [xt, fin], [xt])
                    k.store(xrows(out_d, lt0, lt0 + FT), xt[:], xt, out_d)
        k.barrier()
        XS = XT
    return k


def _host_inputs(inp, depth=4):
    f = np.float32
    L = depth
    x = inp["x"]; ctx = inp["ctx"]
    ORK = [0, 1024, 2048]
    maps = []
    rc2 = np.zeros((128, 5, 2, 4, 64), f); rc1 = np.zeros((128, 2, 256), f)
    for pc in range(2):
        for half in range(2):
            gi = pc * 2 + half; W = 2 ** (gi + 1); hf = W // 2
            ps = slice(half * 64, half * 64 + 64)
            cc = np.arange(64); ccnt = np.clip(cc + hf, 0, 64) - np.clip(cc - hf, 0, 64)
            for var, r0 in enumerate([0, 4, 128, 248, 252]):
                rr = np.arange(r0, r0 + 4); rcnt = np.clip(rr + hf, 0, 256) - np.clip(rr - hf, 0, 256)
                rc2[ps, var, pc] = (1.0 / (rcnt[:, None] * ccnt[None, :])).astype(f)[None]
            tt = np.arange(256); tcnt = np.clip(tt + hf, 0, 256) - np.clip(tt - hf, 0, 256)
            rc1[ps, pc] = (1.0 / tcnt).astype(f)[None]
    pm = lambda v: np.ascontiguousarray(v.reshape(-1, 128).T)
    FG = np.ascontiguousarray(inp["ffn_gate"][:L].reshape(L, 16, 128, FC, 128).transpose(0, 3, 2, 1, 4))
    FU = np.ascontiguousarray(inp["ffn_up"][:L].reshape(L, 16, 128, FC, 128).transpose(0, 3, 2, 1, 4))
    FD = np.ascontiguousarray(inp["ffn_down"][:L].reshape(L, FC, 128, 16, 128).transpose(0, 3, 2, 1, 4))
    for c in range(8):
        b, s_ = c // 4, c % 4
        g = s_
        m = {}
        m["xT"] = np.ascontiguousarray(np.concatenate([ctx[b, 64 * s_:64 * s_ + 64], x[b, 4096 * s_:4096 * s_ + 4096]], 0).T)
        cT = np.zeros((128, 16, 2), f); cT[:, :, 0] = pm(inp["c"][b]); cT[:, :, 1] = pm(inp["c_ctx"]); m["cT"] = cT
        m["adaw"] = np.ascontiguousarray(inp["ada_w"][:L, :, 3072 * s_:3072 * s_ + 3072])
        m["adab"] = np.ascontiguousarray(np.stack([pm(inp["ada_b"][l, 3072 * s_:3072 * s_ + 3072]) for l in range(L)], 1))
        nr = np.zeros((128, 2, L, 16), f)
        for l in range(L):
            nr[:, 0, l] = pm(inp["norm_mix"][l]); nr[:, 1, l] = pm(inp["norm_ffn"][l])
        m["nrm"] = nr; m["fin"] = pm(inp["final_norm"])
        cols = []
        for o in ORK:
            cols += list(range(o + g * 256, o + g * 256 + 256))
        cols += list(range(3072, 3072 + 448))
        for gi in range(4):
            cols += list(range(3520 + gi * 256 + s_ * 64, 3520 + gi * 256 + s_ * 64 + 64))
        cols = np.array(cols)
        m["win"] = np.ascontiguousarray(inp["w_in"][:L][:, :, cols].reshape(L, 16, 128, 1472).transpose(0, 2, 1, 3))
        cw = np.zeros((128, L, 6, 3), f)
        for l in range(L):
            for wh in range(3):
                for q in range(2):
                    cw[:, l, wh * 2 + q, :] = inp["conv_rkv"][l][:, wh * 1024 + g * 256 + q * 128: wh * 1024 + g * 256 + q * 128 + 128].T
        m["convw"] = cw
        hsl = lambda a, l, q: a[l][g * 256 + q * 128: g * 256 + q * 128 + 128]
        db = np.zeros((128, L, 2, 2), f); ib = np.zeros((128, L, 2, 2), f); hvv = np.zeros((128, L, 5, 2), f)
        for l in range(L):
            for q in range(2):
                for d in range(2):
                    db[:, l, d, q] = inp["decay_bias"][l, d, g * 256 + q * 128: g * 256 + q * 128 + 128]
                    ib[:, l, d, q] = inp["iclr_bias"][l, d, g * 256 + q * 128: g * 256 + q * 128 + 128]
                for wi, nm in enumerate(["k_k", "k_a", "r_k", "gn_w", "gn_b"]):
                    hvv[:, l, wi, q] = hsl(inp[nm], l, q)
        m["dbias"] = db; m["ibias"] = ib; m["hv"] = hvv
        m["dup"] = np.ascontiguousarray(np.transpose(inp["decay_up"][:L, :, :, g * 256:g * 256 + 256], (2, 0, 1, 3)))
        m["iup"] = np.ascontiguousarray(np.transpose(inp["iclr_up"][:L, :, :, g * 256:g * 256 + 256], (2, 0, 1, 3)))
        gu = inp["gate_up"][:L, :, g * 256:g * 256 + 256].reshape(L, 2, 128, 256)
        m["gup"] = np.ascontiguousarray(np.transpose(gu, (2, 0, 1, 3)))
        pw = np.zeros((128, L, 2, 256), f)
        for l in range(L):
            for pc in range(2):
                for half in range(2):
                    pw[half * 64:half * 64 + 64, l, pc] = inp["pool_w"][l, pc * 2 + half, s_ * 64:s_ * 64 + 64, :]
        m["poolw"] = pw
        m["pscale"] = np.ascontiguousarray(np.stack([pm(inp["pool_scale"][l]) for l in range(L)], 1))
        m["wout"] = np.ascontiguousarray(np.concatenate([inp["w_out"][:L, g * 256:g * 256 + 256], inp["w_out"][:L, 1024:2048]], 1).reshape(L, 10, 128, D).transpose(0, 2, 1, 3))
        m["fg"] = FG; m["fu"] = FU; m["fd"] = FD
        m["rc2"] = rc2; m["rc1"] = rc1
        maps.append({kk_: np.ascontiguousarray(v, dtype=f) for kk_, v in m.items()})
    return maps


def kernel(**inputs):
    inp = {k_: np.asarray(v) for k_, v in inputs.items()}
    k = build(4)
    maps = _host_inputs(inp, 4)
    res = run_bass_kernel_spmd(k.nc, maps, core_ids=list(range(8)))
    out = np.zeros((2, 16384, 2048), np.float32)
    for c in range(8):
        b, s_ = c // 4, c % 4
        o = res.results[c]["out"]
        out[b, 4096 * s_:4096 * s_ + 4096] = o[:, 64:].T
    return out
```

```python
import numpy as np
from contextlib import ExitStack
import concourse.bass as bass
import concourse.mybir as mybir
from concourse.bass_utils import run_bass_kernel_spmd

F32 = mybir.dt.float32
F32R = mybir.dt.float32r
BF16 = mybir.dt.bfloat16
AF = mybir.ActivationFunctionType
ALU = mybir.AluOpType

D = 2048; KC = 16; NTL = 4160; TT = 416; NTI = 10; CW = 104; NCH = 40
UC = 17920; CTX0 = 256; LAT0 = 1024
DFF = 5632; FC = 44
C0 = float(np.exp(-0.5))
GROUPS = [[0, 1, 2, 3], [4, 5, 6, 7]]
UCH = [(0, 128), (128, 128), (256, 128), (384, 128), (512, 128), (640, 128),
       (768, 96), (864, 96), (960, 128), (1088, 128), (1216, 128), (1344, 128)]


class Sem:
    def __init__(s, h, dma):
        s.h = h; s.dma = dma; s.total = 0


class T:
    def __init__(s, t, name):
        s.t = t; s.name = name; s.w = []; s.r = []; s.dsem = None

    def __getitem__(s, k):
        return s.t[k]


class Eng:
    def __init__(s, h, sem, name):
        s.h = h; s.sem = sem; s.cnt = 0; s.seen = {}; s.name = name


class B:
    def __init__(s):
        nc = bass.Bass("TRN2", target_bir_lowering=False)
        s.nc = nc
        s.ges = ExitStack()
        s.nsem = 0
        s.dsems = []
        s.pe = Eng(nc.tensor, s.newsem(False), "pe")
        s.act = Eng(nc.scalar, s.newsem(False), "act")
        s.dve = Eng(nc.vector, s.newsem(False), "dve")
        s.pool = Eng(nc.gpsimd, s.newsem(False), "pool")
        s.sp = Eng(nc.sync, s.newsem(False), "sp")
        s.engs = [s.pe, s.act, s.dve, s.pool, s.sp]
        s.banks = [T(s.ges.enter_context(nc.psum_tensor("bank%d" % i, [128, 512], F32)), "bank%d" % i)
                   for i in range(8)]
        s.bi = 0
        s.flip = 0

    def newsem(s, dma):
        s.nsem += 1
        sm = Sem(s.ges.enter_context(s.nc.semaphore("sm%d" % s.nsem)), dma)
        if dma:
            s.dsems.append(sm)
        return sm

    def bank(s):
        b = s.banks[s.bi]; s.bi = (s.bi + 1) % 8
        return b

    def sb(s, es, name, shape, dt=F32):
        s.nsb = getattr(s, "nsb", 0) + 1
        return T(es.enter_context(s.nc.sbuf_tensor("sb%d_%s" % (s.nsb, name), list(shape), dt)), name)

    def dram(s, name, shape, kind="Internal", dt=F32):
        return T(s.nc.dram_tensor(name, list(shape), dt, kind=kind).ap(), name)

    def _wait(s, eng, toks):
        need = {}
        for (sm, v) in toks:
            if sm.dma:
                v = sm.total
            if sm is eng.sem and eng is s.pe:
                continue
            if need.get(sm, 0) < v:
                need[sm] = v
        for sm, v in need.items():
            if eng.seen.get(sm, 0) >= v:
                continue
            eng.h.wait_ge(sm.h, v)
            eng.seen[sm] = v

    def _deps(s, reads, writes):
        toks = []
        for b in reads:
            toks += b.w
        for b in writes:
            toks += b.w; toks += b.r
        return toks

    def _upd(s, tok, reads, writes):
        for b in writes:
            b.w = [tok]; b.r = []
        for b in reads:
            if b not in writes:
                b.r = [t for t in b.r if t[0] is not tok[0]] + [tok]

    def op(s, eng, fn, reads=(), writes=()):
        s._wait(eng, s._deps(reads, writes))
        ins = fn(eng.h)
        eng.cnt += 1
        ins.then_inc(eng.sem.h, 1)
        s._upd((eng.sem, eng.cnt), reads, writes)
        return ins

    def dma(s, eng, out, in_, reads, writes, owner=None, fn=None, inc=16):
        s._wait(eng, s._deps(reads, writes))
        ow = owner or writes[0]
        if ow.dsem is None:
            cache = s.__dict__.setdefault('semcache', {})
            if ow.name not in cache:
                cache[ow.name] = s.newsem(True)
            ow.dsem = cache[ow.name]
        sm = ow.dsem
        sm.total += inc
        if fn is None:
            ins = eng.h.dma_start(out=out, in_=in_)
        else:
            ins = fn(eng.h)
        ins.then_inc(sm.h, inc)
        s._upd((sm, sm.total), reads, writes)

    def load(s, out, in_, src, dst, owner=None, r=False, cast=False):
        if cast:
            s.dma(s.pool, out, in_, [src], [dst], owner)
            return
        if r:
            s.dma(s.pool, out.bitcast(F32R), in_, [src], [dst], owner)
            return
        s.dma(s.sp, out, in_, [src], [dst], owner)

    def store(s, out, in_, src, dst, owner=None):
        s.dma(s.pool, out, in_, [src], [dst], owner)

    def barrier(s):
        for e in s.engs:
            toks = [(o.sem, o.cnt) for o in s.engs if o is not e and o.cnt > 0]
            toks += [(sm, sm.total) for sm in s.dsems if sm.total > 0]
            s._wait(e, toks)

    def mm(s, out, lhsT, rhs, start, stop, reads, writes, r=False):
        if r:
            lhsT = lhsT.bitcast(mybir.dt.float32r); rhs = rhs.bitcast(mybir.dt.float32r)
        return s.op(s.pe, lambda h: h.matmul(out, lhsT=lhsT, rhs=rhs, start=start, stop=stop), reads, writes)

    def V(s, fn, reads, writes):
        return s.op(s.dve, fn, reads, writes)

    def A(s, fn, reads, writes):
        return s.op(s.act, fn, reads, writes)

    def G(s, fn, reads, writes):
        return s.op(s.pool, fn, reads, writes)

    def cp(s, out, in_, reads, writes, eng=None):
        if eng is None:
            s.flip ^= 1
            eng = s.act if s.flip else s.dve
        if eng is s.act:
            return s.A(lambda h: h.activation(out=out, in_=in_, func=AF.Copy), reads, writes)
        return s.op(eng, lambda h: h.tensor_copy(out=out, in_=in_), reads, writes)


def rsq(k, out, in_, scale, bias, reads, wT):
    k.A(lambda h: h.activation(out=out, in_=in_, func=AF.Sqrt, scale=scale, bias=bias), reads, [wT])
    k.V(lambda h: h.reciprocal(out=out, in_=out), [wT], [wT])


def bc(ap, shape):
    return ap.broadcast_to(list(shape))


def build(depth=4, dbg=None):
    k = B()
    nc = k.nc
    ges = k.ges
    L = depth

    def din(name, shape, dt=F32):
        return k.dram(name, shape, "ExternalInput", dt)

    xT_in = din("xT", [D, NTL]); cT_in = din("cT", [128, 16, 2])
    adaw_in = din("adaw", [L, D, 3072]); adab_in = din("adab", [128, L, 24])
    nrm_in = din("nrm", [128, 2, L, 16]); fin_in = din("fin", [128, 16])
    win_in = din("win", [L, 128, 16, 1472]); convw_in = din("convw", [128, L, 6, 3])
    dbias_in = din("dbias", [128, L, 2, 2]); ibias_in = din("ibias", [128, L, 2, 2])
    dup_in = din("dup", [96, L, 2, 256]); iup_in = din("iup", [96, L, 2, 256])
    gup_in = din("gup", [128, L, 2, 256]); hv_in = din("hv", [128, L, 5, 2])
    poolw_in = din("poolw", [128, L, 2, 256]); pscale_in = din("pscale", [128, L, 8])
    wout_in = din("wout", [L, 128, 10, D])
    fg_in = din("fg", [L, FC, 128, 16, 128]); fu_in = din("fu", [L, FC, 128, 16, 128]); fd_in = din("fd", [L, 16, 128, FC, 128])
    rc2_in = din("rc2", [128, 5, 2, 4, 64]); rc1_in = din("rc1", [128, 2, 256])
    out_d = k.dram("out", [D, NTL], "ExternalOutput")

    XT = k.dram("XT", [D, NTL]); HT = k.dram("HT", [NCH, D, CW]); HG = k.dram("HG", [NCH, 4 * D, CW])
    UT = k.dram("UT", [1472, UC]); YY = k.dram("YY", [4, 256, UC])
    PT = k.dram("PT", [NCH, 4 * D, CW]); DT = k.dram("DT", [NCH, D, CW])
    MODP = k.dram("MODP", [128, 192]); MODG = k.dram("MODG", [512, 192])
    dbg_t = {}
    if dbg:
        for nm, shp in dbg.items():
            dbg_t[nm] = k.dram("dbg_" + nm, shp, "ExternalOutput")

    cst = T(None, "cstgrp")

    def cload(name, src, shape):
        t = k.sb(ges, name, shape)
        k.load(t[:], src[:], src, t, owner=cst)
        return t

    convw = cload("convw", convw_in, [128, L, 6, 3]); dbias = cload("dbias", dbias_in, [128, L, 2, 2])
    ibias = cload("ibias", ibias_in, [128, L, 2, 2]); hv = cload("hv", hv_in, [128, L, 5, 2])
    pscale = cload("pscale", pscale_in, [128, L, 8]); nrm = cload("nrm", nrm_in, [128, 2, L, 16])
    fin = cload("fin", fin_in, [128, 16]); rc2 = cload("rc2", rc2_in, [128, 5, 2, 4, 64])
    rc1 = cload("rc1", rc1_in, [128, 2, 256]); adab = cload("adab", adab_in, [128, L, 24])
    cT = cload("cT", cT_in, [128, 16, 2])

    ONES = k.sb(ges, "ONES", [128, 128]); IDENT = k.sb(ges, "IDENT", [128, 128]); BONES = k.sb(ges, "BONES", [128, 128])
    MF2 = k.sb(ges, "MF2", [128, 1, 256]); MB2 = k.sb(ges, "MB2", [128, 1, 256])
    MNF = k.sb(ges, "MNF", [128, 1, 128]); MNB = k.sb(ges, "MNB", [128, 1, 128])
    RST = k.sb(ges, "RST", [128, 256]); ZERO = k.sb(ges, "ZERO", [128, 512])
    DER = k.sb(ges, "DER", [128, L, 6, 16, 2])

    k.G(lambda h: h.memset(ONES[:], 1.0), [], [ONES])
    k.G(lambda h: h.memset(ZERO[:], 0.0), [], [ZERO])
    k.G(lambda h: h.memset(IDENT[:], 1.0), [], [IDENT])
    k.G(lambda h: h.affine_select(out=IDENT[:], in_=IDENT[:], pattern=[[1, 128]], base=0, channel_multiplier=-1,
                                  compare_op=ALU.is_equal, fill=0.0), [IDENT], [IDENT])
    k.G(lambda h: h.memset(BONES[:], 0.0), [], [BONES])
    k.G(lambda h: h.memset(BONES[0:64, 0:64], 1.0), [BONES], [BONES])
    k.G(lambda h: h.memset(BONES[64:128, 64:128], 1.0), [BONES], [BONES])

    def mask(t, ap, nrep, step, cm, cmp):
        k.G(lambda h: h.memset(ap, 1.0), [t], [t])
        k.G(lambda h: h.affine_select(out=ap, in_=ap, pattern=[[0, nrep], [step, 128]], base=0, channel_multiplier=cm,
                                      compare_op=cmp, fill=0.0), [t], [t])
    mask(MF2, MF2[:, :, 0:128], 1, 1, -1, ALU.is_gt)
    mask(MF2, MF2[:, :, 128:256], 1, 1, -1, ALU.is_ge)
    mask(MB2, MB2[:, :, 0:128], 1, -1, 1, ALU.is_gt)
    mask(MB2, MB2[:, :, 128:256], 1, -1, 1, ALU.is_ge)
    mask(MNF, MNF[:], 1, -1, 1, ALU.is_gt)
    mask(MNB, MNB[:], 1, 1, -1, ALU.is_gt)
    k.G(lambda h: h.memset(RST[:], 1.0), [], [RST])
    rst4 = RST[:].rearrange("p (c t) -> p c t", t=64)
    k.G(lambda h: h.memset(rst4[:, :, 0:1], 0.0), [RST], [RST])


    def coll(kind, op, src, dst, sap=None, dap=None):
        sap = src[:] if sap is None else sap
        dap = dst[:] if dap is None else dap
        k.dma(k.pool, None, None, [src], [dst], fn=lambda h: h.collective_compute(
            kind, op, replica_groups=GROUPS, ins=[sap], outs=[dap]), inc=1)

    def cdma(load, DR, row0, nrows, sb_fn, T_sb, lt0, n, r=False):
        lt = lt0
        while lt < lt0 + n:
            j = lt // CW; a = lt % CW; ln = min(CW - a, lt0 + n - lt)
            dap = DR[j, row0:row0 + nrows, a:a + ln].rearrange("(kc p) t -> p kc t", p=128)
            sap = sb_fn(lt - lt0, ln)
            if load:
                k.load(sap, dap, DR, T_sb, r=r)
            else:
                k.store(dap, sap, T_sb, DR)
            lt += ln

    with ExitStack() as es:
        sc = k.sb(es, "sc", [128, 16, 2])
        k.A(lambda h: h.activation(out=sc[:], in_=cT[:], func=AF.Silu), [cT], [sc])
        aw = [k.sb(es, "aw%d" % i, [128, 16, 512]) for i in range(2)]
        modp = k.sb(es, "modp", [128, L, 24, 2])
        bk = k.bank()
        n = 0
        for l in range(L):
            for i4 in range(6):
                t = aw[n % 2]; n += 1
                k.load(t[:], adaw_in[l, :, i4 * 512:(i4 + 1) * 512].rearrange("(kc p) n -> p kc n", p=128), adaw_in, t)
                for jj in range(4):
                    i = i4 * 4 + jj
                    col = (l * 24 + i) * 2
                    for kc in range(16):
                        k.mm(bk[:, col:col + 2], t[:, kc, jj * 128:(jj + 1) * 128], sc[:, kc, :], kc == 0, kc == 15, [t, sc], [bk])
        bv = bk[:, 0:L * 48].rearrange("p (l i m) -> p l i m", l=L, m=2)
        k.V(lambda h: h.tensor_tensor(out=modp[:], in0=bv, in1=bc(adab[:].unsqueeze(3), [128, L, 24, 2]), op=ALU.add), [bk, adab], [modp])
        k.store(MODP[:, 0:L * 48], modp[:].rearrange("p l i m -> p (l i m)"), modp, MODP)
        coll("AllGather", ALU.bypass, MODP, MODG)
        MOD = k.sb(es, "MOD", [128, 4, 192])
        k.load(MOD[:], MODG[:].rearrange("(r p) f -> p r f", p=128), MODG, MOD)
        MODv = MOD[:, :, 0:L * 48].rearrange("p r (l i m) -> p r l i m", l=L, m=2)
        for w in range(6):
            j0 = 16 * w
            while j0 < 16 * w + 16:
                r = j0 // 24; j1 = min(16 * w + 16, (r + 1) * 24)
                k.V(lambda h, r=r, j0=j0, j1=j1, w=w: h.tensor_copy(
                    out=DER[:, :, w, j0 - 16 * w:j1 - 16 * w, :], in_=MODv[:, r, :, j0 - 24 * r:j1 - 24 * r, :]), [MOD], [DER])
                j0 = j1
        for (w, wn) in ((1, 0), (4, 1)):
            k.V(lambda h, w=w: h.tensor_scalar(out=DER[:, :, w], in0=DER[:, :, w], scalar1=1.0, scalar2=None, op0=ALU.add), [DER], [DER])
            k.V(lambda h, w=w, wn=wn: h.tensor_tensor(out=DER[:, :, w], in0=DER[:, :, w],
                                                     in1=bc(nrm[:, wn].unsqueeze(3), [128, L, 16, 2]), op=ALU.mult), [DER, nrm], [DER])
    k.barrier()

    def modulate(tile, T_, lt0, ncols, l, wg, wsh, reads, r=False, out_tile=None, outT=None):
        segs = []
        if lt0 < 64:
            segs.append((0, 64 - lt0, 1))
            segs.append((64 - lt0, ncols, 0))
        else:
            segs.append((0, ncols, 0))
        for (a, b_, m) in segs:
            n_ = b_ - a
            if wg is not None:
                k.V(lambda h, a=a, b_=b_, m=m, n_=n_: h.tensor_tensor(out=(tile[:, :, a:b_].bitcast(F32R) if r else tile[:, :, a:b_]), in0=tile[:, :, a:b_],
                    in1=bc(DER[:, l, wg, :, m:m + 1], [128, 16, n_]), op=ALU.mult), [T_, DER] + reads, [T_])
            if wsh is not None:
                k.V(lambda h, a=a, b_=b_, m=m, n_=n_: h.tensor_tensor(out=(out_tile[:, :, a:b_] if out_tile is not None else (tile[:, :, a:b_].bitcast(F32R) if r else tile[:, :, a:b_])), in0=tile[:, :, a:b_],
                    in1=bc(DER[:, l, wsh, :, m:m + 1], [128, 16, n_]), op=ALU.add), [T_, DER] + reads, [T_ if outT is None else outT])

    def rms_tile(es_t, xt, sq, rs, eps=1e-6):
        k.A(lambda h: h.activation(out=sq[:], in_=xt[:], func=AF.Square), [xt], [sq])
        bk = k.bank()
        for kc in range(16):
            k.mm(bk[:, 0:TT], ONES[:], sq[:, kc, :], kc == 0, kc == 15, [ONES, sq], [bk])
        rsq(k, rs[:], bk[:, 0:TT], 1.0 / D, eps, [bk], rs)

    xrows = lambda Tn, c0, c1: Tn[:, c0:c1].rearrange("(kc p) t -> p kc t", p=128)

    def ucol(r, lt):
        return CTX0 + 64 * r + lt if lt < 64 else LAT0 + 4096 * r + (lt - 64)

    for (c0, c1) in ((0, 256), (512, 1024), (17408, 17920)):
        for r0 in range(0, 1472, 128):
            r1 = min(1472, r0 + 128)
            for cc in range(c0, c1, 512):
                ce = min(c1, cc + 512)
                k.store(UT[r0:r1, cc:ce], ZERO[0:r1 - r0, 0:ce - cc], ZERO, UT)

    XS = xT_in
    for l in range(L):
        with ExitStack() as es:
            xts = [k.sb(es, "p1x%d" % i, [128, 16, TT]) for i in range(2)]
            sq = k.sb(es, "p1sq", [128, 16, TT]); rs = k.sb(es, "p1rs", [128, TT])
            for i in range(NTI):
                xt = xts[i % 2]
                k.load(xt[:], xrows(XS, i * TT, (i + 1) * TT), XS, xt)
                rms_tile(es, xt, sq, rs)
                k.V(lambda h: h.tensor_tensor(out=xt[:], in0=xt[:], in1=bc(rs[:].unsqueeze(1), [128, 16, TT]), op=ALU.mult), [xt, rs], [xt])
                modulate(xt, xt, i * TT, TT, l, 1, 0, [])
                cdma(False, HT, 0, D, lambda off, ln, xt=xt: xt[:, :, off:off + ln], xt, i * TT, TT)
            for j in range(NCH):
                coll("AllGather", ALU.bypass, HT, HG, HT[j], HG[j])
        k.barrier()
        with ExitStack() as es:
            wsl = k.sb(es, "wsl", [128, 16, 1472])
            k.load(wsl[:], win_in[l], win_in, wsl, r=True)
            hgs = [k.sb(es, "hg%d" % i, [128, 16, TT]) for i in range(2)]
            ust = k.sb(es, "ust", [128, 12, TT])
            n = 0
            for r in range(4):
                for i in range(NTI):
                    hg = hgs[n % 2]; n += 1
                    cdma(True, HG, r * D, D, lambda off, ln, hg=hg: hg[:, :, off:off + ln], hg, i * TT, TT, r=True)
                    for j, (c0, M) in enumerate(UCH):
                        bk = k.bank()
                        for kc in range(16):
                            k.mm(bk[0:M, 0:TT], wsl[:, kc, c0:c0 + M], hg[:, kc, :], kc == 0, kc == 15, [wsl, hg], [bk], r=True)
                        fn = AF.Tanh if j == 6 else (AF.Sigmoid if j in (8, 9) else None)
                        if fn is not None:
                            k.A(lambda h, fn=fn, M=M, j=j, bk=bk: h.activation(out=ust[0:M, j, :], in_=bk[0:M, 0:TT], func=fn), [bk], [ust])
                        else:
                            k.cp(ust[0:M, j, :], bk[0:M, 0:TT], [bk], [ust])
                    lt0 = i * TT
                    segs = [(0, 64, ucol(r, 0)), (64, TT, ucol(r, 64))] if i == 0 else [(0, TT, ucol(r, lt0))]
                    for (a, b_, uc0) in segs:
                        n_ = b_ - a
                        k.store(UT[0:768, uc0:uc0 + n_].rearrange("(j p) t -> p j t", p=128), ust[:, 0:6, a:b_], ust, UT)
                        k.store(UT[768:864, uc0:uc0 + n_], ust[0:96, 6, a:b_], ust, UT)
                        k.store(UT[864:960, uc0:uc0 + n_], ust[0:96, 7, a:b_], ust, UT)
                        k.store(UT[960:1472, uc0:uc0 + n_].rearrange("(j p) t -> p j t", p=128), ust[:, 8:12, a:b_], ust, UT)
        k.barrier()
        if dbg and "UT" in dbg and l == 0:
            k.dma(k.pool, dbg_t["UT"][:], UT[:], [UT], [dbg_t["UT"]])
            k.barrier()

        with ExitStack() as es:
            dup = k.sb(es, "dup", [96, 2, 256]); iup = k.sb(es, "iup", [96, 2, 256])
            k.load(dup[:], dup_in[:, l], dup_in, dup); k.load(iup[:], iup_in[:, l], iup_in, iup)
            u6s = [k.sb(es, "u6_%d" % i, [128, 6, 258]) for i in range(2)]
            los = [k.sb(es, "lo_%d" % i, [96, 2, 256]) for i in range(2)]
            sbt = lambda nm, shp=(128, 2, 256), dt=F32: k.sb(es, nm, list(shp), dt)
            IDB = k.sb(es, "IDB", [128, 128], BF16)
            k.V(lambda h: h.tensor_copy(out=IDB[:], in_=IDENT[:]), [IDENT], [IDB])
            rkv = sbt("rkv", (128, 6, 256)); tmp6 = sbt("tmp6", (128, 6, 256))
            kk = sbt("kk"); kksq = sbt("kksq"); kkn = sbt("kkn"); rn = sbt("rn")
            sg = sbt("sg"); cs = sbt("cs"); ex = sbt("ex"); sbb = sbt("sbb"); sbx = sbt("sbx")
            eg = sbt("eg"); eng = sbt("eng"); ega = sbt("ega"); av = sbt("av"); tB = sbt("tB"); t1 = sbt("t1"); kd = sbt("kd")
            bon = sbt("bon"); yst = sbt("yst")
            AR = [sbt("AR%d" % q, (128, 4, 256), BF16) for q in range(2)]
            Bt = [sbt("Bt%d" % q, (128, 4, 128), BF16) for q in range(2)]; Kt = [sbt("Kt%d" % q, (128, 4, 128), BF16) for q in range(2)]
            Bht = [sbt("Bht%d" % q, (128, 4, 128), BF16) for q in range(2)]; Kht = [sbt("Kht%d" % q, (128, 4, 128), BF16) for q in range(2)]
            Vt = [sbt("Vt%d" % q, (128, 4, 128), BF16) for q in range(2)]
            for tl in AR + Bt + Kt + Bht + Kht + Vt:
                k.G(lambda h, tl=tl: h.memset(tl[:], 0.0), [], [tl])
            MA = sbt("MA", (128, 4, 256), BF16); MB = sbt("MB", (128, 4, 256), BF16)
            Pq = [sbt("Pq%d" % i, (128, 4, 128), BF16) for i in range(2)]; PTq = [sbt("PTq%d" % i, (128, 4, 128), BF16) for i in range(2)]
            Xq = [sbt("Xq%d" % i, (128, 4, 256), BF16) for i in range(2)]
            VTk = sbt("VTk", (128, 4, 128), BF16); BhT = sbt("BhT", (128, 4, 128), BF16); KhT = sbt("KhT", (128, 4, 128), BF16)
            RTp = sbt("RTp", (128, 4, 128), BF16); YL = sbt("YL", (128, 4, 128)); MCT = sbt("MCT", (128, 4, 128), BF16); NCt = sbt("NCt", (128, 4, 128))
            Z = {}
            for q in range(2):
                for d in range(2):
                    Z[(q, d)] = [sbt("Z%d%d%d" % (q, d, i), (128, 128), BF16) for i in range(2)]
                    z0 = Z[(q, d)][0]
                    k.G(lambda h, z=z0: h.memset(z[:], 0.0), [], [z0])
            zi = {(q, d): 0 for q in range(2) for d in range(2)}
            npass = [0]

            def bdw(dst, dstslice, fn_in, reads, eng=None):
                pass

            def scan_pass(c0, d):
                n = npass[0]; npass[0] += 1
                u6 = u6s[n % 2]; lo = los[n % 2]
                k.load(u6[:], UT[0:768, c0 - 1:c0 + 257].rearrange("(j p) t -> p j t", p=128), UT, u6)
                k.load(lo[:, 0, :], UT[768:864, c0:c0 + 256], UT, lo)
                k.load(lo[:, 1, :], UT[864:960, c0:c0 + 256], UT, lo)
                cw = lambda tap: bc(convw[:, l, :, tap:tap + 1], [128, 6, 256])
                k.V(lambda h: h.tensor_tensor(out=rkv[:], in0=u6[:, :, 0:256], in1=cw(0), op=ALU.mult), [u6, convw], [rkv])
                k.G(lambda h: h.tensor_tensor(out=tmp6[:], in0=u6[:, :, 1:257], in1=cw(1), op=ALU.mult), [u6, convw], [tmp6])
                k.V(lambda h: h.tensor_tensor(out=rkv[:], in0=rkv[:], in1=tmp6[:], op=ALU.add), [rkv, tmp6], [rkv])
                k.G(lambda h: h.tensor_tensor(out=tmp6[:], in0=u6[:, :, 2:258], in1=cw(2), op=ALU.mult), [u6, convw], [tmp6])
                k.V(lambda h: h.tensor_tensor(out=rkv[:], in0=rkv[:], in1=tmp6[:], op=ALU.add), [rkv, tmp6], [rkv])
                R_ = rkv[:, 0:2, :]; K_ = rkv[:, 2:4, :]; V_ = rkv[:, 4:6, :]
                hvb = lambda w: bc(hv[:, l, w, :].unsqueeze(2), [128, 2, 256])
                k.V(lambda h: h.tensor_tensor(out=kk[:], in0=K_, in1=hvb(0), op=ALU.mult), [rkv, hv], [kk])
                k.G(lambda h: h.tensor_tensor(out=kksq[:], in0=kk[:], in1=kk[:], op=ALU.mult), [kk], [kksq])
                bk = k.bank()
                for q in range(2):
                    k.mm(bk[:, q * 256:(q + 1) * 256], BONES[:], kksq[:, q, :], True, True, [BONES, kksq], [bk])
                k.V(lambda h: h.tensor_scalar(out=rn[:].rearrange("p q t -> p (q t)"), in0=bk[:, 0:512], scalar1=1e-24, scalar2=None,
                                              op0=ALU.max), [bk], [rn])
                rsq(k, rn[:], rn[:], 1.0, 0.0, [rn], rn)
                k.V(lambda h: h.tensor_tensor(out=kkn[:], in0=kk[:], in1=rn[:], op=ALU.mult), [kk, rn], [kkn])
                k.V(lambda h: h.tensor_scalar(out=kksq[:], in0=kkn[:], scalar1=-1.0, scalar2=None, op0=ALU.mult), [kkn, kksq], [kksq])
                bk = k.bank(); bk2 = k.bank()
                for q in range(2):
                    k.mm(bk[:, q * 256:(q + 1) * 256], dup[:, d, q * 128:(q + 1) * 128], lo[:, 0, :], True, True, [dup, lo], [bk])
                    k.mm(bk2[:, q * 256:(q + 1) * 256], iup[:, d, q * 128:(q + 1) * 128], lo[:, 1, :], True, True, [iup, lo], [bk2])
                for q in range(2):
                    k.A(lambda h, q=q: h.activation(out=sg[:, q, :], in_=bk[:, q * 256:(q + 1) * 256], func=AF.Sigmoid,
                                                    bias=dbias[:, l, d, q:q + 1], scale=1.0), [bk, dbias], [sg])
                    k.A(lambda h, q=q: h.activation(out=av[:, q, :], in_=bk2[:, q * 256:(q + 1) * 256], func=AF.Sigmoid,
                                                    bias=ibias[:, l, d, q:q + 1], scale=1.0), [bk2, ibias], [av])
                for q in range(2):
                    k.V(lambda h, q=q: h.tensor_tensor_scan(out=cs[:, q, :], data0=RST[:], data1=sg[:, q, :], initial=0.0,
                                                            op0=ALU.mult, op1=ALU.add), [RST, sg], [cs])
                k.V(lambda h: h.tensor_tensor(out=ex[:], in0=cs[:], in1=sg[:], op=ALU.subtract), [cs, sg], [ex])
                if d == 0:
                    Sin, Sex, Tin, Tex = cs, ex, cs, ex
                    gcol = 63
                else:
                    cs5 = cs[:].rearrange("p q (c t) -> p q c t", t=64)
                    k.V(lambda h: h.tensor_tensor(out=sbb[:].rearrange("p q (c t) -> p q c t", t=64),
                                                  in0=bc(cs5[:, :, :, 63:64], [128, 2, 4, 64]),
                                                  in1=ex[:].rearrange("p q (c t) -> p q c t", t=64), op=ALU.subtract), [cs, ex], [sbb])
                    k.V(lambda h: h.tensor_tensor(out=sbx[:], in0=sbb[:], in1=sg[:], op=ALU.subtract), [sbb, sg], [sbx])
                    Sin, Sex, Tin, Tex = sbb[:], sbx[:], sbb, sbx
                    gcol = 0
                SinA = Sin[:] if d == 0 else Sin
                SexA = Sex[:] if d == 0 else Sex
                k.A(lambda h: h.activation(out=eg[:], in_=SinA, func=AF.Exp, scale=-C0), [Tin], [eg])
                k.A(lambda h: h.activation(out=eng[:], in_=SinA, func=AF.Exp, scale=C0), [Tin], [eng])
                k.A(lambda h: h.activation(out=ega[:], in_=SexA, func=AF.Exp, scale=-C0), [Tex], [ega])
                k.V(lambda h: h.tensor_tensor(out=t1[:], in0=av[:], in1=hvb(1), op=ALU.mult), [av, hv], [t1])
                k.V(lambda h: h.tensor_tensor(out=t1[:], in0=t1[:], in1=hvb(1), op=ALU.subtract), [t1, hv], [t1])
                k.V(lambda h: h.tensor_scalar(out=t1[:], in0=t1[:], scalar1=1.0, scalar2=None, op0=ALU.add), [t1], [t1])
                k.V(lambda h: h.tensor_tensor(out=kd[:], in0=t1[:], in1=K_, op=ALU.mult), [t1, rkv], [kd])
                k.G(lambda h: h.tensor_tensor(out=tB[:], in0=av[:], in1=eng[:], op=ALU.mult), [av, eng], [tB])
                k.G(lambda h: h.tensor_tensor(out=t1[:], in0=kd[:], in1=hvb(2), op=ALU.mult), [kd, hv, t1], [t1])
                k.G(lambda h: h.tensor_tensor(out=t1[:], in0=t1[:], in1=R_, op=ALU.mult), [t1, rkv], [t1])
                bk = k.bank()
                for q in range(2):
                    k.mm(bk[:, q * 256:(q + 1) * 256], BONES[:], t1[:, q, :], True, True, [BONES, t1], [bk])
                k.V(lambda h: h.tensor_tensor(out=bon[:].rearrange("p q t -> p (q t)"), in0=bk[:, 0:512],
                                              in1=rkv[:, 4:6, :].rearrange("p q t -> p (q t)"), op=ALU.mult), [bk, rkv], [bon])
                k.store(YY[2 + d, :, c0:c0 + 256].rearrange("(q p) t -> p q t", p=128), bon[:], bon, YY)
                egv = eg[:].rearrange("p q (c t) -> p q c t", t=64)
                ei = 0
                for q in range(2):
                    for hh in range(2):
                        ps = slice(hh * 64, hh * 64 + 64)
                        v4 = lambda tl, lo_=0: tl[ps, :, lo_ + hh * 64:lo_ + hh * 64 + 64]
                        s4 = lambda tl: tl[ps, q, :].rearrange("p (c t) -> p c t", t=64)
                        r4 = lambda j: rkv[ps, j * 2 + q, :].rearrange("p (c t) -> p c t", t=64)
                        gam = bc(egv[ps, q, :, gcol:gcol + 1], [64, 4, 64])
                        E = [k.G, k.G]
                        def em(fn, reads, writes):
                            nonlocal ei
                            ei += 1
                            E[ei % 2](fn, reads, writes)
                        em(lambda h: h.tensor_tensor(out=v4(AR[q]), in0=s4(kksq), in1=s4(ega), op=ALU.mult), [kksq, ega], [AR[q]])
                        em(lambda h: h.tensor_tensor(out=v4(AR[q], 128), in0=r4(0), in1=s4(eg), op=ALU.mult), [rkv, eg], [AR[q]])
                        em(lambda h: h.tensor_tensor(out=v4(Bt[q]), in0=s4(kkn), in1=s4(tB), op=ALU.mult), [kkn, tB], [Bt[q]])
                        em(lambda h: h.tensor_tensor(out=v4(Kt[q]), in0=s4(kd), in1=s4(eng), op=ALU.mult), [kd, eng], [Kt[q]])
                        em(lambda h: h.tensor_tensor(out=v4(Bht[q]), in0=v4(Bt[q]), in1=gam, op=ALU.mult), [Bt[q], eg], [Bht[q]])
                        em(lambda h: h.tensor_tensor(out=v4(Kht[q]), in0=v4(Kt[q]), in1=gam, op=ALU.mult), [Kt[q], eg], [Kht[q]])
                        em(lambda h: h.tensor_copy(out=v4(Vt[q]), in_=r4(2)), [rkv], [Vt[q]])
                M2 = MF2 if d == 0 else MB2
                MN = MNF if d == 0 else MNB
                order = [0, 1, 2, 3] if d == 0 else [3, 2, 1, 0]
                for q in range(2):
                    for (lt, dst) in ((Bt[q], MA), (Kt[q], MB)):
                        for hf in range(2):
                            bk = k.bank()
                            for cc in range(2):
                                c = hf * 2 + cc
                                k.mm(bk[:, cc * 256:(cc + 1) * 256], lt[:, c, :], AR[q][:, c, :], True, True, [lt, AR[q]], [bk])
                            k.V(lambda h, bk=bk, hf=hf, dst=dst: h.tensor_tensor(out=dst[:, hf * 2:hf * 2 + 2, :],
                                in0=bk[:, 0:512].rearrange("p (c n) -> p c n", n=256), in1=bc(M2[:], [128, 2, 256]), op=ALU.mult), [bk, M2], [dst])
                    P0 = Pq[0]
                    bk = k.bank()
                    for c in range(4):
                        k.mm(bk[:, c * 128:(c + 1) * 128], AR[q][:, c, 0:128], Bt[q][:, c, :], True, True, [AR[q], Bt[q]], [bk])
                    k.V(lambda h, bk=bk: h.tensor_tensor(out=P0[:], in0=bk[:, 0:512].rearrange("p (c n) -> p c n", n=128), in1=bc(MN[:], [128, 4, 128]), op=ALU.mult), [bk, MN], [P0])
                    X = Xq[0]
                    def tr(src_ap_fn, srcT, dst_ap, dstT):
                        bk = k.bank()
                        for c in range(4):
                            k.mm(bk[:, c * 128:(c + 1) * 128], src_ap_fn(c), IDB[:], True, True, [srcT, IDB], [bk])
                        k.cp(dst_ap, bk[:, 0:512].rearrange("p (c n) -> p c n", n=128), [bk], [dstT])
                    tr(lambda c: AR[q][:, c, 0:128], AR[q], X[:, :, 0:128], X)
                    tr(lambda c: Vt[q][:, c, :], Vt[q], VTk[:], VTk)
                    tr(lambda c: Bht[q][:, c, :], Bht[q], BhT[:], BhT)
                    tr(lambda c: Kht[q][:, c, :], Kht[q], KhT[:], KhT)
                    bk = k.bank()
                    for c in range(4):
                        k.mm(bk[:, c * 128:(c + 1) * 128], MB[:, c, 0:128], VTk[:, c, :], True, True, [MB, VTk], [bk])
                    k.cp(X[:, :, 128:256], bk[:, 0:512].rearrange("p (c n) -> p c n", n=128), [bk], [X])
                    xi = 0; pi = 0
                    Pc = P0; PTc_ap = lambda c: MA[:, c, 0:128]; PTc_T = MA
                    for it in range(6):
                        Xn = Xq[1 - xi]
                        for hf in range(2):
                            bk = k.bank()
                            for cc in range(2):
                                c = hf * 2 + cc
                                k.mm(bk[:, cc * 256:(cc + 1) * 256], PTc_ap(c), Xq[xi][:, c, :], True, True, [PTc_T, Xq[xi]], [bk])
                            k.V(lambda h, bk=bk, hf=hf, Xn=Xn, Xo=Xq[xi]: h.tensor_tensor(out=Xn[:, hf * 2:hf * 2 + 2, :],
                                in0=bk[:, 0:512].rearrange("p (c n) -> p c n", n=256), in1=Xo[:, hf * 2:hf * 2 + 2, :], op=ALU.add), [bk, Xq[xi]], [Xn])
                        xi = 1 - xi
                        if it < 5:
                            Pn = Pq[1 - pi]; PTn = PTq[1 - pi]
                            bk = k.bank(); bk2 = k.bank()
                            for c in range(4):
                                k.mm(bk[:, c * 128:(c + 1) * 128], PTc_ap(c), Pc[:, c, :], True, True, [PTc_T, Pc], [bk])
                                k.mm(bk2[:, c * 128:(c + 1) * 128], Pc[:, c, :], PTc_ap(c), True, True, [PTc_T, Pc], [bk2])
                            k.cp(Pn[:], bk[:, 0:512].rearrange("p (c n) -> p c n", n=128), [bk], [Pn])
                            k.cp(PTn[:], bk2[:, 0:512].rearrange("p (c n) -> p c n", n=128), [bk2], [PTn])
                            pi = 1 - pi
                            Pc = Pn; PTc_T = PTn; PTc_ap = (lambda c, PTn=PTn: PTn[:, c, :])
                    X = Xq[xi]
                    bk = k.bank(); bk2 = k.bank()
                    for c in range(4):
                        k.mm(bk[:, c * 128:(c + 1) * 128], X[:, c, 0:128], MA[:, c, 128:256], True, True, [X, MA], [bk])
                    for c in range(4):
                        k.mm(bk2[:, c * 128:(c + 1) * 128], X[:, c, 128:256], MA[:, c, 128:256], True, False, [X, MA], [bk2])
                        k.mm(bk2[:, c * 128:(c + 1) * 128], VTk[:, c, :], MB[:, c, 128:256], False, True, [VTk, MB], [bk2])
                    k.V(lambda h, bk=bk: h.tensor_tensor(out=RTp[:], in0=bk[:, 0:512].rearrange("p (c n) -> p c n", n=128),
                                                         in1=AR[q][:, :, 128:256], op=ALU.add), [bk, AR[q]], [RTp])
                    k.cp(YL[:], bk2[:, 0:512].rearrange("p (c n) -> p c n", n=128), [bk2], [YL])
                    bk = k.bank(); bk2 = k.bank()
                    for c in range(4):
                        k.mm(bk[:, c * 128:(c + 1) * 128], X[:, c, 0:128], BhT[:, c, :], True, True, [X, BhT], [bk])
                    for c in range(4):
                        k.mm(bk2[:, c * 128:(c + 1) * 128], BhT[:, c, :], X[:, c, 128:256], True, False, [X, BhT], [bk2])
                        k.mm(bk2[:, c * 128:(c + 1) * 128], KhT[:, c, :], VTk[:, c, :], False, True, [KhT, VTk], [bk2])
                    for c in range(4):
                        k.V(lambda h, c=c, bk=bk: h.scalar_tensor_tensor(out=MCT[:, c, :], in0=IDENT[:], scalar=eg[:, q, c * 64 + gcol:c * 64 + gcol + 1],
                            in1=bk[:, c * 128:(c + 1) * 128], op0=ALU.mult, op1=ALU.add), [IDENT, eg, bk], [MCT])
                    k.cp(NCt[:], bk2[:, 0:512].rearrange("p (c n) -> p c n", n=128), [bk2], [NCt])
                    for c in order:
                        zc = Z[(q, d)][zi[(q, d)]]; zn = Z[(q, d)][1 - zi[(q, d)]]
                        bk = k.bank(); bk2 = k.bank()
                        k.mm(bk[:, 0:128], zc[:], RTp[:, c, :], True, True, [zc, RTp], [bk])
                        k.mm(bk2[:, 0:128], MCT[:, c, :], zc[:], True, True, [zc, MCT], [bk2])
                        for hh in range(2):
                            ps = slice(hh * 64, hh * 64 + 64)
                            k.V(lambda h, ps=ps, hh=hh, c=c, bk=bk: h.tensor_tensor(out=yst[ps, q, c * 64:(c + 1) * 64],
                                in0=bk[ps, hh * 64:hh * 64 + 64], in1=YL[ps, c, hh * 64:hh * 64 + 64], op=ALU.add), [bk, YL], [yst])
                        k.V(lambda h, c=c, bk2=bk2, zn=zn: h.tensor_tensor(out=zn[:], in0=bk2[:, 0:128], in1=NCt[:, c, :], op=ALU.add), [bk2, NCt], [zn])
                        zi[(q, d)] = 1 - zi[(q, d)]
                k.store(YY[d, :, c0:c0 + 256].rearrange("(q p) t -> p q t", p=128), yst[:], yst, YY)

            fseq = [CTX0] + [LAT0 + 256 * i for i in range(64)]
            bseq = [CTX0] + [LAT0 + 256 * i for i in reversed(range(64))]
            for i in range(65):
                scan_pass(fseq[i], 0)
                scan_pass(bseq[i], 1)
        k.barrier()
        if dbg and "YY" in dbg and l == 0:
            k.dma(k.pool, dbg_t["YY"][:], YY[:], [YY], [dbg_t["YY"]])
            k.barrier()

        with ExitStack() as es:
            gup = k.sb(es, "gup", [128, 2, 256]); poolw = k.sb(es, "poolw", [128, 2, 256]); wo = k.sb(es, "wo", [128, 10, D])
            k.load(gup[:], gup_in[:, l], gup_in, gup); k.load(poolw[:], poolw_in[:, l], poolw_in, poolw)
            k.load(wo[:], wout_in[l], wout_in, wo, r=True)
            yy = k.sb(es, "yy", [128, 4, 2, 256]); glo = k.sb(es, "glo", [128, 2, 256])
            up = k.sb(es, "up", [128, 2, 20, 80]); pa = k.sb(es, "pa", [128, 2, 20, 80]); pb = k.sb(es, "pb", [128, 2, 20, 80])
            va = k.sb(es, "va", [128, 2, 20, 64]); vb = k.sb(es, "vb", [128, 2, 20, 64])
            cu = k.sb(es, "cu", [128, 2, 288]); ca = k.sb(es, "ca", [128, 2, 288]); cb = k.sb(es, "cb", [128, 2, 288])
            yv = k.sb(es, "yv", [128, 2, 256]); dv = k.sb(es, "dv", [128, 2, 256]); dsq = k.sb(es, "dsq", [128, 2, 256])
            mu = k.sb(es, "mu", [128, 2, 256]); dd = mu; tmpd = dsq
            mq = k.sb(es, "mq", [128, 10, 256]); ost = k.sb(es, "ost", [128, 4, 256])
            for tl in (up, pa, pb, va, vb, cu, ca, cb):
                k.G(lambda h, tl=tl: h.memset(tl[:], 0.0), [], [tl])
            hvb = lambda w: bc(hv[:, l, w, :].unsqueeze(2), [128, 2, 256])
            GRP = [(0, slice(0, 64)), (0, slice(64, 128)), (1, slice(0, 64)), (1, slice(64, 128))]

            def blk3c(c0, kind, bi):
                k.load(yy[:], YY[:, :, c0:c0 + 256].rearrange("w (q p) t -> p w q t", p=128), YY, yy)
                k.load(glo[:], UT[960:1216, c0:c0 + 256].rearrange("(q p) t -> p q t", p=128), UT, glo)
                k.V(lambda h: h.tensor_tensor(out=yv[:], in0=yy[:, 0], in1=yy[:, 1], op=ALU.add), [yy], [yv])
                bk = k.bank()
                for q in range(2):
                    k.mm(bk[:, q * 256:(q + 1) * 256], BONES[:], yv[:, q, :], True, True, [BONES, yv], [bk])
                k.V(lambda h: h.scalar_tensor_tensor(out=dv[:].rearrange("p q t -> p (q t)"), in0=bk[:, 0:512], scalar=-1.0 / 64,
                    in1=yv[:].rearrange("p q t -> p (q t)"), op0=ALU.mult, op1=ALU.add), [bk, yv], [dv])
                k.G(lambda h: h.tensor_tensor(out=dsq[:], in0=dv[:], in1=dv[:], op=ALU.mult), [dv], [dsq])
                bk = k.bank()
                for q in range(2):
                    k.mm(bk[:, q * 256:(q + 1) * 256], BONES[:], dsq[:, q, :], True, True, [BONES, dsq], [bk])
                rsq(k, mu[:].rearrange("p q t -> p (q t)"), bk[:, 0:512], 1.0 / 64, 64e-5, [bk], mu)
                k.V(lambda h: h.tensor_tensor(out=dv[:], in0=dv[:], in1=mu[:], op=ALU.mult), [dv, mu], [dv])
                k.V(lambda h: h.tensor_tensor(out=dv[:], in0=dv[:], in1=hvb(3), op=ALU.mult), [dv, hv], [dv])
                k.V(lambda h: h.tensor_tensor(out=dv[:], in0=dv[:], in1=hvb(4), op=ALU.add), [dv, hv], [dv])
                k.V(lambda h: h.tensor_tensor(out=dv[:], in0=dv[:], in1=yy[:, 2], op=ALU.add), [dv, yy], [dv])
                k.V(lambda h: h.tensor_tensor(out=dv[:], in0=dv[:], in1=yy[:, 3], op=ALU.add), [dv, yy], [dv])
                bk = k.bank()
                for q in range(2):
                    for kc in range(2):
                        k.mm(bk[:, q * 256:(q + 1) * 256], gup[:, kc, q * 128:(q + 1) * 128], glo[:, kc, :], kc == 0, kc == 1, [gup, glo], [bk])
                k.V(lambda h: h.tensor_tensor(out=mq[:, 0:2, :].rearrange("p q t -> p (q t)").bitcast(F32R), in0=bk[:, 0:512],
                                              in1=dv[:].rearrange("p q t -> p (q t)"), op=ALU.mult), [bk, dv], [mq])
                if kind == "lat":
                    r0 = bi * 4
                    k.load(up[:, 0, :, 8:72], UT[1216:1344, c0 - 512:c0 + 768].rearrange("p (r c) -> p r c", c=64), UT, up)
                    k.load(up[:, 1, :, 8:72], UT[1344:1472, c0 - 512:c0 + 768].rearrange("p (r c) -> p r c", c=64), UT, up)
                    k.V(lambda h: h.tensor_tensor(out=pb[:, :, :, 1:80], in0=up[:, :, :, 0:79], in1=up[:, :, :, 1:80], op=ALU.add), [up], [pb])
                    k.V(lambda h: h.tensor_tensor(out=pa[64:128, 0, :, 2:79], in0=pb[64:128, 0, :, 1:78], in1=pb[64:128, 0, :, 3:80], op=ALU.add), [pb], [pa])
                    k.V(lambda h: h.tensor_tensor(out=pa[:, 1, :, 2:79], in0=pb[:, 1, :, 1:78], in1=pb[:, 1, :, 3:80], op=ALU.add), [pb], [pa])
                    k.V(lambda h: h.tensor_tensor(out=pb[:, 1, :, 4:77], in0=pa[:, 1, :, 2:75], in1=pa[:, 1, :, 6:79], op=ALU.add), [pa], [pb])
                    k.V(lambda h: h.tensor_tensor(out=pa[64:128, 1, :, 8:73], in0=pb[64:128, 1, :, 4:69], in1=pb[64:128, 1, :, 12:77], op=ALU.add), [pb], [pa])
                    hs = [pb, pa, pb, pa]
                    for gi, (pc, ps) in enumerate(GRP):
                        k.G(lambda h, gi=gi, pc=pc, ps=ps: h.tensor_tensor(out=va[ps, pc, 1:20, :], in0=hs[gi][ps, pc, 0:19, 8:72],
                                                                          in1=hs[gi][ps, pc, 1:20, 8:72], op=ALU.add), [pa, pb], [va])
                    k.G(lambda h: h.tensor_tensor(out=vb[64:128, 0, 2:19, :], in0=va[64:128, 0, 1:18, :], in1=va[64:128, 0, 3:20, :], op=ALU.add), [va], [vb])
                    k.G(lambda h: h.tensor_tensor(out=vb[:, 1, 2:19, :], in0=va[:, 1, 1:18, :], in1=va[:, 1, 3:20, :], op=ALU.add), [va], [vb])
                    k.G(lambda h: h.tensor_tensor(out=va[:, 1, 4:17, :], in0=vb[:, 1, 2:15, :], in1=vb[:, 1, 6:19, :], op=ALU.add), [vb], [va])
                    k.G(lambda h: h.tensor_tensor(out=vb[64:128, 1, 8:13, :], in0=va[64:128, 1, 4:9, :], in1=va[64:128, 1, 12:17, :], op=ALU.add), [va], [vb])
                    vs = [va, vb, va, vb]
                    var = 0 if bi == 0 else (1 if bi == 1 else (3 if bi == 62 else (4 if bi == 63 else 2)))
                    for gi, (pc, ps) in enumerate(GRP):
                        d4 = lambda tl: tl[ps, pc, :].rearrange("p (r c) -> p r c", c=64)
                        k.V(lambda h: h.tensor_tensor(out=d4(tmpd), in0=vs[gi][ps, pc, 8:12, :], in1=rc2[ps, var, pc], op=ALU.mult), [va, vb, rc2], [tmpd])
                        k.V(lambda h: h.tensor_tensor(out=d4(dd), in0=d4(tmpd), in1=up[ps, pc, 8:12, 8:72], op=ALU.subtract), [tmpd, up], [dd])
                else:
                    k.load(cu[:, 0, 16:272], UT[1216:1344, c0:c0 + 256], UT, cu)
                    k.load(cu[:, 1, 16:272], UT[1344:1472, c0:c0 + 256], UT, cu)
                    k.V(lambda h: h.tensor_tensor(out=cb[:, :, 1:288], in0=cu[:, :, 0:287], in1=cu[:, :, 1:288], op=ALU.add), [cu], [cb])
                    k.V(lambda h: h.tensor_tensor(out=ca[64:128, 0, 2:287], in0=cb[64:128, 0, 1:286], in1=cb[64:128, 0, 3:288], op=ALU.add), [cb], [ca])
                    k.V(lambda h: h.tensor_tensor(out=ca[:, 1, 2:287], in0=cb[:, 1, 1:286], in1=cb[:, 1, 3:288], op=ALU.add), [cb], [ca])
                    k.V(lambda h: h.tensor_tensor(out=cb[:, 1, 4:285], in0=ca[:, 1, 2:283], in1=ca[:, 1, 6:287], op=ALU.add), [ca], [cb])
                    k.V(lambda h: h.tensor_tensor(out=ca[64:128, 1, 8:281], in0=cb[64:128, 1, 4:277], in1=cb[64:128, 1, 12:285], op=ALU.add), [cb], [ca])
                    hs = [cb, ca, cb, ca]
                    for gi, (pc, ps) in enumerate(GRP):
                        k.V(lambda h: h.tensor_tensor(out=tmpd[ps, pc, :], in0=hs[gi][ps, pc, 16:272], in1=rc1[ps, pc, :], op=ALU.mult), [ca, cb, rc1], [tmpd])
                        k.V(lambda h: h.tensor_tensor(out=dd[ps, pc, :], in0=tmpd[ps, pc, :], in1=cu[ps, pc, 16:272], op=ALU.subtract), [tmpd, cu], [dd])
                for gi, (pc, ps) in enumerate(GRP):
                    for nn in range(2):
                        bk = k.bank()
                        k.mm(bk[:, 0:256], poolw[ps, pc, nn * 128:(nn + 1) * 128], dd[ps, pc, :], True, True, [poolw, dd], [bk])
                        k.A(lambda h, bk=bk, gi=gi, nn=nn: h.activation(out=mq[:, 2 + gi * 2 + nn, :].bitcast(F32R), in_=bk[:, 0:256], func=AF.Copy,
                                                                       scale=pscale[:, l, gi * 2 + nn:gi * 2 + nn + 1]), [bk, pscale], [mq])
                for n4 in range(4):
                    for nn in range(4):
                        n_ = n4 * 4 + nn
                        bk = k.bank()
                        for kc in range(10):
                            k.mm(bk[:, 0:256], wo[:, kc, n_ * 128:(n_ + 1) * 128], mq[:, kc, :], kc == 0, kc == 9, [wo, mq], [bk], r=True)
                        k.cp(ost[:, nn, :], bk[:, 0:256], [bk], [ost])
                    if kind == "lat":
                        r = bi // 16; lc = 64 + (bi % 16) * 256
                        cdma(False, PT, r * D + n4 * 512, 512, lambda off, ln: ost[:, :, off:off + ln], ost, lc, 256)
                    else:
                        for r in range(4):
                            cdma(False, PT, r * D + n4 * 512, 512, lambda off, ln, r=r: ost[:, :, r * 64 + off:r * 64 + off + ln], ost, 0, 64)

            blk3c(CTX0, "ctx", 0)
            for bi in range(64):
                blk3c(LAT0 + 256 * bi, "lat", bi)
            for j in range(NCH):
                coll("ReduceScatter", ALU.add, PT, DT, PT[j], DT[j])
        k.barrier()
        FT = 416
        with ExitStack() as es:
            xt = k.sb(es, "f_x", [128, 16, FT]); h2 = k.sb(es, "f_h", [128, 16, FT], BF16); sq = k.sb(es, "f_sq", [128, 16, FT])
            rs = k.sb(es, "f_rs", [128, FT]); ff = k.sb(es, "f_f", [128, FC, FT], BF16); sl = k.sb(es, "f_sl", [128, FT])
            wgs = [k.sb(es, "f_wg%d" % i, [128, 16, 128], BF16) for i in range(2)]; wus = [k.sb(es, "f_wu%d" % i, [128, 16, 128], BF16) for i in range(2)]
            wds = [k.sb(es, "f_wd%d" % i, [128, FC, 128], BF16) for i in range(2)]
            last = (l == L - 1)
            for i in range(NTL // FT):
                lt0 = i * FT
                k.load(xt[:], xrows(XS, lt0, lt0 + FT), XS, xt)
                cdma(True, DT, 0, D, lambda off, ln: sq[:, :, off:off + ln], sq, lt0, FT)
                modulate(sq, sq, lt0, FT, l, 2, None, [])
                k.V(lambda h: h.tensor_tensor(out=xt[:], in0=xt[:], in1=sq[:], op=ALU.add), [xt, sq], [xt])
                k.A(lambda h: h.activation(out=sq[:], in_=xt[:], func=AF.Square), [xt], [sq])
                bk = k.bank()
                for kc in range(16):
                    k.mm(bk[:, 0:FT], ONES[:], sq[:, kc, :], kc == 0, kc == 15, [ONES, sq], [bk])
                rsq(k, rs[:], bk[:, 0:FT], 1.0 / D, 1e-6, [bk], rs)
                k.V(lambda h: h.tensor_tensor(out=sq[:], in0=xt[:], in1=bc(rs[:].unsqueeze(1), [128, 16, FT]), op=ALU.mult), [xt, rs], [sq])
                modulate(sq, sq, lt0, FT, l, 4, 3, [], out_tile=h2, outT=h2)
                for j in range(FC):
                    wg = wgs[j % 2]; wu = wus[j % 2]
                    k.load(wg[:], fg_in[l, j], fg_in, wg, cast=True)
                    k.load(wu[:], fu_in[l, j], fu_in, wu, cast=True)
                    bk = k.bank(); bk2 = k.bank()
                    for kc in range(16):
                        k.mm(bk[:, 0:FT], wg[:, kc, :], h2[:, kc, :], kc == 0, kc == 15, [wg, h2], [bk])
                    for kc in range(16):
                        k.mm(bk2[:, 0:FT], wu[:, kc, :], h2[:, kc, :], kc == 0, kc == 15, [wu, h2], [bk2])
                    k.A(lambda h, bk=bk: h.activation(out=sl[:], in_=bk[:, 0:FT], func=AF.Silu), [bk], [sl])
                    k.V(lambda h, bk2=bk2, j=j: h.tensor_tensor(out=ff[:, j, :], in0=bk2[:, 0:FT], in1=sl[:], op=ALU.mult), [bk2, sl], [ff])
                for n_ in range(16):
                    wd = wds[n_ % 2]
                    k.load(wd[:], fd_in[l, n_], fd_in, wd, cast=True)
                    bk = k.bank()
                    for fc in range(FC):
                        k.mm(bk[:, 0:FT], wd[:, fc, :], ff[:, fc, :], fc == 0, fc == FC - 1, [wd, ff], [bk])
                    segs = [(0, 64, 1), (64, FT, 0)] if lt0 == 0 else [(0, FT, 0)]
                    for (a, b_, m) in segs:
                        k.V(lambda h, a=a, b_=b_, m=m, n_=n_, bk=bk: h.scalar_tensor_tensor(out=xt[:, n_, a:b_], in0=bk[:, a:b_],
                            scalar=DER[:, l, 5, n_, m:m + 1], in1=xt[:, n_, a:b_], op0=ALU.mult, op1=ALU.add), [bk, DER, xt], [xt])
                if not last:
                    k.store(xrows(XT, lt0, lt0 + FT), xt[:], xt, XT)
                else:
                    k.A(lambda h: h.activation(out=sq[:], in_=xt[:], func=AF.Square), [xt], [sq])
                    bk = k.bank()
                    for kc in range(16):
                        k.mm(bk[:, 0:FT], ONES[:], sq[:, kc, :], kc == 0, kc == 15, [ONES, sq], [bk])
                    rsq(k, rs[:], bk[:, 0:FT], 1.0 / D, 1e-6, [bk], rs)
                    k.V(lambda h: h.tensor_tensor(out=xt[:], in0=xt[:], in1=bc(rs[:].unsqueeze(1), [128, 16, FT]), op=ALU.mult), [xt, rs], [xt])
                    k.V(lambda h: h.tensor_tensor(out=xt[:], in0=xt[:], in1=bc(fin[:].unsqueeze(2), [128, 16, FT]), op=ALU.mult), [xt, fin], [xt])
                    k.store(xrows(out_d, lt0, lt0 + FT), xt[:], xt, out_d)
        k.barrier()
        XS = XT
    return k


def _host_inputs(inp, depth=4):
    f = np.float32
    L = depth
    x = inp["x"]; ctx = inp["ctx"]
    ORK = [0, 1024, 2048]
    maps = []
    rc2 = np.zeros((128, 5, 2, 4, 64), f); rc1 = np.zeros((128, 2, 256), f)
    for pc in range(2):
        for half in range(2):
            gi = pc * 2 + half; W = 2 ** (gi + 1); hf = W // 2
            ps = slice(half * 64, half * 64 + 64)
            cc = np.arange(64); ccnt = np.clip(cc + hf, 0, 64) - np.clip(cc - hf, 0, 64)
            for var, r0 in enumerate([0, 4, 128, 248, 252]):
                rr = np.arange(r0, r0 + 4); rcnt = np.clip(rr + hf, 0, 256) - np.clip(rr - hf, 0, 256)
                rc2[ps, var, pc] = (1.0 / (rcnt[:, None] * ccnt[None, :])).astype(f)[None]
            tt = np.arange(256); tcnt = np.clip(tt + hf, 0, 256) - np.clip(tt - hf, 0, 256)
            rc1[ps, pc] = (1.0 / tcnt).astype(f)[None]
    pm = lambda v: np.ascontiguousarray(v.reshape(-1, 128).T)
    FG = np.ascontiguousarray(inp["ffn_gate"][:L].reshape(L, 16, 128, FC, 128).transpose(0, 3, 2, 1, 4))
    FU = np.ascontiguousarray(inp["ffn_up"][:L].reshape(L, 16, 128, FC, 128).transpose(0, 3, 2, 1, 4))
    FD = np.ascontiguousarray(inp["ffn_down"][:L].reshape(L, FC, 128, 16, 128).transpose(0, 3, 2, 1, 4))
    for c in range(8):
        b, s_ = c // 4, c % 4
        g = s_
        m = {}
        m["xT"] = np.ascontiguousarray(np.concatenate([ctx[b, 64 * s_:64 * s_ + 64], x[b, 4096 * s_:4096 * s_ + 4096]], 0).T)
        cT = np.zeros((128, 16, 2), f); cT[:, :, 0] = pm(inp["c"][b]); cT[:, :, 1] = pm(inp["c_ctx"]); m["cT"] = cT
        m["adaw"] = np.ascontiguousarray(inp["ada_w"][:L, :, 3072 * s_:3072 * s_ + 3072])
        m["adab"] = np.ascontiguousarray(np.stack([pm(inp["ada_b"][l, 3072 * s_:3072 * s_ + 3072]) for l in range(L)], 1))
        nr = np.zeros((128, 2, L, 16), f)
        for l in range(L):
            nr[:, 0, l] = pm(inp["norm_mix"][l]); nr[:, 1, l] = pm(inp["norm_ffn"][l])
        m["nrm"] = nr; m["fin"] = pm(inp["final_norm"])
        cols = []
        for o in ORK:
            cols += list(range(o + g * 256, o + g * 256 + 256))
        cols += list(range(3072, 3072 + 448))
        for gi in range(4):
            cols += list(range(3520 + gi * 256 + s_ * 64, 3520 + gi * 256 + s_ * 64 + 64))
        cols = np.array(cols)
        m["win"] = np.ascontiguousarray(inp["w_in"][:L][:, :, cols].reshape(L, 16, 128, 1472).transpose(0, 2, 1, 3))
        cw = np.zeros((128, L, 6, 3), f)
        for l in range(L):
            for wh in range(3):
                for q in range(2):
                    cw[:, l, wh * 2 + q, :] = inp["conv_rkv"][l][:, wh * 1024 + g * 256 + q * 128: wh * 1024 + g * 256 + q * 128 + 128].T
        m["convw"] = cw
        hsl = lambda a, l, q: a[l][g * 256 + q * 128: g * 256 + q * 128 + 128]
        db = np.zeros((128, L, 2, 2), f); ib = np.zeros((128, L, 2, 2), f); hvv = np.zeros((128, L, 5, 2), f)
        for l in range(L):
            for q in range(2):
                for d in range(2):
                    db[:, l, d, q] = inp["decay_bias"][l, d, g * 256 + q * 128: g * 256 + q * 128 + 128]
                    ib[:, l, d, q] = inp["iclr_bias"][l, d, g * 256 + q * 128: g * 256 + q * 128 + 128]
                for wi, nm in enumerate(["k_k", "k_a", "r_k", "gn_w", "gn_b"]):
                    hvv[:, l, wi, q] = hsl(inp[nm], l, q)
        m["dbias"] = db; m["ibias"] = ib; m["hv"] = hvv
        m["dup"] = np.ascontiguousarray(np.transpose(inp["decay_up"][:L, :, :, g * 256:g * 256 + 256], (2, 0, 1, 3)))
        m["iup"] = np.ascontiguousarray(np.transpose(inp["iclr_up"][:L, :, :, g * 256:g * 256 + 256], (2, 0, 1, 3)))
        gu = inp["gate_up"][:L, :, g * 256:g * 256 + 256].reshape(L, 2, 128, 256)
        m["gup"] = np.ascontiguousarray(np.transpose(gu, (2, 0, 1, 3)))
        pw = np.zeros((128, L, 2, 256), f)
        for l in range(L):
            for pc in range(2):
                for half in range(2):
                    pw[half * 64:half * 64 + 64, l, pc] = inp["pool_w"][l, pc * 2 + half, s_ * 64:s_ * 64 + 64, :]
        m["poolw"] = pw
        m["pscale"] = np.ascontiguousarray(np.stack([pm(inp["pool_scale"][l]) for l in range(L)], 1))
        m["wout"] = np.ascontiguousarray(np.concatenate([inp["w_out"][:L, g * 256:g * 256 + 256], inp["w_out"][:L, 1024:2048]], 1).reshape(L, 10, 128, D).transpose(0, 2, 1, 3))
        m["fg"] = FG; m["fu"] = FU; m["fd"] = FD
        m["rc2"] = rc2; m["rc1"] = rc1
        maps.append({kk_: np.ascontiguousarray(v, dtype=f) for kk_, v in m.items()})
    return maps


def kernel(**inputs):
    inp = {k_: np.asarray(v) for k_, v in inputs.items()}
    k = build(4)
    maps = _host_inputs(inp, 4)
    res = run_bass_kernel_spmd(k.nc, maps, core_ids=list(range(8)))
    out = np.zeros((2, 16384, 2048), np.float32)
    for c in range(8):
        b, s_ = c // 4, c % 4
        o = res.results[c]["out"]
        out[b, 4096 * s_:4096 * s_ + 4096] = o[:, 64:].T
    return out
```

```python
import numpy as np
from contextlib import ExitStack
import concourse.bass as bass
import concourse.mybir as mybir
from concourse.bass_utils import run_bass_kernel_spmd

F32 = mybir.dt.float32
F32R = mybir.dt.float32r
BF16 = mybir.dt.bfloat16
AF = mybir.ActivationFunctionType
ALU = mybir.AluOpType

D = 2048; KC = 16; NTL = 4160; TT = 416; NTI = 10; CW = 104; NCH = 40
UC = 17920; CTX0 = 256; LAT0 = 1024
DFF = 5632; FC = 44
C0 = float(np.exp(-0.5))
GROUPS = [[0, 1, 2, 3], [4, 5, 6, 7]]
UCH = [(0, 128), (128, 128), (256, 128), (384, 128), (512, 128), (640, 128),
       (768, 96), (864, 96), (960, 128), (1088, 128), (1216, 128), (1344, 128)]


class Sem:
    def __init__(s, h, dma):
        s.h = h; s.dma = dma; s.total = 0


class T:
    def __init__(s, t, name):
        s.t = t; s.name = name; s.w = []; s.r = []; s.dsem = None

    def __getitem__(s, k):
        return s.t[k]


class Eng:
    def __init__(s, h, sem, name):
        s.h = h; s.sem = sem; s.cnt = 0; s.seen = {}; s.name = name


class B:
    def __init__(s):
        nc = bass.Bass("TRN2", target_bir_lowering=False)
        s.nc = nc
        s.ges = ExitStack()
        s.nsem = 0
        s.dsems = []
        s.pe = Eng(nc.tensor, s.newsem(False), "pe")
        s.act = Eng(nc.scalar, s.newsem(False), "act")
        s.dve = Eng(nc.vector, s.newsem(False), "dve")
        s.pool = Eng(nc.gpsimd, s.newsem(False), "pool")
        s.sp = Eng(nc.sync, s.newsem(False), "sp")
        s.engs = [s.pe, s.act, s.dve, s.pool, s.sp]
        s.banks = [T(s.ges.enter_context(nc.psum_tensor("bank%d" % i, [128, 512], F32)), "bank%d" % i)
                   for i in range(8)]
        s.bi = 0
        s.flip = 0

    def newsem(s, dma):
        s.nsem += 1
        sm = Sem(s.ges.enter_context(s.nc.semaphore("sm%d" % s.nsem)), dma)
        if dma:
            s.dsems.append(sm)
        return sm

    def bank(s):
        b = s.banks[s.bi]; s.bi = (s.bi + 1) % 8
        return b

    def sb(s, es, name, shape, dt=F32):
        s.nsb = getattr(s, "nsb", 0) + 1
        return T(es.enter_context(s.nc.sbuf_tensor("sb%d_%s" % (s.nsb, name), list(shape), dt)), name)

    def dram(s, name, shape, kind="Internal", dt=F32):
        return T(s.nc.dram_tensor(name, list(shape), dt, kind=kind).ap(), name)

    def _wait(s, eng, toks):
        need = {}
        for (sm, v) in toks:
            if sm.dma:
                v = sm.total
            if sm is eng.sem and eng is s.pe:
                continue
            if need.get(sm, 0) < v:
                need[sm] = v
        for sm, v in need.items():
            if eng.seen.get(sm, 0) >= v:
                continue
            eng.h.wait_ge(sm.h, v)
            eng.seen[sm] = v

    def _deps(s, reads, writes):
        toks = []
        for b in reads:
            toks += b.w
        for b in writes:
            toks += b.w; toks += b.r
        return toks

    def _upd(s, tok, reads, writes):
        for b in writes:
            b.w = [tok]; b.r = []
        for b in reads:
            if b not in writes:
                b.r = [t for t in b.r if t[0] is not tok[0]] + [tok]

    def op(s, eng, fn, reads=(), writes=()):
        s._wait(eng, s._deps(reads, writes))
        ins = fn(eng.h)
        eng.cnt += 1
        ins.then_inc(eng.sem.h, 1)
        s._upd((eng.sem, eng.cnt), reads, writes)
        return ins

    def dma(s, eng, out, in_, reads, writes, owner=None, fn=None, inc=16):
        s._wait(eng, s._deps(reads, writes))
        ow = owner or writes[0]
        if ow.dsem is None:
            cache = s.__dict__.setdefault('semcache', {})
            if ow.name not in cache:
                cache[ow.name] = s.newsem(True)
            ow.dsem = cache[ow.name]
        sm = ow.dsem
        sm.total += inc
        if fn is None:
            ins = eng.h.dma_start(out=out, in_=in_)
        else:
            ins = fn(eng.h)
        ins.then_inc(sm.h, inc)
        s._upd((sm, sm.total), reads, writes)

    def load(s, out, in_, src, dst, owner=None, r=False, cast=False):
        if cast:
            s.dma(s.pool, out, in_, [src], [dst], owner)
            return
        if r:
            s.dma(s.pool, out.bitcast(F32R), in_, [src], [dst], owner)
            return
        s.dma(s.sp, out, in_, [src], [dst], owner)

    def store(s, out, in_, src, dst, owner=None):
        s.dma(s.pool, out, in_, [src], [dst], owner)

    def barrier(s):
        for e in s.engs:
            toks = [(o.sem, o.cnt) for o in s.engs if o is not e and o.cnt > 0]
            toks += [(sm, sm.total) for sm in s.dsems if sm.total > 0]
            s._wait(e, toks)

    def mm(s, out, lhsT, rhs, start, stop, reads, writes, r=False):
        if r:
            lhsT = lhsT.bitcast(mybir.dt.float32r); rhs = rhs.bitcast(mybir.dt.float32r)
        return s.op(s.pe, lambda h: h.matmul(out, lhsT=lhsT, rhs=rhs, start=start, stop=stop), reads, writes)

    def V(s, fn, reads, writes):
        return s.op(s.dve, fn, reads, writes)

    def A(s, fn, reads, writes):
        return s.op(s.act, fn, reads, writes)

    def G(s, fn, reads, writes):
        return s.op(s.pool, fn, reads, writes)

    def cp(s, out, in_, reads, writes, eng=None):
        if eng is None:
            s.flip ^= 1
            eng = s.act if s.flip else s.dve
        if eng is s.act:
            return s.A(lambda h: h.activation(out=out, in_=in_, func=AF.Copy), reads, writes)
        return s.op(eng, lambda h: h.tensor_copy(out=out, in_=in_), reads, writes)


def rsq(k, out, in_, scale, bias, reads, wT):
    k.A(lambda h: h.activation(out=out, in_=in_, func=AF.Sqrt, scale=scale, bias=bias), reads, [wT])
    k.V(lambda h: h.reciprocal(out=out, in_=out), [wT], [wT])


def bc(ap, shape):
    return ap.broadcast_to(list(shape))


def build(depth=4, dbg=None):
    k = B()
    nc = k.nc
    ges = k.ges
    L = depth

    def din(name, shape, dt=F32):
        return k.dram(name, shape, "ExternalInput", dt)

    xT_in = din("xT", [D, NTL]); cT_in = din("cT", [128, 16, 2])
    adaw_in = din("adaw", [L, D, 3072]); adab_in = din("adab", [128, L, 24])
    nrm_in = din("nrm", [128, 2, L, 16]); fin_in = din("fin", [128, 16])
    win_in = din("win", [L, 128, 16, 1472]); convw_in = din("convw", [128, L, 6, 3])
    dbias_in = din("dbias", [128, L, 2, 2]); ibias_in = din("ibias", [128, L, 2, 2])
    dup_in = din("dup", [96, L, 2, 256]); iup_in = din("iup", [96, L, 2, 256])
    gup_in = din("gup", [128, L, 2, 256]); hv_in = din("hv", [128, L, 5, 2])
    poolw_in = din("poolw", [128, L, 2, 256]); pscale_in = din("pscale", [128, L, 8])
    wout_in = din("wout", [L, 128, 10, D])
    fg_in = din("fg", [L, FC, 128, 16, 128]); fu_in = din("fu", [L, FC, 128, 16, 128]); fd_in = din("fd", [L, 16, 128, FC, 128])
    rc2_in = din("rc2", [128, 5, 2, 4, 64]); rc1_in = din("rc1", [128, 2, 256])
    out_d = k.dram("out", [D, NTL], "ExternalOutput")

    XT = k.dram("XT", [D, NTL]); HT = k.dram("HT", [NCH, D, CW]); HG = k.dram("HG", [NCH, 4 * D, CW])
    UT = k.dram("UT", [1472, UC]); YY = k.dram("YY", [4, 256, UC])
    PT = k.dram("PT", [NCH, 4 * D, CW]); DT = k.dram("DT", [NCH, D, CW])
    MODP = k.dram("MODP", [128, 192]); MODG = k.dram("MODG", [512, 192])
    FGB = k.dram("FGB", [FC, 128, 16, 128], dt=BF16); FUB = k.dram("FUB", [FC, 128, 16, 128], dt=BF16); FDB = k.dram("FDB", [16, 128, FC, 128], dt=BF16)
    dbg_t = {}
    if dbg:
        for nm, shp in dbg.items():
            dbg_t[nm] = k.dram("dbg_" + nm, shp, "ExternalOutput")

    cst = T(None, "cstgrp")

    def cload(name, src, shape):
        t = k.sb(ges, name, shape)
        k.load(t[:], src[:], src, t, owner=cst)
        return t

    convw = cload("convw", convw_in, [128, L, 6, 3]); dbias = cload("dbias", dbias_in, [128, L, 2, 2])
    ibias = cload("ibias", ibias_in, [128, L, 2, 2]); hv = cload("hv", hv_in, [128, L, 5, 2])
    pscale = cload("pscale", pscale_in, [128, L, 8]); nrm = cload("nrm", nrm_in, [128, 2, L, 16])
    fin = cload("fin", fin_in, [128, 16]); rc2 = cload("rc2", rc2_in, [128, 5, 2, 4, 64])
    rc1 = cload("rc1", rc1_in, [128, 2, 256]); adab = cload("adab", adab_in, [128, L, 24])
    cT = cload("cT", cT_in, [128, 16, 2])

    ONES = k.sb(ges, "ONES", [128, 128]); IDENT = k.sb(ges, "IDENT", [128, 128]); BONES = k.sb(ges, "BONES", [128, 128])
    MF2 = k.sb(ges, "MF2", [128, 1, 256]); MB2 = k.sb(ges, "MB2", [128, 1, 256])
    MNF = k.sb(ges, "MNF", [128, 1, 128]); MNB = k.sb(ges, "MNB", [128, 1, 128])
    RST = k.sb(ges, "RST", [128, 256]); ZERO = k.sb(ges, "ZERO", [128, 512])
    DER = k.sb(ges, "DER", [128, L, 6, 16, 2])

    k.G(lambda h: h.memset(ONES[:], 1.0), [], [ONES])
    k.G(lambda h: h.memset(ZERO[:], 0.0), [], [ZERO])
    k.G(lambda h: h.memset(IDENT[:], 1.0), [], [IDENT])
    k.G(lambda h: h.affine_select(out=IDENT[:], in_=IDENT[:], pattern=[[1, 128]], base=0, channel_multiplier=-1,
                                  compare_op=ALU.is_equal, fill=0.0), [IDENT], [IDENT])
    k.G(lambda h: h.memset(BONES[:], 0.0), [], [BONES])
    k.G(lambda h: h.memset(BONES[0:64, 0:64], 1.0), [BONES], [BONES])
    k.G(lambda h: h.memset(BONES[64:128, 64:128], 1.0), [BONES], [BONES])

    def mask(t, ap, nrep, step, cm, cmp):
        k.G(lambda h: h.memset(ap, 1.0), [t], [t])
        k.G(lambda h: h.affine_select(out=ap, in_=ap, pattern=[[0, nrep], [step, 128]], base=0, channel_multiplier=cm,
                                      compare_op=cmp, fill=0.0), [t], [t])
    mask(MF2, MF2[:, :, 0:128], 1, 1, -1, ALU.is_gt)
    mask(MF2, MF2[:, :, 128:256], 1, 1, -1, ALU.is_ge)
    mask(MB2, MB2[:, :, 0:128], 1, -1, 1, ALU.is_gt)
    mask(MB2, MB2[:, :, 128:256], 1, -1, 1, ALU.is_ge)
    mask(MNF, MNF[:], 1, -1, 1, ALU.is_gt)
    mask(MNB, MNB[:], 1, 1, -1, ALU.is_gt)
    k.G(lambda h: h.memset(RST[:], 1.0), [], [RST])
    rst4 = RST[:].rearrange("p (c t) -> p c t", t=64)
    k.G(lambda h: h.memset(rst4[:, :, 0:1], 0.0), [RST], [RST])


    def coll(kind, op, src, dst, sap=None, dap=None):
        sap = src[:] if sap is None else sap
        dap = dst[:] if dap is None else dap
        k.dma(k.pool, None, None, [src], [dst], fn=lambda h: h.collective_compute(
            kind, op, replica_groups=GROUPS, ins=[sap], outs=[dap]), inc=1)

    def cdma(load, DR, row0, nrows, sb_fn, T_sb, lt0, n, r=False):
        lt = lt0
        while lt < lt0 + n:
            j = lt // CW; a = lt % CW; ln = min(CW - a, lt0 + n - lt)
            dap = DR[j, row0:row0 + nrows, a:a + ln].rearrange("(kc p) t -> p kc t", p=128)
            sap = sb_fn(lt - lt0, ln)
            if load:
                k.load(sap, dap, DR, T_sb, r=r)
            else:
                k.store(dap, sap, T_sb, DR)
            lt += ln

    with ExitStack() as es:
        sc = k.sb(es, "sc", [128, 16, 2])
        k.A(lambda h: h.activation(out=sc[:], in_=cT[:], func=AF.Silu), [cT], [sc])
        aw = [k.sb(es, "aw%d" % i, [128, 16, 512]) for i in range(2)]
        modp = k.sb(es, "modp", [128, L, 24, 2])
        bk = k.bank()
        n = 0
        for l in range(L):
            for i4 in range(6):
                t = aw[n % 2]; n += 1
                k.load(t[:], adaw_in[l, :, i4 * 512:(i4 + 1) * 512].rearrange("(kc p) n -> p kc n", p=128), adaw_in, t)
                for jj in range(4):
                    i = i4 * 4 + jj
                    col = (l * 24 + i) * 2
                    for kc in range(16):
                        k.mm(bk[:, col:col + 2], t[:, kc, jj * 128:(jj + 1) * 128], sc[:, kc, :], kc == 0, kc == 15, [t, sc], [bk])
        bv = bk[:, 0:L * 48].rearrange("p (l i m) -> p l i m", l=L, m=2)
        k.V(lambda h: h.tensor_tensor(out=modp[:], in0=bv, in1=bc(adab[:].unsqueeze(3), [128, L, 24, 2]), op=ALU.add), [bk, adab], [modp])
        k.store(MODP[:, 0:L * 48], modp[:].rearrange("p l i m -> p (l i m)"), modp, MODP)
        coll("AllGather", ALU.bypass, MODP, MODG)
        MOD = k.sb(es, "MOD", [128, 4, 192])
        k.load(MOD[:], MODG[:].rearrange("(r p) f -> p r f", p=128), MODG, MOD)
        MODv = MOD[:, :, 0:L * 48].rearrange("p r (l i m) -> p r l i m", l=L, m=2)
        for w in range(6):
            j0 = 16 * w
            while j0 < 16 * w + 16:
                r = j0 // 24; j1 = min(16 * w + 16, (r + 1) * 24)
                k.V(lambda h, r=r, j0=j0, j1=j1, w=w: h.tensor_copy(
                    out=DER[:, :, w, j0 - 16 * w:j1 - 16 * w, :], in_=MODv[:, r, :, j0 - 24 * r:j1 - 24 * r, :]), [MOD], [DER])
                j0 = j1
        for (w, wn) in ((1, 0), (4, 1)):
            k.V(lambda h, w=w: h.tensor_scalar(out=DER[:, :, w], in0=DER[:, :, w], scalar1=1.0, scalar2=None, op0=ALU.add), [DER], [DER])
            k.V(lambda h, w=w, wn=wn: h.tensor_tensor(out=DER[:, :, w], in0=DER[:, :, w],
                                                     in1=bc(nrm[:, wn].unsqueeze(3), [128, L, 16, 2]), op=ALU.mult), [DER, nrm], [DER])
    k.barrier()

    def modulate(tile, T_, lt0, ncols, l, wg, wsh, reads, r=False, out_tile=None, outT=None):
        segs = []
        if lt0 < 64:
            segs.append((0, 64 - lt0, 1))
            segs.append((64 - lt0, ncols, 0))
        else:
            segs.append((0, ncols, 0))
        for (a, b_, m) in segs:
            n_ = b_ - a
            if wg is not None:
                k.V(lambda h, a=a, b_=b_, m=m, n_=n_: h.tensor_tensor(out=(tile[:, :, a:b_].bitcast(F32R) if r else tile[:, :, a:b_]), in0=tile[:, :, a:b_],
                    in1=bc(DER[:, l, wg, :, m:m + 1], [128, 16, n_]), op=ALU.mult), [T_, DER] + reads, [T_])
            if wsh is not None:
                k.V(lambda h, a=a, b_=b_, m=m, n_=n_: h.tensor_tensor(out=(out_tile[:, :, a:b_] if out_tile is not None else (tile[:, :, a:b_].bitcast(F32R) if r else tile[:, :, a:b_])), in0=tile[:, :, a:b_],
                    in1=bc(DER[:, l, wsh, :, m:m + 1], [128, 16, n_]), op=ALU.add), [T_, DER] + reads, [T_ if outT is None else outT])

    def rms_tile(es_t, xt, sq, rs, eps=1e-6):
        k.A(lambda h: h.activation(out=sq[:], in_=xt[:], func=AF.Square), [xt], [sq])
        bk = k.bank()
        for kc in range(16):
            k.mm(bk[:, 0:TT], ONES[:], sq[:, kc, :], kc == 0, kc == 15, [ONES, sq], [bk])
        rsq(k, rs[:], bk[:, 0:TT], 1.0 / D, eps, [bk], rs)

    xrows = lambda Tn, c0, c1: Tn[:, c0:c1].rearrange("(kc p) t -> p kc t", p=128)

    def ucol(r, lt):
        return CTX0 + 64 * r + lt if lt < 64 else LAT0 + 4096 * r + (lt - 64)

    for (c0, c1) in ((0, 256), (512, 1024), (17408, 17920)):
        for r0 in range(0, 1472, 128):
            r1 = min(1472, r0 + 128)
            for cc in range(c0, c1, 512):
                ce = min(c1, cc + 512)
                k.store(UT[r0:r1, cc:ce], ZERO[0:r1 - r0, 0:ce - cc], ZERO, UT)

    XS = xT_in
    for l in range(L):
        with ExitStack() as es:
            xts = [k.sb(es, "p1x%d" % i, [128, 16, TT]) for i in range(2)]
            sq = k.sb(es, "p1sq", [128, 16, TT]); rs = k.sb(es, "p1rs", [128, TT])
            for j in range(FC):
                k.dma(k.pool, FGB[j], fg_in[l, j], [fg_in], [FGB])
                k.dma(k.pool, FUB[j], fu_in[l, j], [fu_in], [FUB])
            for n_ in range(16):
                k.dma(k.pool, FDB[n_], fd_in[l, n_], [fd_in], [FDB])
            for i in range(NTI):
                xt = xts[i % 2]
                k.load(xt[:], xrows(XS, i * TT, (i + 1) * TT), XS, xt)
                rms_tile(es, xt, sq, rs)
                k.V(lambda h: h.tensor_tensor(out=xt[:], in0=xt[:], in1=bc(rs[:].unsqueeze(1), [128, 16, TT]), op=ALU.mult), [xt, rs], [xt])
                modulate(xt, xt, i * TT, TT, l, 1, 0, [])
                cdma(False, HT, 0, D, lambda off, ln, xt=xt: xt[:, :, off:off + ln], xt, i * TT, TT)
            for j in range(NCH):
                coll("AllGather", ALU.bypass, HT, HG, HT[j], HG[j])
        k.barrier()
        with ExitStack() as es:
            wsl = k.sb(es, "wsl", [128, 16, 1472])
            k.load(wsl[:], win_in[l], win_in, wsl, r=True)
            hgs = [k.sb(es, "hg%d" % i, [128, 16, TT]) for i in range(2)]
            ust = k.sb(es, "ust", [128, 12, TT])
            n = 0
            for r in range(4):
                for i in range(NTI):
                    hg = hgs[n % 2]; n += 1
                    cdma(True, HG, r * D, D, lambda off, ln, hg=hg: hg[:, :, off:off + ln], hg, i * TT, TT, r=True)
                    for j, (c0, M) in enumerate(UCH):
                        bk = k.bank()
                        for kc in range(16):
                            k.mm(bk[0:M, 0:TT], wsl[:, kc, c0:c0 + M], hg[:, kc, :], kc == 0, kc == 15, [wsl, hg], [bk], r=True)
                        fn = AF.Tanh if j == 6 else (AF.Sigmoid if j in (8, 9) else None)
                        if fn is not None:
                            k.A(lambda h, fn=fn, M=M, j=j, bk=bk: h.activation(out=ust[0:M, j, :], in_=bk[0:M, 0:TT], func=fn), [bk], [ust])
                        else:
                            k.cp(ust[0:M, j, :], bk[0:M, 0:TT], [bk], [ust])
                    lt0 = i * TT
                    segs = [(0, 64, ucol(r, 0)), (64, TT, ucol(r, 64))] if i == 0 else [(0, TT, ucol(r, lt0))]
                    for (a, b_, uc0) in segs:
                        n_ = b_ - a
                        k.store(UT[0:768, uc0:uc0 + n_].rearrange("(j p) t -> p j t", p=128), ust[:, 0:6, a:b_], ust, UT)
                        k.store(UT[768:864, uc0:uc0 + n_], ust[0:96, 6, a:b_], ust, UT)
                        k.store(UT[864:960, uc0:uc0 + n_], ust[0:96, 7, a:b_], ust, UT)
                        k.store(UT[960:1472, uc0:uc0 + n_].rearrange("(j p) t -> p j t", p=128), ust[:, 8:12, a:b_], ust, UT)
        k.barrier()
        if dbg and "UT" in dbg and l == 0:
            k.dma(k.pool, dbg_t["UT"][:], UT[:], [UT], [dbg_t["UT"]])
            k.barrier()

        with ExitStack() as es:
            dup = k.sb(es, "dup", [96, 2, 256]); iup = k.sb(es, "iup", [96, 2, 256])
            k.load(dup[:], dup_in[:, l], dup_in, dup); k.load(iup[:], iup_in[:, l], iup_in, iup)
            u6s = [k.sb(es, "u6_%d" % i, [128, 6, 258]) for i in range(2)]
            los = [k.sb(es, "lo_%d" % i, [96, 2, 256]) for i in range(2)]
            sbt = lambda nm, shp=(128, 2, 256), dt=F32: k.sb(es, nm, list(shp), dt)
            IDB = k.sb(es, "IDB", [128, 128], BF16)
            k.V(lambda h: h.tensor_copy(out=IDB[:], in_=IDENT[:]), [IDENT], [IDB])
            rkv = sbt("rkv", (128, 6, 256)); tmp6 = sbt("tmp6", (128, 6, 256))
            kk = sbt("kk"); kksq = sbt("kksq"); kkn = sbt("kkn"); rn = sbt("rn")
            sg = sbt("sg"); cs = sbt("cs"); ex = sbt("ex"); sbb = sbt("sbb"); sbx = sbt("sbx")
            eg = sbt("eg"); eng = sbt("eng"); ega = sbt("ega"); av = sbt("av"); tB = sbt("tB"); t1 = sbt("t1"); kd = sbt("kd")
            bon = sbt("bon"); yst = sbt("yst")
            AR = [sbt("AR%d" % q, (128, 4, 256), BF16) for q in range(2)]
            Bt = [sbt("Bt%d" % q, (128, 4, 128), BF16) for q in range(2)]; Kt = [sbt("Kt%d" % q, (128, 4, 128), BF16) for q in range(2)]
            Bht = [sbt("Bht%d" % q, (128, 4, 128), BF16) for q in range(2)]; Kht = [sbt("Kht%d" % q, (128, 4, 128), BF16) for q in range(2)]
            Vt = [sbt("Vt%d" % q, (128, 4, 128), BF16) for q in range(2)]
            for tl in AR + Bt + Kt + Bht + Kht + Vt:
                k.G(lambda h, tl=tl: h.memset(tl[:], 0.0), [], [tl])
            MAs = [sbt("MA%d" % q, (128, 4, 256), BF16) for q in range(2)]; MBs = [sbt("MB%d" % q, (128, 4, 256), BF16) for q in range(2)]
            Pqs = [[sbt("Pq%d_%d" % (q, i), (128, 4, 128), BF16) for i in range(2)] for q in range(2)]; PTqs = [[sbt("PTq%d_%d" % (q, i), (128, 4, 128), BF16) for i in range(2)] for q in range(2)]
            Xqs = [[sbt("Xq%d_%d" % (q, i), (128, 4, 256), BF16) for i in range(2)] for q in range(2)]
            VTks = [sbt("VTk%d" % q, (128, 4, 128), BF16) for q in range(2)]; BhTs = [sbt("BhT%d" % q, (128, 4, 128), BF16) for q in range(2)]; KhTs = [sbt("KhT%d" % q, (128, 4, 128), BF16) for q in range(2)]
            RTps = [sbt("RTp%d" % q, (128, 4, 128), BF16) for q in range(2)]; YLs = [sbt("YL%d" % q, (128, 4, 128)) for q in range(2)]; MCTs = [sbt("MCT%d" % q, (128, 4, 128), BF16) for q in range(2)]; NCts = [sbt("NCt%d" % q, (128, 4, 128)) for q in range(2)]
            Z = {}
            for q in range(2):
                for d in range(2):
                    Z[(q, d)] = [sbt("Z%d%d%d" % (q, d, i), (128, 128), BF16) for i in range(2)]
                    z0 = Z[(q, d)][0]
                    k.G(lambda h, z=z0: h.memset(z[:], 0.0), [], [z0])
            zi = {(q, d): 0 for q in range(2) for d in range(2)}
            npass = [0]

            def bdw(dst, dstslice, fn_in, reads, eng=None):
                pass

            def scan_pass(c0, d):
                n = npass[0]; npass[0] += 1
                u6 = u6s[n % 2]; lo = los[n % 2]
                k.load(u6[:], UT[0:768, c0 - 1:c0 + 257].rearrange("(j p) t -> p j t", p=128), UT, u6)
                k.load(lo[:, 0, :], UT[768:864, c0:c0 + 256], UT, lo)
                k.load(lo[:, 1, :], UT[864:960, c0:c0 + 256], UT, lo)
                cw = lambda tap: bc(convw[:, l, :, tap:tap + 1], [128, 6, 256])
                k.V(lambda h: h.tensor_tensor(out=rkv[:], in0=u6[:, :, 0:256], in1=cw(0), op=ALU.mult), [u6, convw], [rkv])
                k.G(lambda h: h.tensor_tensor(out=tmp6[:], in0=u6[:, :, 1:257], in1=cw(1), op=ALU.mult), [u6, convw], [tmp6])
                k.V(lambda h: h.tensor_tensor(out=rkv[:], in0=rkv[:], in1=tmp6[:], op=ALU.add), [rkv, tmp6], [rkv])
                k.G(lambda h: h.tensor_tensor(out=tmp6[:], in0=u6[:, :, 2:258], in1=cw(2), op=ALU.mult), [u6, convw], [tmp6])
                k.V(lambda h: h.tensor_tensor(out=rkv[:], in0=rkv[:], in1=tmp6[:], op=ALU.add), [rkv, tmp6], [rkv])
                R_ = rkv[:, 0:2, :]; K_ = rkv[:, 2:4, :]; V_ = rkv[:, 4:6, :]
                hvb = lambda w: bc(hv[:, l, w, :].unsqueeze(2), [128, 2, 256])
                k.V(lambda h: h.tensor_tensor(out=kk[:], in0=K_, in1=hvb(0), op=ALU.mult), [rkv, hv], [kk])
                k.G(lambda h: h.tensor_tensor(out=kksq[:], in0=kk[:], in1=kk[:], op=ALU.mult), [kk], [kksq])
                bk = k.bank()
                for q in range(2):
                    k.mm(bk[:, q * 256:(q + 1) * 256], BONES[:], kksq[:, q, :], True, True, [BONES, kksq], [bk])
                k.V(lambda h: h.tensor_scalar(out=rn[:].rearrange("p q t -> p (q t)"), in0=bk[:, 0:512], scalar1=1e-24, scalar2=None,
                                              op0=ALU.max), [bk], [rn])
                rsq(k, rn[:], rn[:], 1.0, 0.0, [rn], rn)
                k.V(lambda h: h.tensor_tensor(out=kkn[:], in0=kk[:], in1=rn[:], op=ALU.mult), [kk, rn], [kkn])
                k.V(lambda h: h.tensor_scalar(out=kksq[:], in0=kkn[:], scalar1=-1.0, scalar2=None, op0=ALU.mult), [kkn, kksq], [kksq])
                bk = k.bank(); bk2 = k.bank()
                for q in range(2):
                    k.mm(bk[:, q * 256:(q + 1) * 256], dup[:, d, q * 128:(q + 1) * 128], lo[:, 0, :], True, True, [dup, lo], [bk])
                    k.mm(bk2[:, q * 256:(q + 1) * 256], iup[:, d, q * 128:(q + 1) * 128], lo[:, 1, :], True, True, [iup, lo], [bk2])
                for q in range(2):
                    k.A(lambda h, q=q: h.activation(out=sg[:, q, :], in_=bk[:, q * 256:(q + 1) * 256], func=AF.Sigmoid,
                                                    bias=dbias[:, l, d, q:q + 1], scale=1.0), [bk, dbias], [sg])
                    k.A(lambda h, q=q: h.activation(out=av[:, q, :], in_=bk2[:, q * 256:(q + 1) * 256], func=AF.Sigmoid,
                                                    bias=ibias[:, l, d, q:q + 1], scale=1.0), [bk2, ibias], [av])
                for q in range(2):
                    k.V(lambda h, q=q: h.tensor_tensor_scan(out=cs[:, q, :], data0=RST[:], data1=sg[:, q, :], initial=0.0,
                                                            op0=ALU.mult, op1=ALU.add), [RST, sg], [cs])
                k.V(lambda h: h.tensor_tensor(out=ex[:], in0=cs[:], in1=sg[:], op=ALU.subtract), [cs, sg], [ex])
                if d == 0:
                    Sin, Sex, Tin, Tex = cs, ex, cs, ex
                    gcol = 63
                else:
                    cs5 = cs[:].rearrange("p q (c t) -> p q c t", t=64)
                    k.V(lambda h: h.tensor_tensor(out=sbb[:].rearrange("p q (c t) -> p q c t", t=64),
                                                  in0=bc(cs5[:, :, :, 63:64], [128, 2, 4, 64]),
                                                  in1=ex[:].rearrange("p q (c t) -> p q c t", t=64), op=ALU.subtract), [cs, ex], [sbb])
                    k.V(lambda h: h.tensor_tensor(out=sbx[:], in0=sbb[:], in1=sg[:], op=ALU.subtract), [sbb, sg], [sbx])
                    Sin, Sex, Tin, Tex = sbb[:], sbx[:], sbb, sbx
                    gcol = 0
                SinA = Sin[:] if d == 0 else Sin
                SexA = Sex[:] if d == 0 else Sex
                k.A(lambda h: h.activation(out=eg[:], in_=SinA, func=AF.Exp, scale=-C0), [Tin], [eg])
                k.A(lambda h: h.activation(out=eng[:], in_=SinA, func=AF.Exp, scale=C0), [Tin], [eng])
                k.A(lambda h: h.activation(out=ega[:], in_=SexA, func=AF.Exp, scale=-C0), [Tex], [ega])
                k.V(lambda h: h.tensor_tensor(out=t1[:], in0=av[:], in1=hvb(1), op=ALU.mult), [av, hv], [t1])
                k.V(lambda h: h.tensor_tensor(out=t1[:], in0=t1[:], in1=hvb(1), op=ALU.subtract), [t1, hv], [t1])
                k.V(lambda h: h.tensor_scalar(out=t1[:], in0=t1[:], scalar1=1.0, scalar2=None, op0=ALU.add), [t1], [t1])
                k.V(lambda h: h.tensor_tensor(out=kd[:], in0=t1[:], in1=K_, op=ALU.mult), [t1, rkv], [kd])
                k.G(lambda h: h.tensor_tensor(out=tB[:], in0=av[:], in1=eng[:], op=ALU.mult), [av, eng], [tB])
                k.G(lambda h: h.tensor_tensor(out=t1[:], in0=kd[:], in1=hvb(2), op=ALU.mult), [kd, hv, t1], [t1])
                k.G(lambda h: h.tensor_tensor(out=t1[:], in0=t1[:], in1=R_, op=ALU.mult), [t1, rkv], [t1])
                bk = k.bank()
                for q in range(2):
                    k.mm(bk[:, q * 256:(q + 1) * 256], BONES[:], t1[:, q, :], True, True, [BONES, t1], [bk])
                k.V(lambda h: h.tensor_tensor(out=bon[:].rearrange("p q t -> p (q t)"), in0=bk[:, 0:512],
                                              in1=rkv[:, 4:6, :].rearrange("p q t -> p (q t)"), op=ALU.mult), [bk, rkv], [bon])
                k.store(YY[2 + d, :, c0:c0 + 256].rearrange("(q p) t -> p q t", p=128), bon[:], bon, YY)
                egv = eg[:].rearrange("p q (c t) -> p q c t", t=64)
                ei = 0
                for q in range(2):
                    for hh in range(2):
                        ps = slice(hh * 64, hh * 64 + 64)
                        v4 = lambda tl, lo_=0: tl[ps, :, lo_ + hh * 64:lo_ + hh * 64 + 64]
                        s4 = lambda tl: tl[ps, q, :].rearrange("p (c t) -> p c t", t=64)
                        r4 = lambda j: rkv[ps, j * 2 + q, :].rearrange("p (c t) -> p c t", t=64)
                        gam = bc(egv[ps, q, :, gcol:gcol + 1], [64, 4, 64])
                        E = [k.G, k.G]
                        def em(fn, reads, writes):
                            nonlocal ei
                            ei += 1
                            E[ei % 2](fn, reads, writes)
                        em(lambda h: h.tensor_tensor(out=v4(AR[q]), in0=s4(kksq), in1=s4(ega), op=ALU.mult), [kksq, ega], [AR[q]])
                        em(lambda h: h.tensor_tensor(out=v4(AR[q], 128), in0=r4(0), in1=s4(eg), op=ALU.mult), [rkv, eg], [AR[q]])
                        em(lambda h: h.tensor_tensor(out=v4(Bt[q]), in0=s4(kkn), in1=s4(tB), op=ALU.mult), [kkn, tB], [Bt[q]])
                        em(lambda h: h.tensor_tensor(out=v4(Kt[q]), in0=s4(kd), in1=s4(eng), op=ALU.mult), [kd, eng], [Kt[q]])
                        em(lambda h: h.tensor_tensor(out=v4(Bht[q]), in0=v4(Bt[q]), in1=gam, op=ALU.mult), [Bt[q], eg], [Bht[q]])
                        em(lambda h: h.tensor_tensor(out=v4(Kht[q]), in0=v4(Kt[q]), in1=gam, op=ALU.mult), [Kt[q], eg], [Kht[q]])
                        em(lambda h: h.tensor_copy(out=v4(Vt[q]), in_=r4(2)), [rkv], [Vt[q]])
                M2 = MF2 if d == 0 else MB2
                MN = MNF if d == 0 else MNB
                order = [0, 1, 2, 3] if d == 0 else [3, 2, 1, 0]
                def qbody(q):
                    MA, MB, Pq, PTq, Xq = MAs[q], MBs[q], Pqs[q], PTqs[q], Xqs[q]
                    VTk, BhT, KhT, RTp, YL, MCT, NCt = VTks[q], BhTs[q], KhTs[q], RTps[q], YLs[q], MCTs[q], NCts[q]
                    for (lt, dst) in ((Bt[q], MA), (Kt[q], MB)):
                        for hf in range(2):
                            bk = k.bank()
                            for cc in range(2):
                                c = hf * 2 + cc
                                k.mm(bk[:, cc * 256:(cc + 1) * 256], lt[:, c, :], AR[q][:, c, :], True, True, [lt, AR[q]], [bk])
                            k.V(lambda h, bk=bk, hf=hf, dst=dst: h.tensor_tensor(out=dst[:, hf * 2:hf * 2 + 2, :],
                                in0=bk[:, 0:512].rearrange("p (c n) -> p c n", n=256), in1=bc(M2[:], [128, 2, 256]), op=ALU.mult), [bk, M2], [dst])
                    P0 = Pq[0]
                    bk = k.bank()
                    for c in range(4):
                        k.mm(bk[:, c * 128:(c + 1) * 128], AR[q][:, c, 0:128], Bt[q][:, c, :], True, True, [AR[q], Bt[q]], [bk])
                    k.V(lambda h, bk=bk: h.tensor_tensor(out=P0[:], in0=bk[:, 0:512].rearrange("p (c n) -> p c n", n=128), in1=bc(MN[:], [128, 4, 128]), op=ALU.mult), [bk, MN], [P0])
                    yield
                    X = Xq[0]
                    def tr(src_ap_fn, srcT, dst_ap, dstT):
                        bk = k.bank()
                        for c in range(4):
                            k.mm(bk[:, c * 128:(c + 1) * 128], src_ap_fn(c), IDB[:], True, True, [srcT, IDB], [bk])
                        k.cp(dst_ap, bk[:, 0:512].rearrange("p (c n) -> p c n", n=128), [bk], [dstT])
                    tr(lambda c: AR[q][:, c, 0:128], AR[q], X[:, :, 0:128], X)
                    yield
                    tr(lambda c: Vt[q][:, c, :], Vt[q], VTk[:], VTk)
                    tr(lambda c: Bht[q][:, c, :], Bht[q], BhT[:], BhT)
                    yield
                    tr(lambda c: Kht[q][:, c, :], Kht[q], KhT[:], KhT)
                    yield
                    bk = k.bank()
                    for c in range(4):
                        k.mm(bk[:, c * 128:(c + 1) * 128], MB[:, c, 0:128], VTk[:, c, :], True, True, [MB, VTk], [bk])
                    k.cp(X[:, :, 128:256], bk[:, 0:512].rearrange("p (c n) -> p c n", n=128), [bk], [X])
                    yield
                    xi = 0; pi = 0
                    Pc = P0; PTc_ap = lambda c: MA[:, c, 0:128]; PTc_T = MA
                    for it in range(6):
                        Xn = Xq[1 - xi]
                        for hf in range(2):
                            bk = k.bank()
                            for cc in range(2):
                                c = hf * 2 + cc
                                k.mm(bk[:, cc * 256:(cc + 1) * 256], PTc_ap(c), Xq[xi][:, c, :], True, True, [PTc_T, Xq[xi]], [bk])
                            k.V(lambda h, bk=bk, hf=hf, Xn=Xn, Xo=Xq[xi]: h.tensor_tensor(out=Xn[:, hf * 2:hf * 2 + 2, :],
                                in0=bk[:, 0:512].rearrange("p (c n) -> p c n", n=256), in1=Xo[:, hf * 2:hf * 2 + 2, :], op=ALU.add), [bk, Xq[xi]], [Xn])
                        xi = 1 - xi
                        yield
                        if it < 5:
                            Pn = Pq[1 - pi]; PTn = PTq[1 - pi]
                            bk = k.bank(); bk2 = k.bank()
                            for c in range(4):
                                k.mm(bk[:, c * 128:(c + 1) * 128], PTc_ap(c), Pc[:, c, :], True, True, [PTc_T, Pc], [bk])
                                k.mm(bk2[:, c * 128:(c + 1) * 128], Pc[:, c, :], PTc_ap(c), True, True, [PTc_T, Pc], [bk2])
                            k.cp(Pn[:], bk[:, 0:512].rearrange("p (c n) -> p c n", n=128), [bk], [Pn])
                            k.cp(PTn[:], bk2[:, 0:512].rearrange("p (c n) -> p c n", n=128), [bk2], [PTn])
                            pi = 1 - pi
                            Pc = Pn; PTc_T = PTn; PTc_ap = (lambda c, PTn=PTn: PTn[:, c, :])
                            yield
                    X = Xq[xi]
                    yield
                    bk = k.bank(); bk2 = k.bank()
                    for c in range(4):
                        k.mm(bk[:, c * 128:(c + 1) * 128], X[:, c, 0:128], MA[:, c, 128:256], True, True, [X, MA], [bk])
                    for c in range(4):
                        k.mm(bk2[:, c * 128:(c + 1) * 128], X[:, c, 128:256], MA[:, c, 128:256], True, False, [X, MA], [bk2])
                        k.mm(bk2[:, c * 128:(c + 1) * 128], VTk[:, c, :], MB[:, c, 128:256], False, True, [VTk, MB], [bk2])
                    k.V(lambda h, bk=bk: h.tensor_tensor(out=RTp[:], in0=bk[:, 0:512].rearrange("p (c n) -> p c n", n=128),
                                                         in1=AR[q][:, :, 128:256], op=ALU.add), [bk, AR[q]], [RTp])
                    k.cp(YL[:], bk2[:, 0:512].rearrange("p (c n) -> p c n", n=128), [bk2], [YL])
                    yield
                    bk = k.bank(); bk2 = k.bank()
                    for c in range(4):
                        k.mm(bk[:, c * 128:(c + 1) * 128], X[:, c, 0:128], BhT[:, c, :], True, True, [X, BhT], [bk])
                    for c in range(4):
                        k.mm(bk2[:, c * 128:(c + 1) * 128], BhT[:, c, :], X[:, c, 128:256], True, False, [X, BhT], [bk2])
                        k.mm(bk2[:, c * 128:(c + 1) * 128], KhT[:, c, :], VTk[:, c, :], False, True, [KhT, VTk], [bk2])
                    for c in range(4):
                        k.V(lambda h, c=c, bk=bk: h.scalar_tensor_tensor(out=MCT[:, c, :], in0=IDENT[:], scalar=eg[:, q, c * 64 + gcol:c * 64 + gcol + 1],
                            in1=bk[:, c * 128:(c + 1) * 128], op0=ALU.mult, op1=ALU.add), [IDENT, eg, bk], [MCT])
                    k.cp(NCt[:], bk2[:, 0:512].rearrange("p (c n) -> p c n", n=128), [bk2], [NCt])
                    yield
                    for c in order:
                        zc = Z[(q, d)][zi[(q, d)]]; zn = Z[(q, d)][1 - zi[(q, d)]]
                        bk = k.bank(); bk2 = k.bank()
                        k.mm(bk[:, 0:128], zc[:], RTp[:, c, :], True, True, [zc, RTp], [bk])
                        k.mm(bk2[:, 0:128], MCT[:, c, :], zc[:], True, True, [zc, MCT], [bk2])
                        for hh in range(2):
                            ps = slice(hh * 64, hh * 64 + 64)
                            k.V(lambda h, ps=ps, hh=hh, c=c, bk=bk: h.tensor_tensor(out=yst[ps, q, c * 64:(c + 1) * 64],
                                in0=bk[ps, hh * 64:hh * 64 + 64], in1=YL[ps, c, hh * 64:hh * 64 + 64], op=ALU.add), [bk, YL], [yst])
                        k.V(lambda h, c=c, bk2=bk2, zn=zn: h.tensor_tensor(out=zn[:], in0=bk2[:, 0:128], in1=NCt[:, c, :], op=ALU.add), [bk2, NCt], [zn])
                        zi[(q, d)] = 1 - zi[(q, d)]
                        yield
                gens = [qbody(0), qbody(1)]
                alive = [True, True]
                while any(alive):
                    for gi_ in range(2):
                        if alive[gi_]:
                            try:
                                next(gens[gi_])
                            except StopIteration:
                                alive[gi_] = False
                k.store(YY[d, :, c0:c0 + 256].rearrange("(q p) t -> p q t", p=128), yst[:], yst, YY)

            fseq = [CTX0] + [LAT0 + 256 * i for i in range(64)]
            bseq = [CTX0] + [LAT0 + 256 * i for i in reversed(range(64))]
            for i in range(65):
                scan_pass(fseq[i], 0)
                scan_pass(bseq[i], 1)
        k.barrier()
        if dbg and "YY" in dbg and l == 0:
            k.dma(k.pool, dbg_t["YY"][:], YY[:], [YY], [dbg_t["YY"]])
            k.barrier()

        with ExitStack() as es:
            gup = k.sb(es, "gup", [128, 2, 256]); poolw = k.sb(es, "poolw", [128, 2, 256]); wo = k.sb(es, "wo", [128, 10, D])
            k.load(gup[:], gup_in[:, l], gup_in, gup); k.load(poolw[:], poolw_in[:, l], poolw_in, poolw)
            k.load(wo[:], wout_in[l], wout_in, wo, r=True)
            yy = k.sb(es, "yy", [128, 4, 2, 256]); glo = k.sb(es, "glo", [128, 2, 256])
            up = k.sb(es, "up", [128, 2, 20, 80]); pa = k.sb(es, "pa", [128, 2, 20, 80]); pb = k.sb(es, "pb", [128, 2, 20, 80])
            va = k.sb(es, "va", [128, 2, 20, 64]); vb = k.sb(es, "vb", [128, 2, 20, 64])
            cu = k.sb(es, "cu", [128, 2, 288]); ca = k.sb(es, "ca", [128, 2, 288]); cb = k.sb(es, "cb", [128, 2, 288])
            yv = k.sb(es, "yv", [128, 2, 256]); dv = k.sb(es, "dv", [128, 2, 256]); dsq = k.sb(es, "dsq", [128, 2, 256])
            mu = k.sb(es, "mu", [128, 2, 256]); dd = mu; tmpd = dsq
            mq = k.sb(es, "mq", [128, 10, 256]); ost = k.sb(es, "ost", [128, 4, 256])
            for tl in (up, pa, pb, va, vb, cu, ca, cb):
                k.G(lambda h, tl=tl: h.memset(tl[:], 0.0), [], [tl])
            hvb = lambda w: bc(hv[:, l, w, :].unsqueeze(2), [128, 2, 256])
            GRP = [(0, slice(0, 64)), (0, slice(64, 128)), (1, slice(0, 64)), (1, slice(64, 128))]

            def blk3c(c0, kind, bi):
                k.load(yy[:], YY[:, :, c0:c0 + 256].rearrange("w (q p) t -> p w q t", p=128), YY, yy)
                k.load(glo[:], UT[960:1216, c0:c0 + 256].rearrange("(q p) t -> p q t", p=128), UT, glo)
                k.V(lambda h: h.tensor_tensor(out=yv[:], in0=yy[:, 0], in1=yy[:, 1], op=ALU.add), [yy], [yv])
                bk = k.bank()
                for q in range(2):
                    k.mm(bk[:, q * 256:(q + 1) * 256], BONES[:], yv[:, q, :], True, True, [BONES, yv], [bk])
                k.V(lambda h: h.scalar_tensor_tensor(out=dv[:].rearrange("p q t -> p (q t)"), in0=bk[:, 0:512], scalar=-1.0 / 64,
                    in1=yv[:].rearrange("p q t -> p (q t)"), op0=ALU.mult, op1=ALU.add), [bk, yv], [dv])
                k.G(lambda h: h.tensor_tensor(out=dsq[:], in0=dv[:], in1=dv[:], op=ALU.mult), [dv], [dsq])
                bk = k.bank()
                for q in range(2):
                    k.mm(bk[:, q * 256:(q + 1) * 256], BONES[:], dsq[:, q, :], True, True, [BONES, dsq], [bk])
                rsq(k, mu[:].rearrange("p q t -> p (q t)"), bk[:, 0:512], 1.0 / 64, 64e-5, [bk], mu)
                k.V(lambda h: h.tensor_tensor(out=dv[:], in0=dv[:], in1=mu[:], op=ALU.mult), [dv, mu], [dv])
                k.V(lambda h: h.tensor_tensor(out=dv[:], in0=dv[:], in1=hvb(3), op=ALU.mult), [dv, hv], [dv])
                k.V(lambda h: h.tensor_tensor(out=dv[:], in0=dv[:], in1=hvb(4), op=ALU.add), [dv, hv], [dv])
                k.V(lambda h: h.tensor_tensor(out=dv[:], in0=dv[:], in1=yy[:, 2], op=ALU.add), [dv, yy], [dv])
                k.V(lambda h: h.tensor_tensor(out=dv[:], in0=dv[:], in1=yy[:, 3], op=ALU.add), [dv, yy], [dv])
                bk = k.bank()
                for q in range(2):
                    for kc in range(2):
                        k.mm(bk[:, q * 256:(q + 1) * 256], gup[:, kc, q * 128:(q + 1) * 128], glo[:, kc, :], kc == 0, kc == 1, [gup, glo], [bk])
                k.V(lambda h: h.tensor_tensor(out=mq[:, 0:2, :].rearrange("p q t -> p (q t)").bitcast(F32R), in0=bk[:, 0:512],
                                              in1=dv[:].rearrange("p q t -> p (q t)"), op=ALU.mult), [bk, dv], [mq])
                if kind == "lat":
                    r0 = bi * 4
                    k.load(up[:, 0, :, 8:72], UT[1216:1344, c0 - 512:c0 + 768].rearrange("p (r c) -> p r c", c=64), UT, up)
                    k.load(up[:, 1, :, 8:72], UT[1344:1472, c0 - 512:c0 + 768].rearrange("p (r c) -> p r c", c=64), UT, up)
                    k.V(lambda h: h.tensor_tensor(out=pb[:, :, :, 1:80], in0=up[:, :, :, 0:79], in1=up[:, :, :, 1:80], op=ALU.add), [up], [pb])
                    k.V(lambda h: h.tensor_tensor(out=pa[64:128, 0, :, 2:79], in0=pb[64:128, 0, :, 1:78], in1=pb[64:128, 0, :, 3:80], op=ALU.add), [pb], [pa])
                    k.V(lambda h: h.tensor_tensor(out=pa[:, 1, :, 2:79], in0=pb[:, 1, :, 1:78], in1=pb[:, 1, :, 3:80], op=ALU.add), [pb], [pa])
                    k.V(lambda h: h.tensor_tensor(out=pb[:, 1, :, 4:77], in0=pa[:, 1, :, 2:75], in1=pa[:, 1, :, 6:79], op=ALU.add), [pa], [pb])
                    k.V(lambda h: h.tensor_tensor(out=pa[64:128, 1, :, 8:73], in0=pb[64:128, 1, :, 4:69], in1=pb[64:128, 1, :, 12:77], op=ALU.add), [pb], [pa])
                    hs = [pb, pa, pb, pa]
                    for gi, (pc, ps) in enumerate(GRP):
                        k.G(lambda h, gi=gi, pc=pc, ps=ps: h.tensor_tensor(out=va[ps, pc, 1:20, :], in0=hs[gi][ps, pc, 0:19, 8:72],
                                                                          in1=hs[gi][ps, pc, 1:20, 8:72], op=ALU.add), [pa, pb], [va])
                    k.G(lambda h: h.tensor_tensor(out=vb[64:128, 0, 2:19, :], in0=va[64:128, 0, 1:18, :], in1=va[64:128, 0, 3:20, :], op=ALU.add), [va], [vb])
                    k.G(lambda h: h.tensor_tensor(out=vb[:, 1, 2:19, :], in0=va[:, 1, 1:18, :], in1=va[:, 1, 3:20, :], op=ALU.add), [va], [vb])
                    k.G(lambda h: h.tensor_tensor(out=va[:, 1, 4:17, :], in0=vb[:, 1, 2:15, :], in1=vb[:, 1, 6:19, :], op=ALU.add), [vb], [va])
                    k.G(lambda h: h.tensor_tensor(out=vb[64:128, 1, 8:13, :], in0=va[64:128, 1, 4:9, :], in1=va[64:128, 1, 12:17, :], op=ALU.add), [va], [vb])
                    vs = [va, vb, va, vb]
                    var = 0 if bi == 0 else (1 if bi == 1 else (3 if bi == 62 else (4 if bi == 63 else 2)))
                    for gi, (pc, ps) in enumerate(GRP):
                        d4 = lambda tl: tl[ps, pc, :].rearrange("p (r c) -> p r c", c=64)
                        k.V(lambda h: h.tensor_tensor(out=d4(tmpd), in0=vs[gi][ps, pc, 8:12, :], in1=rc2[ps, var, pc], op=ALU.mult), [va, vb, rc2], [tmpd])
                        k.V(lambda h: h.tensor_tensor(out=d4(dd), in0=d4(tmpd), in1=up[ps, pc, 8:12, 8:72], op=ALU.subtract), [tmpd, up], [dd])
                else:
                    k.load(cu[:, 0, 16:272], UT[1216:1344, c0:c0 + 256], UT, cu)
                    k.load(cu[:, 1, 16:272], UT[1344:1472, c0:c0 + 256], UT, cu)
                    k.V(lambda h: h.tensor_tensor(out=cb[:, :, 1:288], in0=cu[:, :, 0:287], in1=cu[:, :, 1:288], op=ALU.add), [cu], [cb])
                    k.V(lambda h: h.tensor_tensor(out=ca[64:128, 0, 2:287], in0=cb[64:128, 0, 1:286], in1=cb[64:128, 0, 3:288], op=ALU.add), [cb], [ca])
                    k.V(lambda h: h.tensor_tensor(out=ca[:, 1, 2:287], in0=cb[:, 1, 1:286], in1=cb[:, 1, 3:288], op=ALU.add), [cb], [ca])
                    k.V(lambda h: h.tensor_tensor(out=cb[:, 1, 4:285], in0=ca[:, 1, 2:283], in1=ca[:, 1, 6:287], op=ALU.add), [ca], [cb])
                    k.V(lambda h: h.tensor_tensor(out=ca[64:128, 1, 8:281], in0=cb[64:128, 1, 4:277], in1=cb[64:128, 1, 12:285], op=ALU.add), [cb], [ca])
                    hs = [cb, ca, cb, ca]
                    for gi, (pc, ps) in enumerate(GRP):
                        k.V(lambda h: h.tensor_tensor(out=tmpd[ps, pc, :], in0=hs[gi][ps, pc, 16:272], in1=rc1[ps, pc, :], op=ALU.mult), [ca, cb, rc1], [tmpd])
                        k.V(lambda h: h.tensor_tensor(out=dd[ps, pc, :], in0=tmpd[ps, pc, :], in1=cu[ps, pc, 16:272], op=ALU.subtract), [tmpd, cu], [dd])
                for gi, (pc, ps) in enumerate(GRP):
                    for nn in range(2):
                        bk = k.bank()
                        k.mm(bk[:, 0:256], poolw[ps, pc, nn * 128:(nn + 1) * 128], dd[ps, pc, :], True, True, [poolw, dd], [bk])
                        k.A(lambda h, bk=bk, gi=gi, nn=nn: h.activation(out=mq[:, 2 + gi * 2 + nn, :].bitcast(F32R), in_=bk[:, 0:256], func=AF.Copy,
                                                                       scale=pscale[:, l, gi * 2 + nn:gi * 2 + nn + 1]), [bk, pscale], [mq])
                for n4 in range(4):
                    for nn in range(4):
                        n_ = n4 * 4 + nn
                        bk = k.bank()
                        for kc in range(10):
                            k.mm(bk[:, 0:256], wo[:, kc, n_ * 128:(n_ + 1) * 128], mq[:, kc, :], kc == 0, kc == 9, [wo, mq], [bk], r=True)
                        k.cp(ost[:, nn, :], bk[:, 0:256], [bk], [ost])
                    if kind == "lat":
                        r = bi // 16; lc = 64 + (bi % 16) * 256
                        cdma(False, PT, r * D + n4 * 512, 512, lambda off, ln: ost[:, :, off:off + ln], ost, lc, 256)
                    else:
                        for r in range(4):
                            cdma(False, PT, r * D + n4 * 512, 512, lambda off, ln, r=r: ost[:, :, r * 64 + off:r * 64 + off + ln], ost, 0, 64)

            blk3c(CTX0, "ctx", 0)
            for bi in range(64):
                blk3c(LAT0 + 256 * bi, "lat", bi)
            for j in range(NCH):
                coll("ReduceScatter", ALU.add, PT, DT, PT[j], DT[j])
        k.barrier()
        FT = 416
        with ExitStack() as es:
            xt = k.sb(es, "f_x", [128, 16, FT]); h2 = k.sb(es, "f_h", [128, 16, FT], BF16); sq = k.sb(es, "f_sq", [128, 16, FT])
            rs = k.sb(es, "f_rs", [128, FT]); ff = k.sb(es, "f_f", [128, FC, FT], BF16); sl = k.sb(es, "f_sl", [128, FT])
            wgs = [k.sb(es, "f_wg%d" % i, [128, 16, 128], BF16) for i in range(2)]; wus = [k.sb(es, "f_wu%d" % i, [128, 16, 128], BF16) for i in range(2)]
            wds = [k.sb(es, "f_wd%d" % i, [128, FC, 128], BF16) for i in range(2)]
            last = (l == L - 1)
            for i in range(NTL // FT):
                lt0 = i * FT
                k.load(xt[:], xrows(XS, lt0, lt0 + FT), XS, xt)
                cdma(True, DT, 0, D, lambda off, ln: sq[:, :, off:off + ln], sq, lt0, FT)
                modulate(sq, sq, lt0, FT, l, 2, None, [])
                k.V(lambda h: h.tensor_tensor(out=xt[:], in0=xt[:], in1=sq[:], op=ALU.add), [xt, sq], [xt])
                k.A(lambda h: h.activation(out=sq[:], in_=xt[:], func=AF.Square), [xt], [sq])
                bk = k.bank()
                for kc in range(16):
                    k.mm(bk[:, 0:FT], ONES[:], sq[:, kc, :], kc == 0, kc == 15, [ONES, sq], [bk])
                rsq(k, rs[:], bk[:, 0:FT], 1.0 / D, 1e-6, [bk], rs)
                k.V(lambda h: h.tensor_tensor(out=sq[:], in0=xt[:], in1=bc(rs[:].unsqueeze(1), [128, 16, FT]), op=ALU.mult), [xt, rs], [sq])
                modulate(sq, sq, lt0, FT, l, 4, 3, [], out_tile=h2, outT=h2)
                for j in range(FC):
                    wg = wgs[j % 2]; wu = wus[j % 2]
                    k.load(wg[:], FGB[j], FGB, wg)
                    k.load(wu[:], FUB[j], FUB, wu)
                    bk = k.bank(); bk2 = k.bank()
                    for kc in range(16):
                        k.mm(bk[:, 0:FT], wg[:, kc, :], h2[:, kc, :], kc == 0, kc == 15, [wg, h2], [bk])
                    for kc in range(16):
                        k.mm(bk2[:, 0:FT], wu[:, kc, :], h2[:, kc, :], kc == 0, kc == 15, [wu, h2], [bk2])
                    k.A(lambda h, bk=bk: h.activation(out=sl[:], in_=bk[:, 0:FT], func=AF.Silu), [bk], [sl])
                    k.V(lambda h, bk2=bk2, j=j: h.tensor_tensor(out=ff[:, j, :], in0=bk2[:, 0:FT], in1=sl[:], op=ALU.mult), [bk2, sl], [ff])
                for n_ in range(16):
                    wd = wds[n_ % 2]
                    k.load(wd[:], FDB[n_], FDB, wd)
                    bk = k.bank()
                    for fc in range(FC):
                        k.mm(bk[:, 0:FT], wd[:, fc, :], ff[:, fc, :], fc == 0, fc == FC - 1, [wd, ff], [bk])
                    segs = [(0, 64, 1), (64, FT, 0)] if lt0 == 0 else [(0, FT, 0)]
                    for (a, b_, m) in segs:
                        k.V(lambda h, a=a, b_=b_, m=m, n_=n_, bk=bk: h.scalar_tensor_tensor(out=xt[:, n_, a:b_], in0=bk[:, a:b_],
                            scalar=DER[:, l, 5, n_, m:m + 1], in1=xt[:, n_, a:b_], op0=ALU.mult, op1=ALU.add), [bk, DER, xt], [xt])
                if not last:
                    k.store(xrows(XT, lt0, lt0 + FT), xt[:], xt, XT)
                else:
                    k.A(lambda h: h.activation(out=sq[:], in_=xt[:], func=AF.Square), [xt], [sq])
                    bk = k.bank()
                    for kc in range(16):
                        k.mm(bk[:, 0:FT], ONES[:], sq[:, kc, :], kc == 0, kc == 15, [ONES, sq], [bk])
                    rsq(k, rs[:], bk[:, 0:FT], 1.0 / D, 1e-6, [bk], rs)
                    k.V(lambda h: h.tensor_tensor(out=xt[:], in0=xt[:], in1=bc(rs[:].unsqueeze(1), [128, 16, FT]), op=ALU.mult), [xt, rs], [xt])
                    k.V(lambda h: h.tensor_tensor(out=xt[:], in0=xt[:], in1=bc(fin[:].unsqueeze(2), [128, 16, FT]), op=ALU.mult), [xt, fin], [xt])
                    k.store(xrows(out_d, lt0, lt0 + FT), xt[:], xt, out_d)
        k.barrier()
        XS = XT
    return k


def _host_inputs(inp, depth=4):
    f = np.float32
    L = depth
    x = inp["x"]; ctx = inp["ctx"]
    ORK = [0, 1024, 2048]
    maps = []
    rc2 = np.zeros((128, 5, 2, 4, 64), f); rc1 = np.zeros((128, 2, 256), f)
    for pc in range(2):
        for half in range(2):
            gi = pc * 2 + half; W = 2 ** (gi + 1); hf = W // 2
            ps = slice(half * 64, half * 64 + 64)
            cc = np.arange(64); ccnt = np.clip(cc + hf, 0, 64) - np.clip(cc - hf, 0, 64)
            for var, r0 in enumerate([0, 4, 128, 248, 252]):
                rr = np.arange(r0, r0 + 4); rcnt = np.clip(rr + hf, 0, 256) - np.clip(rr - hf, 0, 256)
                rc2[ps, var, pc] = (1.0 / (rcnt[:, None] * ccnt[None, :])).astype(f)[None]
            tt = np.arange(256); tcnt = np.clip(tt + hf, 0, 256) - np.clip(tt - hf, 0, 256)
            rc1[ps, pc] = (1.0 / tcnt).astype(f)[None]
    pm = lambda v: np.ascontiguousarray(v.reshape(-1, 128).T)
    FG = np.ascontiguousarray(inp["ffn_gate"][:L].reshape(L, 16, 128, FC, 128).transpose(0, 3, 2, 1, 4))
    FU = np.ascontiguousarray(inp["ffn_up"][:L].reshape(L, 16, 128, FC, 128).transpose(0, 3, 2, 1, 4))
    FD = np.ascontiguousarray(inp["ffn_down"][:L].reshape(L, FC, 128, 16, 128).transpose(0, 3, 2, 1, 4))
    for c in range(8):
        b, s_ = c // 4, c % 4
        g = s_
        m = {}
        m["xT"] = np.ascontiguousarray(np.concatenate([ctx[b, 64 * s_:64 * s_ + 64], x[b, 4096 * s_:4096 * s_ + 4096]], 0).T)
        cT = np.zeros((128, 16, 2), f); cT[:, :, 0] = pm(inp["c"][b]); cT[:, :, 1] = pm(inp["c_ctx"]); m["cT"] = cT
        m["adaw"] = np.ascontiguousarray(inp["ada_w"][:L, :, 3072 * s_:3072 * s_ + 3072])
        m["adab"] = np.ascontiguousarray(np.stack([pm(inp["ada_b"][l, 3072 * s_:3072 * s_ + 3072]) for l in range(L)], 1))
        nr = np.zeros((128, 2, L, 16), f)
        for l in range(L):
            nr[:, 0, l] = pm(inp["norm_mix"][l]); nr[:, 1, l] = pm(inp["norm_ffn"][l])
        m["nrm"] = nr; m["fin"] = pm(inp["final_norm"])
        cols = []
        for o in ORK:
            cols += list(range(o + g * 256, o + g * 256 + 256))
        cols += list(range(3072, 3072 + 448))
        for gi in range(4):
            cols += list(range(3520 + gi * 256 + s_ * 64, 3520 + gi * 256 + s_ * 64 + 64))
        cols = np.array(cols)
        m["win"] = np.ascontiguousarray(inp["w_in"][:L][:, :, cols].reshape(L, 16, 128, 1472).transpose(0, 2, 1, 3))
        cw = np.zeros((128, L, 6, 3), f)
        for l in range(L):
            for wh in range(3):
                for q in range(2):
                    cw[:, l, wh * 2 + q, :] = inp["conv_rkv"][l][:, wh * 1024 + g * 256 + q * 128: wh * 1024 + g * 256 + q * 128 + 128].T
        m["convw"] = cw
        hsl = lambda a, l, q: a[l][g * 256 + q * 128: g * 256 + q * 128 + 128]
        db = np.zeros((128, L, 2, 2), f); ib = np.zeros((128, L, 2, 2), f); hvv = np.zeros((128, L, 5, 2), f)
        for l in range(L):
            for q in range(2):
                for d in range(2):
                    db[:, l, d, q] = inp["decay_bias"][l, d, g * 256 + q * 128: g * 256 + q * 128 + 128]
                    ib[:, l, d, q] = inp["iclr_bias"][l, d, g * 256 + q * 128: g * 256 + q * 128 + 128]
                for wi, nm in enumerate(["k_k", "k_a", "r_k", "gn_w", "gn_b"]):
                    hvv[:, l, wi, q] = hsl(inp[nm], l, q)
        m["dbias"] = db; m["ibias"] = ib; m["hv"] = hvv
        m["dup"] = np.ascontiguousarray(np.transpose(inp["decay_up"][:L, :, :, g * 256:g * 256 + 256], (2, 0, 1, 3)))
        m["iup"] = np.ascontiguousarray(np.transpose(inp["iclr_up"][:L, :, :, g * 256:g * 256 + 256], (2, 0, 1, 3)))
        gu = inp["gate_up"][:L, :, g * 256:g * 256 + 256].reshape(L, 2, 128, 256)
        m["gup"] = np.ascontiguousarray(np.transpose(gu, (2, 0, 1, 3)))
        pw = np.zeros((128, L, 2, 256), f)
        for l in range(L):
            for pc in range(2):
                for half in range(2):
                    pw[half * 64:half * 64 + 64, l, pc] = inp["pool_w"][l, pc * 2 + half, s_ * 64:s_ * 64 + 64, :]
        m["poolw"] = pw
        m["pscale"] = np.ascontiguousarray(np.stack([pm(inp["pool_scale"][l]) for l in range(L)], 1))
        m["wout"] = np.ascontiguousarray(np.concatenate([inp["w_out"][:L, g * 256:g * 256 + 256], inp["w_out"][:L, 1024:2048]], 1).reshape(L, 10, 128, D).transpose(0, 2, 1, 3))
        m["fg"] = FG; m["fu"] = FU; m["fd"] = FD
        m["rc2"] = rc2; m["rc1"] = rc1
        maps.append({kk_: np.ascontiguousarray(v, dtype=f) for kk_, v in m.items()})
    return maps


def kernel(**inputs):
    inp = {k_: np.asarray(v) for k_, v in inputs.items()}
    k = build(4)
    maps = _host_inputs(inp, 4)
    res = run_bass_kernel_spmd(k.nc, maps, core_ids=list(range(8)))
    out = np.zeros((2, 16384, 2048), np.float32)
    for c in range(8):
        b, s_ = c // 4, c % 4
        o = res.results[c]["out"]
        out[b, 4096 * s_:4096 * s_ + 4096] = o[:, 64:].T
    return out
```

```python
import numpy as np
from contextlib import ExitStack
import concourse.bass as bass
import concourse.mybir as mybir
from concourse.bass_utils import run_bass_kernel_spmd

F32 = mybir.dt.float32
F32R = mybir.dt.float32r
BF16 = mybir.dt.bfloat16
AF = mybir.ActivationFunctionType
ALU = mybir.AluOpType

D = 2048; KC = 16; NTL = 4160; TT = 416; NTI = 10; CW = 104; NCH = 40
UC = 17920; CTX0 = 256; LAT0 = 1024
DFF = 5632; FC = 44
C0 = float(np.exp(-0.5))
GROUPS = [[0, 1, 2, 3], [4, 5, 6, 7]]
UCH = [(0, 128), (128, 128), (256, 128), (384, 128), (512, 128), (640, 128),
       (768, 96), (864, 96), (960, 128), (1088, 128), (1216, 128), (1344, 128)]


class Sem:
    def __init__(s, h, dma):
        s.h = h; s.dma = dma; s.total = 0


class T:
    def __init__(s, t, name):
        s.t = t; s.name = name; s.w = []; s.r = []; s.dsem = None

    def __getitem__(s, k):
        return s.t[k]


class Eng:
    def __init__(s, h, sem, name):
        s.h = h; s.sem = sem; s.cnt = 0; s.seen = {}; s.name = name


class B:
    def __init__(s):
        nc = bass.Bass("TRN2", target_bir_lowering=False)
        s.nc = nc
        s.ges = ExitStack()
        s.nsem = 0
        s.dsems = []
        s.pe = Eng(nc.tensor, s.newsem(False), "pe")
        s.act = Eng(nc.scalar, s.newsem(False), "act")
        s.dve = Eng(nc.vector, s.newsem(False), "dve")
        s.pool = Eng(nc.gpsimd, s.newsem(False), "pool")
        s.sp = Eng(nc.sync, s.newsem(False), "sp")
        s.engs = [s.pe, s.act, s.dve, s.pool, s.sp]
        s.banks = [T(s.ges.enter_context(nc.psum_tensor("bank%d" % i, [128, 512], F32)), "bank%d" % i)
                   for i in range(8)]
        s.bi = 0
        s.flip = 0

    def newsem(s, dma):
        s.nsem += 1
        sm = Sem(s.ges.enter_context(s.nc.semaphore("sm%d" % s.nsem)), dma)
        if dma:
            s.dsems.append(sm)
        return sm

    def bank(s):
        b = s.banks[s.bi]; s.bi = (s.bi + 1) % 8
        return b

    def sb(s, es, name, shape, dt=F32):
        s.nsb = getattr(s, "nsb", 0) + 1
        return T(es.enter_context(s.nc.sbuf_tensor("sb%d_%s" % (s.nsb, name), list(shape), dt)), name)

    def dram(s, name, shape, kind="Internal", dt=F32):
        return T(s.nc.dram_tensor(name, list(shape), dt, kind=kind).ap(), name)

    def _wait(s, eng, toks):
        need = {}
        for (sm, v) in toks:
            if sm.dma:
                v = sm.total
            if sm is eng.sem and eng is s.pe:
                continue
            if need.get(sm, 0) < v:
                need[sm] = v
        for sm, v in need.items():
            if eng.seen.get(sm, 0) >= v:
                continue
            eng.h.wait_ge(sm.h, v)
            eng.seen[sm] = v

    def _deps(s, reads, writes):
        toks = []
        for b in reads:
            toks += b.w
        for b in writes:
            toks += b.w; toks += b.r
        return toks

    def _upd(s, tok, reads, writes):
        for b in writes:
            b.w = [tok]; b.r = []
        for b in reads:
            if b not in writes:
                b.r = [t for t in b.r if t[0] is not tok[0]] + [tok]

    def op(s, eng, fn, reads=(), writes=()):
        s._wait(eng, s._deps(reads, writes))
        ins = fn(eng.h)
        eng.cnt += 1
        ins.then_inc(eng.sem.h, 1)
        s._upd((eng.sem, eng.cnt), reads, writes)
        return ins

    def dma(s, eng, out, in_, reads, writes, owner=None, fn=None, inc=16):
        s._wait(eng, s._deps(reads, writes))
        ow = owner or writes[0]
        if ow.dsem is None:
            cache = s.__dict__.setdefault('semcache', {})
            if ow.name not in cache:
                cache[ow.name] = s.newsem(True)
            ow.dsem = cache[ow.name]
        sm = ow.dsem
        sm.total += inc
        if fn is None:
            ins = eng.h.dma_start(out=out, in_=in_)
        else:
            ins = fn(eng.h)
        ins.then_inc(sm.h, inc)
        s._upd((sm, sm.total), reads, writes)

    def load(s, out, in_, src, dst, owner=None, r=False, cast=False):
        if cast:
            s.dma(s.pool, out, in_, [src], [dst], owner)
            return
        if r:
            s.dma(s.pool, out.bitcast(F32R), in_, [src], [dst], owner)
            return
        s.dma(s.sp, out, in_, [src], [dst], owner)

    def store(s, out, in_, src, dst, owner=None):
        s.dma(s.pool, out, in_, [src], [dst], owner)

    def barrier(s):
        for e in s.engs:
            toks = [(o.sem, o.cnt) for o in s.engs if o is not e and o.cnt > 0]
            toks += [(sm, sm.total) for sm in s.dsems if sm.total > 0]
            s._wait(e, toks)

    def mm(s, out, lhsT, rhs, start, stop, reads, writes, r=False):
        if r:
            lhsT = lhsT.bitcast(mybir.dt.float32r); rhs = rhs.bitcast(mybir.dt.float32r)
        return s.op(s.pe, lambda h: h.matmul(out, lhsT=lhsT, rhs=rhs, start=start, stop=stop), reads, writes)

    def V(s, fn, reads, writes):
        return s.op(s.dve, fn, reads, writes)

    def A(s, fn, reads, writes):
        return s.op(s.act, fn, reads, writes)

    def G(s, fn, reads, writes):
        return s.op(s.pool, fn, reads, writes)

    def cp(s, out, in_, reads, writes, eng=None):
        if eng is None:
            s.flip ^= 1
            eng = s.act if s.flip else s.dve
        if eng is s.act:
            return s.A(lambda h: h.activation(out=out, in_=in_, func=AF.Copy), reads, writes)
        return s.op(eng, lambda h: h.tensor_copy(out=out, in_=in_), reads, writes)


def rsq(k, out, in_, scale, bias, reads, wT):
    k.A(lambda h: h.activation(out=out, in_=in_, func=AF.Sqrt, scale=scale, bias=bias), reads, [wT])
    k.V(lambda h: h.reciprocal(out=out, in_=out), [wT], [wT])


def bc(ap, shape):
    return ap.broadcast_to(list(shape))


def build(depth=4, dbg=None):
    k = B()
    nc = k.nc
    ges = k.ges
    L = depth

    def din(name, shape, dt=F32):
        return k.dram(name, shape, "ExternalInput", dt)

    xT_in = din("xT", [D, NTL]); cT_in = din("cT", [128, 16, 2])
    adaw_in = din("adaw", [L, D, 3072]); adab_in = din("adab", [128, L, 24])
    nrm_in = din("nrm", [128, 2, L, 16]); fin_in = din("fin", [128, 16])
    win_in = din("win", [L, 128, 16, 1472]); convw_in = din("convw", [128, L, 6, 3])
    dbias_in = din("dbias", [128, L, 2, 2]); ibias_in = din("ibias", [128, L, 2, 2])
    dup_in = din("dup", [96, L, 2, 256]); iup_in = din("iup", [96, L, 2, 256])
    gup_in = din("gup", [128, L, 2, 256]); hv_in = din("hv", [128, L, 5, 2])
    poolw_in = din("poolw", [128, L, 2, 256]); pscale_in = din("pscale", [128, L, 8])
    wout_in = din("wout", [L, 128, 10, D])
    fg_in = din("fg", [L, FC, 128, 16, 128]); fu_in = din("fu", [L, FC, 128, 16, 128]); fd_in = din("fd", [L, 16, 128, FC, 128])
    rc2_in = din("rc2", [128, 5, 2, 4, 64]); rc1_in = din("rc1", [128, 2, 256])
    out_d = k.dram("out", [D, NTL], "ExternalOutput")

    XT = k.dram("XT", [D, NTL]); HT = k.dram("HT", [NCH, D, CW]); HG = k.dram("HG", [NCH, 4 * D, CW])
    UT = k.dram("UT", [1472, UC]); YY = k.dram("YY", [4, 256, UC])
    PT = k.dram("PT", [NCH, 4 * D, CW]); DT = k.dram("DT", [NCH, D, CW])
    MODP = k.dram("MODP", [128, 192]); MODG = k.dram("MODG", [512, 192])
    FGB = k.dram("FGB", [FC, 128, 16, 128], dt=BF16); FUB = k.dram("FUB", [FC, 128, 16, 128], dt=BF16); FDB = k.dram("FDB", [16, 128, FC, 128], dt=BF16)
    dbg_t = {}
    if dbg:
        for nm, shp in dbg.items():
            dbg_t[nm] = k.dram("dbg_" + nm, shp, "ExternalOutput")

    cst = T(None, "cstgrp")

    def cload(name, src, shape):
        t = k.sb(ges, name, shape)
        k.load(t[:], src[:], src, t, owner=cst)
        return t

    convw = cload("convw", convw_in, [128, L, 6, 3]); dbias = cload("dbias", dbias_in, [128, L, 2, 2])
    ibias = cload("ibias", ibias_in, [128, L, 2, 2]); hv = cload("hv", hv_in, [128, L, 5, 2])
    pscale = cload("pscale", pscale_in, [128, L, 8]); nrm = cload("nrm", nrm_in, [128, 2, L, 16])
    fin = cload("fin", fin_in, [128, 16]); rc2 = cload("rc2", rc2_in, [128, 5, 2, 4, 64])
    rc1 = cload("rc1", rc1_in, [128, 2, 256]); adab = cload("adab", adab_in, [128, L, 24])
    cT = cload("cT", cT_in, [128, 16, 2])

    ONES = k.sb(ges, "ONES", [128, 128]); IDENT = k.sb(ges, "IDENT", [128, 128]); BONES = k.sb(ges, "BONES", [128, 128])
    MF2 = k.sb(ges, "MF2", [128, 1, 256]); MB2 = k.sb(ges, "MB2", [128, 1, 256])
    MNF = k.sb(ges, "MNF", [128, 1, 128]); MNB = k.sb(ges, "MNB", [128, 1, 128])
    RST = k.sb(ges, "RST", [128, 256]); ZERO = k.sb(ges, "ZERO", [128, 512])
    DER = k.sb(ges, "DER", [128, L, 6, 16, 2])

    k.G(lambda h: h.memset(ONES[:], 1.0), [], [ONES])
    k.G(lambda h: h.memset(ZERO[:], 0.0), [], [ZERO])
    k.G(lambda h: h.memset(IDENT[:], 1.0), [], [IDENT])
    k.G(lambda h: h.affine_select(out=IDENT[:], in_=IDENT[:], pattern=[[1, 128]], base=0, channel_multiplier=-1,
                                  compare_op=ALU.is_equal, fill=0.0), [IDENT], [IDENT])
    k.G(lambda h: h.memset(BONES[:], 0.0), [], [BONES])
    k.G(lambda h: h.memset(BONES[0:64, 0:64], 1.0), [BONES], [BONES])
    k.G(lambda h: h.memset(BONES[64:128, 64:128], 1.0), [BONES], [BONES])

    def mask(t, ap, nrep, step, cm, cmp):
        k.G(lambda h: h.memset(ap, 1.0), [t], [t])
        k.G(lambda h: h.affine_select(out=ap, in_=ap, pattern=[[0, nrep], [step, 128]], base=0, channel_multiplier=cm,
                                      compare_op=cmp, fill=0.0), [t], [t])
    mask(MF2, MF2[:, :, 0:128], 1, 1, -1, ALU.is_gt)
    mask(MF2, MF2[:, :, 128:256], 1, 1, -1, ALU.is_ge)
    mask(MB2, MB2[:, :, 0:128], 1, -1, 1, ALU.is_gt)
    mask(MB2, MB2[:, :, 128:256], 1, -1, 1, ALU.is_ge)
    mask(MNF, MNF[:], 1, -1, 1, ALU.is_gt)
    mask(MNB, MNB[:], 1, 1, -1, ALU.is_gt)
    k.G(lambda h: h.memset(RST[:], 1.0), [], [RST])
    rst4 = RST[:].rearrange("p (c t) -> p c t", t=64)
    k.G(lambda h: h.memset(rst4[:, :, 0:1], 0.0), [RST], [RST])


    def coll(kind, op, src, dst, sap=None, dap=None):
        sap = src[:] if sap is None else sap
        dap = dst[:] if dap is None else dap
        k.dma(k.pool, None, None, [src], [dst], fn=lambda h: h.collective_compute(
            kind, op, replica_groups=GROUPS, ins=[sap], outs=[dap]), inc=1)

    def cdma(load, DR, row0, nrows, sb_fn, T_sb, lt0, n, r=False, cast=False):
        lt = lt0
        while lt < lt0 + n:
            j = lt // CW; a = lt % CW; ln = min(CW - a, lt0 + n - lt)
            dap = DR[j, row0:row0 + nrows, a:a + ln].rearrange("(kc p) t -> p kc t", p=128)
            sap = sb_fn(lt - lt0, ln)
            if load:
                k.load(sap, dap, DR, T_sb, r=r, cast=cast)
            else:
                k.store(dap, sap, T_sb, DR)
            lt += ln

    with ExitStack() as es:
        sc = k.sb(es, "sc", [128, 16, 2])
        k.A(lambda h: h.activation(out=sc[:], in_=cT[:], func=AF.Silu), [cT], [sc])
        aw = [k.sb(es, "aw%d" % i, [128, 16, 512]) for i in range(2)]
        modp = k.sb(es, "modp", [128, L, 24, 2])
        bk = k.bank()
        n = 0
        for l in range(L):
            for i4 in range(6):
                t = aw[n % 2]; n += 1
                k.load(t[:], adaw_in[l, :, i4 * 512:(i4 + 1) * 512].rearrange("(kc p) n -> p kc n", p=128), adaw_in, t)
                for jj in range(4):
                    i = i4 * 4 + jj
                    col = (l * 24 + i) * 2
                    for kc in range(16):
                        k.mm(bk[:, col:col + 2], t[:, kc, jj * 128:(jj + 1) * 128], sc[:, kc, :], kc == 0, kc == 15, [t, sc], [bk])
        bv = bk[:, 0:L * 48].rearrange("p (l i m) -> p l i m", l=L, m=2)
        k.V(lambda h: h.tensor_tensor(out=modp[:], in0=bv, in1=bc(adab[:].unsqueeze(3), [128, L, 24, 2]), op=ALU.add), [bk, adab], [modp])
        k.store(MODP[:, 0:L * 48], modp[:].rearrange("p l i m -> p (l i m)"), modp, MODP)
        coll("AllGather", ALU.bypass, MODP, MODG)
        MOD = k.sb(es, "MOD", [128, 4, 192])
        k.load(MOD[:], MODG[:].rearrange("(r p) f -> p r f", p=128), MODG, MOD)
        MODv = MOD[:, :, 0:L * 48].rearrange("p r (l i m) -> p r l i m", l=L, m=2)
        for w in range(6):
            j0 = 16 * w
            while j0 < 16 * w + 16:
                r = j0 // 24; j1 = min(16 * w + 16, (r + 1) * 24)
                k.V(lambda h, r=r, j0=j0, j1=j1, w=w: h.tensor_copy(
                    out=DER[:, :, w, j0 - 16 * w:j1 - 16 * w, :], in_=MODv[:, r, :, j0 - 24 * r:j1 - 24 * r, :]), [MOD], [DER])
                j0 = j1
        for (w, wn) in ((1, 0), (4, 1)):
            k.V(lambda h, w=w: h.tensor_scalar(out=DER[:, :, w], in0=DER[:, :, w], scalar1=1.0, scalar2=None, op0=ALU.add), [DER], [DER])
            k.V(lambda h, w=w, wn=wn: h.tensor_tensor(out=DER[:, :, w], in0=DER[:, :, w],
                                                     in1=bc(nrm[:, wn].unsqueeze(3), [128, L, 16, 2]), op=ALU.mult), [DER, nrm], [DER])
    k.barrier()

    def modulate(tile, T_, lt0, ncols, l, wg, wsh, reads, r=False, out_tile=None, outT=None):
        segs = []
        if lt0 < 64:
            segs.append((0, 64 - lt0, 1))
            segs.append((64 - lt0, ncols, 0))
        else:
            segs.append((0, ncols, 0))
        for (a, b_, m) in segs:
            n_ = b_ - a
            if wg is not None:
                k.V(lambda h, a=a, b_=b_, m=m, n_=n_: h.tensor_tensor(out=(tile[:, :, a:b_].bitcast(F32R) if r else tile[:, :, a:b_]), in0=tile[:, :, a:b_],
                    in1=bc(DER[:, l, wg, :, m:m + 1], [128, 16, n_]), op=ALU.mult), [T_, DER] + reads, [T_])
            if wsh is not None:
                k.V(lambda h, a=a, b_=b_, m=m, n_=n_: h.tensor_tensor(out=(out_tile[:, :, a:b_] if out_tile is not None else (tile[:, :, a:b_].bitcast(F32R) if r else tile[:, :, a:b_])), in0=tile[:, :, a:b_],
                    in1=bc(DER[:, l, wsh, :, m:m + 1], [128, 16, n_]), op=ALU.add), [T_, DER] + reads, [T_ if outT is None else outT])

    def rms_tile(es_t, xt, sq, rs, eps=1e-6):
        k.A(lambda h: h.activation(out=sq[:], in_=xt[:], func=AF.Square), [xt], [sq])
        bk = k.bank()
        for kc in range(16):
            k.mm(bk[:, 0:TT], ONES[:], sq[:, kc, :], kc == 0, kc == 15, [ONES, sq], [bk])
        rsq(k, rs[:], bk[:, 0:TT], 1.0 / D, eps, [bk], rs)

    xrows = lambda Tn, c0, c1: Tn[:, c0:c1].rearrange("(kc p) t -> p kc t", p=128)

    def ucol(r, lt):
        return CTX0 + 64 * r + lt if lt < 64 else LAT0 + 4096 * r + (lt - 64)

    for (c0, c1) in ((0, 256), (512, 1024), (17408, 17920)):
        for r0 in range(0, 1472, 128):
            r1 = min(1472, r0 + 128)
            for cc in range(c0, c1, 512):
                ce = min(c1, cc + 512)
                k.store(UT[r0:r1, cc:ce], ZERO[0:r1 - r0, 0:ce - cc], ZERO, UT)

    XS = xT_in
    for l in range(L):
        with ExitStack() as es:
            xts = [k.sb(es, "p1x%d" % i, [128, 16, TT]) for i in range(2)]
            sq = k.sb(es, "p1sq", [128, 16, TT]); rs = k.sb(es, "p1rs", [128, TT])
            for j in range(FC):
                k.dma(k.pool, FGB[j], fg_in[l, j], [fg_in], [FGB])
                k.dma(k.pool, FUB[j], fu_in[l, j], [fu_in], [FUB])
            for n_ in range(16):
                k.dma(k.pool, FDB[n_], fd_in[l, n_], [fd_in], [FDB])
            for i in range(NTI):
                xt = xts[i % 2]
                k.load(xt[:], xrows(XS, i * TT, (i + 1) * TT), XS, xt)
                rms_tile(es, xt, sq, rs)
                k.V(lambda h: h.tensor_tensor(out=xt[:], in0=xt[:], in1=bc(rs[:].unsqueeze(1), [128, 16, TT]), op=ALU.mult), [xt, rs], [xt])
                modulate(xt, xt, i * TT, TT, l, 1, 0, [])
                cdma(False, HT, 0, D, lambda off, ln, xt=xt: xt[:, :, off:off + ln], xt, i * TT, TT)
            for j in range(NCH):
                coll("AllGather", ALU.bypass, HT, HG, HT[j], HG[j])
        k.barrier()
        with ExitStack() as es:
            wsl = k.sb(es, "wsl", [128, 16, 1472], BF16)
            k.load(wsl[:], win_in[l], win_in, wsl, cast=True)
            hgs = [k.sb(es, "hg%d" % i, [128, 16, TT], BF16) for i in range(2)]
            ust = k.sb(es, "ust", [128, 12, TT])
            n = 0
            for r in range(4):
                for i in range(NTI):
                    hg = hgs[n % 2]; n += 1
                    cdma(True, HG, r * D, D, lambda off, ln, hg=hg: hg[:, :, off:off + ln], hg, i * TT, TT, cast=True)
                    for j, (c0, M) in enumerate(UCH):
                        bk = k.bank()
                        for kc in range(16):
                            k.mm(bk[0:M, 0:TT], wsl[:, kc, c0:c0 + M], hg[:, kc, :], kc == 0, kc == 15, [wsl, hg], [bk])
                        fn = AF.Tanh if j == 6 else (AF.Sigmoid if j in (8, 9) else None)
                        if fn is not None:
                            k.A(lambda h, fn=fn, M=M, j=j, bk=bk: h.activation(out=ust[0:M, j, :], in_=bk[0:M, 0:TT], func=fn), [bk], [ust])
                        else:
                            k.cp(ust[0:M, j, :], bk[0:M, 0:TT], [bk], [ust])
                    lt0 = i * TT
                    segs = [(0, 64, ucol(r, 0)), (64, TT, ucol(r, 64))] if i == 0 else [(0, TT, ucol(r, lt0))]
                    for (a, b_, uc0) in segs:
                        n_ = b_ - a
                        k.store(UT[0:768, uc0:uc0 + n_].rearrange("(j p) t -> p j t", p=128), ust[:, 0:6, a:b_], ust, UT)
                        k.store(UT[768:864, uc0:uc0 + n_], ust[0:96, 6, a:b_], ust, UT)
                        k.store(UT[864:960, uc0:uc0 + n_], ust[0:96, 7, a:b_], ust, UT)
                        k.store(UT[960:1472, uc0:uc0 + n_].rearrange("(j p) t -> p j t", p=128), ust[:, 8:12, a:b_], ust, UT)
        k.barrier()
        if dbg and "UT" in dbg and l == 0:
            k.dma(k.pool, dbg_t["UT"][:], UT[:], [UT], [dbg_t["UT"]])
            k.barrier()

        with ExitStack() as es:
            dup = k.sb(es, "dup", [96, 2, 256]); iup = k.sb(es, "iup", [96, 2, 256])
            k.load(dup[:], dup_in[:, l], dup_in, dup); k.load(iup[:], iup_in[:, l], iup_in, iup)
            u6s = [k.sb(es, "u6_%d" % i, [128, 6, 258]) for i in range(2)]
            los = [k.sb(es, "lo_%d" % i, [96, 2, 256]) for i in range(2)]
            sbt = lambda nm, shp=(128, 2, 256), dt=F32: k.sb(es, nm, list(shp), dt)
            IDB = k.sb(es, "IDB", [128, 128], BF16)
            k.V(lambda h: h.tensor_copy(out=IDB[:], in_=IDENT[:]), [IDENT], [IDB])
            rkv = sbt("rkv", (128, 6, 256)); tmp6 = sbt("tmp6", (128, 6, 256))
            kk = sbt("kk"); kksq = sbt("kksq"); kkn = sbt("kkn"); rn = sbt("rn")
            sg = sbt("sg"); cs = sbt("cs"); ex = sbt("ex"); sbb = sbt("sbb"); sbx = sbt("sbx")
            eg = sbt("eg"); eng = sbt("eng"); ega = sbt("ega"); av = sbt("av"); tB = sbt("tB"); t1 = sbt("t1"); kd = sbt("kd")
            bon = sbt("bon"); yst = sbt("yst")
            AR = [sbt("AR%d" % q, (128, 4, 256), BF16) for q in range(2)]
            Bt = [sbt("Bt%d" % q, (128, 4, 128), BF16) for q in range(2)]; Kt = [sbt("Kt%d" % q, (128, 4, 128), BF16) for q in range(2)]
            Bht = [sbt("Bht%d" % q, (128, 4, 128), BF16) for q in range(2)]; Kht = [sbt("Kht%d" % q, (128, 4, 128), BF16) for q in range(2)]
            Vt = [sbt("Vt%d" % q, (128, 4, 128), BF16) for q in range(2)]
            for tl in AR + Bt + Kt + Bht + Kht + Vt:
                k.G(lambda h, tl=tl: h.memset(tl[:], 0.0), [], [tl])
            MAs = [sbt("MA%d" % q, (128, 4, 256), BF16) for q in range(2)]; MBs = [sbt("MB%d" % q, (128, 4, 256), BF16) for q in range(2)]
            Pqs = [[sbt("Pq%d_%d" % (q, i), (128, 4, 128), BF16) for i in range(2)] for q in range(2)]; PTqs = [[sbt("PTq%d_%d" % (q, i), (128, 4, 128), BF16) for i in range(2)] for q in range(2)]
            Xqs = [[sbt("Xq%d_%d" % (q, i), (128, 4, 256), BF16) for i in range(2)] for q in range(2)]
            VTks = [sbt("VTk%d" % q, (128, 4, 128), BF16) for q in range(2)]; BhTs = [sbt("BhT%d" % q, (128, 4, 128), BF16) for q in range(2)]; KhTs = [sbt("KhT%d" % q, (128, 4, 128), BF16) for q in range(2)]
            RTps = [sbt("RTp%d" % q, (128, 4, 128), BF16) for q in range(2)]; YLs = [sbt("YL%d" % q, (128, 4, 128)) for q in range(2)]; MCTs = [sbt("MCT%d" % q, (128, 4, 128), BF16) for q in range(2)]; NCts = [sbt("NCt%d" % q, (128, 4, 128)) for q in range(2)]
            Z = {}
            for q in range(2):
                for d in range(2):
                    Z[(q, d)] = [sbt("Z%d%d%d" % (q, d, i), (128, 128), BF16) for i in range(2)]
                    z0 = Z[(q, d)][0]
                    k.G(lambda h, z=z0: h.memset(z[:], 0.0), [], [z0])
            zi = {(q, d): 0 for q in range(2) for d in range(2)}
            npass = [0]

            def bdw(dst, dstslice, fn_in, reads, eng=None):
                pass

            def scan_pass(c0, d):
                n = npass[0]; npass[0] += 1
                u6 = u6s[n % 2]; lo = los[n % 2]
                k.load(u6[:], UT[0:768, c0 - 1:c0 + 257].rearrange("(j p) t -> p j t", p=128), UT, u6)
                k.load(lo[:, 0, :], UT[768:864, c0:c0 + 256], UT, lo)
                k.load(lo[:, 1, :], UT[864:960, c0:c0 + 256], UT, lo)
                cw = lambda tap: bc(convw[:, l, :, tap:tap + 1], [128, 6, 256])
                k.V(lambda h: h.tensor_tensor(out=rkv[:], in0=u6[:, :, 0:256], in1=cw(0), op=ALU.mult), [u6, convw], [rkv])
                k.G(lambda h: h.tensor_tensor(out=tmp6[:], in0=u6[:, :, 1:257], in1=cw(1), op=ALU.mult), [u6, convw], [tmp6])
                k.V(lambda h: h.tensor_tensor(out=rkv[:], in0=rkv[:], in1=tmp6[:], op=ALU.add), [rkv, tmp6], [rkv])
                k.G(lambda h: h.tensor_tensor(out=tmp6[:], in0=u6[:, :, 2:258], in1=cw(2), op=ALU.mult), [u6, convw], [tmp6])
                k.V(lambda h: h.tensor_tensor(out=rkv[:], in0=rkv[:], in1=tmp6[:], op=ALU.add), [rkv, tmp6], [rkv])
                R_ = rkv[:, 0:2, :]; K_ = rkv[:, 2:4, :]; V_ = rkv[:, 4:6, :]
                hvb = lambda w: bc(hv[:, l, w, :].unsqueeze(2), [128, 2, 256])
                k.V(lambda h: h.tensor_tensor(out=kk[:], in0=K_, in1=hvb(0), op=ALU.mult), [rkv, hv], [kk])
                k.G(lambda h: h.tensor_tensor(out=kksq[:], in0=kk[:], in1=kk[:], op=ALU.mult), [kk], [kksq])
                bk = k.bank()
                for q in range(2):
                    k.mm(bk[:, q * 256:(q + 1) * 256], BONES[:], kksq[:, q, :], True, True, [BONES, kksq], [bk])
                k.V(lambda h: h.tensor_scalar(out=rn[:].rearrange("p q t -> p (q t)"), in0=bk[:, 0:512], scalar1=1e-24, scalar2=None,
                                              op0=ALU.max), [bk], [rn])
                rsq(k, rn[:], rn[:], 1.0, 0.0, [rn], rn)
                k.V(lambda h: h.tensor_tensor(out=kkn[:], in0=kk[:], in1=rn[:], op=ALU.mult), [kk, rn], [kkn])
                k.V(lambda h: h.tensor_scalar(out=kksq[:], in0=kkn[:], scalar1=-1.0, scalar2=None, op0=ALU.mult), [kkn, kksq], [kksq])
                bk = k.bank(); bk2 = k.bank()
                for q in range(2):
                    k.mm(bk[:, q * 256:(q + 1) * 256], dup[:, d, q * 128:(q + 1) * 128], lo[:, 0, :], True, True, [dup, lo], [bk])
                    k.mm(bk2[:, q * 256:(q + 1) * 256], iup[:, d, q * 128:(q + 1) * 128], lo[:, 1, :], True, True, [iup, lo], [bk2])
                for q in range(2):
                    k.A(lambda h, q=q: h.activation(out=sg[:, q, :], in_=bk[:, q * 256:(q + 1) * 256], func=AF.Sigmoid,
                                                    bias=dbias[:, l, d, q:q + 1], scale=1.0), [bk, dbias], [sg])
                    k.A(lambda h, q=q: h.activation(out=av[:, q, :], in_=bk2[:, q * 256:(q + 1) * 256], func=AF.Sigmoid,
                                                    bias=ibias[:, l, d, q:q + 1], scale=1.0), [bk2, ibias], [av])
                for q in range(2):
                    k.V(lambda h, q=q: h.tensor_tensor_scan(out=cs[:, q, :], data0=RST[:], data1=sg[:, q, :], initial=0.0,
                                                            op0=ALU.mult, op1=ALU.add), [RST, sg], [cs])
                k.V(lambda h: h.tensor_tensor(out=ex[:], in0=cs[:], in1=sg[:], op=ALU.subtract), [cs, sg], [ex])
                if d == 0:
                    Sin, Sex, Tin, Tex = cs, ex, cs, ex
                    gcol = 63
                else:
                    cs5 = cs[:].rearrange("p q (c t) -> p q c t", t=64)
                    k.V(lambda h: h.tensor_tensor(out=sbb[:].rearrange("p q (c t) -> p q c t", t=64),
                                                  in0=bc(cs5[:, :, :, 63:64], [128, 2, 4, 64]),
                                                  in1=ex[:].rearrange("p q (c t) -> p q c t", t=64), op=ALU.subtract), [cs, ex], [sbb])
                    k.V(lambda h: h.tensor_tensor(out=sbx[:], in0=sbb[:], in1=sg[:], op=ALU.subtract), [sbb, sg], [sbx])
                    Sin, Sex, Tin, Tex = sbb[:], sbx[:], sbb, sbx
                    gcol = 0
                SinA = Sin[:] if d == 0 else Sin
                SexA = Sex[:] if d == 0 else Sex
                k.A(lambda h: h.activation(out=eg[:], in_=SinA, func=AF.Exp, scale=-C0), [Tin], [eg])
                k.A(lambda h: h.activation(out=eng[:], in_=SinA, func=AF.Exp, scale=C0), [Tin], [eng])
                k.A(lambda h: h.activation(out=ega[:], in_=SexA, func=AF.Exp, scale=-C0), [Tex], [ega])
                k.V(lambda h: h.tensor_tensor(out=t1[:], in0=av[:], in1=hvb(1), op=ALU.mult), [av, hv], [t1])
                k.V(lambda h: h.tensor_tensor(out=t1[:], in0=t1[:], in1=hvb(1), op=ALU.subtract), [t1, hv], [t1])
                k.V(lambda h: h.tensor_scalar(out=t1[:], in0=t1[:], scalar1=1.0, scalar2=None, op0=ALU.add), [t1], [t1])
                k.V(lambda h: h.tensor_tensor(out=kd[:], in0=t1[:], in1=K_, op=ALU.mult), [t1, rkv], [kd])
                k.G(lambda h: h.tensor_tensor(out=tB[:], in0=av[:], in1=eng[:], op=ALU.mult), [av, eng], [tB])
                k.G(lambda h: h.tensor_tensor(out=t1[:], in0=kd[:], in1=hvb(2), op=ALU.mult), [kd, hv, t1], [t1])
                k.G(lambda h: h.tensor_tensor(out=t1[:], in0=t1[:], in1=R_, op=ALU.mult), [t1, rkv], [t1])
                bk = k.bank()
                for q in range(2):
                    k.mm(bk[:, q * 256:(q + 1) * 256], BONES[:], t1[:, q, :], True, True, [BONES, t1], [bk])
                k.V(lambda h: h.tensor_tensor(out=bon[:].rearrange("p q t -> p (q t)"), in0=bk[:, 0:512],
                                              in1=rkv[:, 4:6, :].rearrange("p q t -> p (q t)"), op=ALU.mult), [bk, rkv], [bon])
                k.store(YY[2 + d, :, c0:c0 + 256].rearrange("(q p) t -> p q t", p=128), bon[:], bon, YY)
                egv = eg[:].rearrange("p q (c t) -> p q c t", t=64)
                ei = 0
                for q in range(2):
                    for hh in range(2):
                        ps = slice(hh * 64, hh * 64 + 64)
                        v4 = lambda tl, lo_=0: tl[ps, :, lo_ + hh * 64:lo_ + hh * 64 + 64]
                        s4 = lambda tl: tl[ps, q, :].rearrange("p (c t) -> p c t", t=64)
                        r4 = lambda j: rkv[ps, j * 2 + q, :].rearrange("p (c t) -> p c t", t=64)
                        gam = bc(egv[ps, q, :, gcol:gcol + 1], [64, 4, 64])
                        E = [k.G, k.G]
                        def em(fn, reads, writes):
                            nonlocal ei
                            ei += 1
                            E[ei % 2](fn, reads, writes)
                        em(lambda h: h.tensor_tensor(out=v4(AR[q]), in0=s4(kksq), in1=s4(ega), op=ALU.mult), [kksq, ega], [AR[q]])
                        em(lambda h: h.tensor_tensor(out=v4(AR[q], 128), in0=r4(0), in1=s4(eg), op=ALU.mult), [rkv, eg], [AR[q]])
                        em(lambda h: h.tensor_tensor(out=v4(Bt[q]), in0=s4(kkn), in1=s4(tB), op=ALU.mult), [kkn, tB], [Bt[q]])
                        em(lambda h: h.tensor_tensor(out=v4(Kt[q]), in0=s4(kd), in1=s4(eng), op=ALU.mult), [kd, eng], [Kt[q]])
                        em(lambda h: h.tensor_tensor(out=v4(Bht[q]), in0=v4(Bt[q]), in1=gam, op=ALU.mult), [Bt[q], eg], [Bht[q]])
                        em(lambda h: h.tensor_tensor(out=v4(Kht[q]), in0=v4(Kt[q]), in1=gam, op=ALU.mult), [Kt[q], eg], [Kht[q]])
                        em(lambda h: h.tensor_copy(out=v4(Vt[q]), in_=r4(2)), [rkv], [Vt[q]])
                M2 = MF2 if d == 0 else MB2
                MN = MNF if d == 0 else MNB
                order = [0, 1, 2, 3] if d == 0 else [3, 2, 1, 0]
                def qbody(q):
                    MA, MB, Pq, PTq, Xq = MAs[q], MBs[q], Pqs[q], PTqs[q], Xqs[q]
                    VTk, BhT, KhT, RTp, YL, MCT, NCt = VTks[q], BhTs[q], KhTs[q], RTps[q], YLs[q], MCTs[q], NCts[q]
                    for (lt, dst) in ((Bt[q], MA), (Kt[q], MB)):
                        for hf in range(2):
                            bk = k.bank()
                            for cc in range(2):
                                c = hf * 2 + cc
                                k.mm(bk[:, cc * 256:(cc + 1) * 256], lt[:, c, :], AR[q][:, c, :], True, True, [lt, AR[q]], [bk])
                            k.V(lambda h, bk=bk, hf=hf, dst=dst: h.tensor_tensor(out=dst[:, hf * 2:hf * 2 + 2, :],
                                in0=bk[:, 0:512].rearrange("p (c n) -> p c n", n=256), in1=bc(M2[:], [128, 2, 256]), op=ALU.mult), [bk, M2], [dst])
                    P0 = Pq[0]
                    bk = k.bank()
                    for c in range(4):
                        k.mm(bk[:, c * 128:(c + 1) * 128], AR[q][:, c, 0:128], Bt[q][:, c, :], True, True, [AR[q], Bt[q]], [bk])
                    k.V(lambda h, bk=bk: h.tensor_tensor(out=P0[:], in0=bk[:, 0:512].rearrange("p (c n) -> p c n", n=128), in1=bc(MN[:], [128, 4, 128]), op=ALU.mult), [bk, MN], [P0])
                    yield
                    X = Xq[0]
                    def tr(src_ap_fn, srcT, dst_ap, dstT):
                        bk = k.bank()
                        for c in range(4):
                            k.mm(bk[:, c * 128:(c + 1) * 128], src_ap_fn(c), IDB[:], True, True, [srcT, IDB], [bk])
                        k.cp(dst_ap, bk[:, 0:512].rearrange("p (c n) -> p c n", n=128), [bk], [dstT])
                    tr(lambda c: AR[q][:, c, 0:128], AR[q], X[:, :, 0:128], X)
                    yield
                    tr(lambda c: Vt[q][:, c, :], Vt[q], VTk[:], VTk)
                    tr(lambda c: Bht[q][:, c, :], Bht[q], BhT[:], BhT)
                    yield
                    tr(lambda c: Kht[q][:, c, :], Kht[q], KhT[:], KhT)
                    yield
                    bk = k.bank()
                    for c in range(4):
                        k.mm(bk[:, c * 128:(c + 1) * 128], MB[:, c, 0:128], VTk[:, c, :], True, True, [MB, VTk], [bk])
                    k.cp(X[:, :, 128:256], bk[:, 0:512].rearrange("p (c n) -> p c n", n=128), [bk], [X])
                    yield
                    xi = 0; pi = 0
                    Pc = P0; PTc_ap = lambda c: MA[:, c, 0:128]; PTc_T = MA
                    for it in range(6):
                        Xn = Xq[1 - xi]
                        for hf in range(2):
                            bk = k.bank()
                            for cc in range(2):
                                c = hf * 2 + cc
                                k.mm(bk[:, cc * 256:(cc + 1) * 256], PTc_ap(c), Xq[xi][:, c, :], True, True, [PTc_T, Xq[xi]], [bk])
                            k.V(lambda h, bk=bk, hf=hf, Xn=Xn, Xo=Xq[xi]: h.tensor_tensor(out=Xn[:, hf * 2:hf * 2 + 2, :],
                                in0=bk[:, 0:512].rearrange("p (c n) -> p c n", n=256), in1=Xo[:, hf * 2:hf * 2 + 2, :], op=ALU.add), [bk, Xq[xi]], [Xn])
                        xi = 1 - xi
                        yield
                        if it < 5:
                            Pn = Pq[1 - pi]; PTn = PTq[1 - pi]
                            bk = k.bank(); bk2 = k.bank()
                            for c in range(4):
                                k.mm(bk[:, c * 128:(c + 1) * 128], PTc_ap(c), Pc[:, c, :], True, True, [PTc_T, Pc], [bk])
                                k.mm(bk2[:, c * 128:(c + 1) * 128], Pc[:, c, :], PTc_ap(c), True, True, [PTc_T, Pc], [bk2])
                            k.cp(Pn[:], bk[:, 0:512].rearrange("p (c n) -> p c n", n=128), [bk], [Pn])
                            k.cp(PTn[:], bk2[:, 0:512].rearrange("p (c n) -> p c n", n=128), [bk2], [PTn])
                            pi = 1 - pi
                            Pc = Pn; PTc_T = PTn; PTc_ap = (lambda c, PTn=PTn: PTn[:, c, :])
                            yield
                    X = Xq[xi]
                    yield
                    bk = k.bank(); bk2 = k.bank()
                    for c in range(4):
                        k.mm(bk[:, c * 128:(c + 1) * 128], X[:, c, 0:128], MA[:, c, 128:256], True, True, [X, MA], [bk])
                    for c in range(4):
                        k.mm(bk2[:, c * 128:(c + 1) * 128], X[:, c, 128:256], MA[:, c, 128:256], True, False, [X, MA], [bk2])
                        k.mm(bk2[:, c * 128:(c + 1) * 128], VTk[:, c, :], MB[:, c, 128:256], False, True, [VTk, MB], [bk2])
                    k.V(lambda h, bk=bk: h.tensor_tensor(out=RTp[:], in0=bk[:, 0:512].rearrange("p (c n) -> p c n", n=128),
                                                         in1=AR[q][:, :, 128:256], op=ALU.add), [bk, AR[q]], [RTp])
                    k.cp(YL[:], bk2[:, 0:512].rearrange("p (c n) -> p c n", n=128), [bk2], [YL])
                    yield
                    bk = k.bank(); bk2 = k.bank()
                    for c in range(4):
                        k.mm(bk[:, c * 128:(c + 1) * 128], X[:, c, 0:128], BhT[:, c, :], True, True, [X, BhT], [bk])
                    for c in range(4):
                        k.mm(bk2[:, c * 128:(c + 1) * 128], BhT[:, c, :], X[:, c, 128:256], True, False, [X, BhT], [bk2])
                        k.mm(bk2[:, c * 128:(c + 1) * 128], KhT[:, c, :], VTk[:, c, :], False, True, [KhT, VTk], [bk2])
                    for c in range(4):
                        k.V(lambda h, c=c, bk=bk: h.scalar_tensor_tensor(out=MCT[:, c, :], in0=IDENT[:], scalar=eg[:, q, c * 64 + gcol:c * 64 + gcol + 1],
                            in1=bk[:, c * 128:(c + 1) * 128], op0=ALU.mult, op1=ALU.add), [IDENT, eg, bk], [MCT])
                    k.cp(NCt[:], bk2[:, 0:512].rearrange("p (c n) -> p c n", n=128), [bk2], [NCt])
                    yield
                    for c in order:
                        zc = Z[(q, d)][zi[(q, d)]]; zn = Z[(q, d)][1 - zi[(q, d)]]
                        bk = k.bank(); bk2 = k.bank()
                        k.mm(bk[:, 0:128], zc[:], RTp[:, c, :], True, True, [zc, RTp], [bk])
                        k.mm(bk2[:, 0:128], MCT[:, c, :], zc[:], True, True, [zc, MCT], [bk2])
                        for hh in range(2):
                            ps = slice(hh * 64, hh * 64 + 64)
                            k.V(lambda h, ps=ps, hh=hh, c=c, bk=bk: h.tensor_tensor(out=yst[ps, q, c * 64:(c + 1) * 64],
                                in0=bk[ps, hh * 64:hh * 64 + 64], in1=YL[ps, c, hh * 64:hh * 64 + 64], op=ALU.add), [bk, YL], [yst])
                        k.V(lambda h, c=c, bk2=bk2, zn=zn: h.tensor_tensor(out=zn[:], in0=bk2[:, 0:128], in1=NCt[:, c, :], op=ALU.add), [bk2, NCt], [zn])
                        zi[(q, d)] = 1 - zi[(q, d)]
                        yield
                gens = [qbody(0), qbody(1)]
                alive = [True, True]
                while any(alive):
                    for gi_ in range(2):
                        if alive[gi_]:
                            try:
                                next(gens[gi_])
                            except StopIteration:
                                alive[gi_] = False
                k.store(YY[d, :, c0:c0 + 256].rearrange("(q p) t -> p q t", p=128), yst[:], yst, YY)

            fseq = [CTX0] + [LAT0 + 256 * i for i in range(64)]
            bseq = [CTX0] + [LAT0 + 256 * i for i in reversed(range(64))]
            for i in range(65):
                scan_pass(fseq[i], 0)
                scan_pass(bseq[i], 1)
        k.barrier()
        if dbg and "YY" in dbg and l == 0:
            k.dma(k.pool, dbg_t["YY"][:], YY[:], [YY], [dbg_t["YY"]])
            k.barrier()

        with ExitStack() as es:
            gup = k.sb(es, "gup", [128, 2, 256]); poolw = k.sb(es, "poolw", [128, 2, 256]); wo = k.sb(es, "wo", [128, 10, D], BF16)
            k.load(gup[:], gup_in[:, l], gup_in, gup); k.load(poolw[:], poolw_in[:, l], poolw_in, poolw)
            k.load(wo[:], wout_in[l], wout_in, wo, cast=True)
            yy = k.sb(es, "yy", [128, 4, 2, 256]); glo = k.sb(es, "glo", [128, 2, 256])
            up = k.sb(es, "up", [128, 2, 20, 80]); pa = k.sb(es, "pa", [128, 2, 20, 80]); pb = k.sb(es, "pb", [128, 2, 20, 80])
            va = k.sb(es, "va", [128, 2, 20, 64]); vb = k.sb(es, "vb", [128, 2, 20, 64])
            cu = k.sb(es, "cu", [128, 2, 288]); ca = k.sb(es, "ca", [128, 2, 288]); cb = k.sb(es, "cb", [128, 2, 288])
            yv = k.sb(es, "yv", [128, 2, 256]); dv = k.sb(es, "dv", [128, 2, 256]); dsq = k.sb(es, "dsq", [128, 2, 256])
            mu = k.sb(es, "mu", [128, 2, 256]); dd = mu; tmpd = dsq
            mq = k.sb(es, "mq", [128, 10, 256], BF16); ost = k.sb(es, "ost", [128, 4, 256])
            for tl in (up, pa, pb, va, vb, cu, ca, cb):
                k.G(lambda h, tl=tl: h.memset(tl[:], 0.0), [], [tl])
            hvb = lambda w: bc(hv[:, l, w, :].unsqueeze(2), [128, 2, 256])
            GRP = [(0, slice(0, 64)), (0, slice(64, 128)), (1, slice(0, 64)), (1, slice(64, 128))]

            def blk3c(c0, kind, bi):
                k.load(yy[:], YY[:, :, c0:c0 + 256].rearrange("w (q p) t -> p w q t", p=128), YY, yy)
                k.load(glo[:], UT[960:1216, c0:c0 + 256].rearrange("(q p) t -> p q t", p=128), UT, glo)
                k.V(lambda h: h.tensor_tensor(out=yv[:], in0=yy[:, 0], in1=yy[:, 1], op=ALU.add), [yy], [yv])
                bk = k.bank()
                for q in range(2):
                    k.mm(bk[:, q * 256:(q + 1) * 256], BONES[:], yv[:, q, :], True, True, [BONES, yv], [bk])
                k.V(lambda h: h.scalar_tensor_tensor(out=dv[:].rearrange("p q t -> p (q t)"), in0=bk[:, 0:512], scalar=-1.0 / 64,
                    in1=yv[:].rearrange("p q t -> p (q t)"), op0=ALU.mult, op1=ALU.add), [bk, yv], [dv])
                k.G(lambda h: h.tensor_tensor(out=dsq[:], in0=dv[:], in1=dv[:], op=ALU.mult), [dv], [dsq])
                bk = k.bank()
                for q in range(2):
                    k.mm(bk[:, q * 256:(q + 1) * 256], BONES[:], dsq[:, q, :], True, True, [BONES, dsq], [bk])
                rsq(k, mu[:].rearrange("p q t -> p (q t)"), bk[:, 0:512], 1.0 / 64, 64e-5, [bk], mu)
                k.V(lambda h: h.tensor_tensor(out=dv[:], in0=dv[:], in1=mu[:], op=ALU.mult), [dv, mu], [dv])
                k.V(lambda h: h.tensor_tensor(out=dv[:], in0=dv[:], in1=hvb(3), op=ALU.mult), [dv, hv], [dv])
                k.V(lambda h: h.tensor_tensor(out=dv[:], in0=dv[:], in1=hvb(4), op=ALU.add), [dv, hv], [dv])
                k.V(lambda h: h.tensor_tensor(out=dv[:], in0=dv[:], in1=yy[:, 2], op=ALU.add), [dv, yy], [dv])
                k.V(lambda h: h.tensor_tensor(out=dv[:], in0=dv[:], in1=yy[:, 3], op=ALU.add), [dv, yy], [dv])
                bk = k.bank()
                for q in range(2):
                    for kc in range(2):
                        k.mm(bk[:, q * 256:(q + 1) * 256], gup[:, kc, q * 128:(q + 1) * 128], glo[:, kc, :], kc == 0, kc == 1, [gup, glo], [bk])
                k.V(lambda h: h.tensor_tensor(out=mq[:, 0:2, :].rearrange("p q t -> p (q t)"), in0=bk[:, 0:512],
                                              in1=dv[:].rearrange("p q t -> p (q t)"), op=ALU.mult), [bk, dv], [mq])
                if kind == "lat":
                    r0 = bi * 4
                    k.load(up[:, 0, :, 8:72], UT[1216:1344, c0 - 512:c0 + 768].rearrange("p (r c) -> p r c", c=64), UT, up)
                    k.load(up[:, 1, :, 8:72], UT[1344:1472, c0 - 512:c0 + 768].rearrange("p (r c) -> p r c", c=64), UT, up)
                    k.V(lambda h: h.tensor_tensor(out=pb[:, :, :, 1:80], in0=up[:, :, :, 0:79], in1=up[:, :, :, 1:80], op=ALU.add), [up], [pb])
                    k.V(lambda h: h.tensor_tensor(out=pa[64:128, 0, :, 2:79], in0=pb[64:128, 0, :, 1:78], in1=pb[64:128, 0, :, 3:80], op=ALU.add), [pb], [pa])
                    k.V(lambda h: h.tensor_tensor(out=pa[:, 1, :, 2:79], in0=pb[:, 1, :, 1:78], in1=pb[:, 1, :, 3:80], op=ALU.add), [pb], [pa])
                    k.V(lambda h: h.tensor_tensor(out=pb[:, 1, :, 4:77], in0=pa[:, 1, :, 2:75], in1=pa[:, 1, :, 6:79], op=ALU.add), [pa], [pb])
                    k.V(lambda h: h.tensor_tensor(out=pa[64:128, 1, :, 8:73], in0=pb[64:128, 1, :, 4:69], in1=pb[64:128, 1, :, 12:77], op=ALU.add), [pb], [pa])
                    hs = [pb, pa, pb, pa]
                    for gi, (pc, ps) in enumerate(GRP):
                        k.G(lambda h, gi=gi, pc=pc, ps=ps: h.tensor_tensor(out=va[ps, pc, 1:20, :], in0=hs[gi][ps, pc, 0:19, 8:72],
                                                                          in1=hs[gi][ps, pc, 1:20, 8:72], op=ALU.add), [pa, pb], [va])
                    k.G(lambda h: h.tensor_tensor(out=vb[64:128, 0, 2:19, :], in0=va[64:128, 0, 1:18, :], in1=va[64:128, 0, 3:20, :], op=ALU.add), [va], [vb])
                    k.G(lambda h: h.tensor_tensor(out=vb[:, 1, 2:19, :], in0=va[:, 1, 1:18, :], in1=va[:, 1, 3:20, :], op=ALU.add), [va], [vb])
                    k.G(lambda h: h.tensor_tensor(out=va[:, 1, 4:17, :], in0=vb[:, 1, 2:15, :], in1=vb[:, 1, 6:19, :], op=ALU.add), [vb], [va])
                    k.G(lambda h: h.tensor_tensor(out=vb[64:128, 1, 8:13, :], in0=va[64:128, 1, 4:9, :], in1=va[64:128, 1, 12:17, :], op=ALU.add), [va], [vb])
                    vs = [va, vb, va, vb]
                    var = 0 if bi == 0 else (1 if bi == 1 else (3 if bi == 62 else (4 if bi == 63 else 2)))
                    for gi, (pc, ps) in enumerate(GRP):
                        d4 = lambda tl: tl[ps, pc, :].rearrange("p (r c) -> p r c", c=64)
                        k.V(lambda h: h.tensor_tensor(out=d4(tmpd), in0=vs[gi][ps, pc, 8:12, :], in1=rc2[ps, var, pc], op=ALU.mult), [va, vb, rc2], [tmpd])
                        k.V(lambda h: h.tensor_tensor(out=d4(dd), in0=d4(tmpd), in1=up[ps, pc, 8:12, 8:72], op=ALU.subtract), [tmpd, up], [dd])
                else:
                    k.load(cu[:, 0, 16:272], UT[1216:1344, c0:c0 + 256], UT, cu)
                    k.load(cu[:, 1, 16:272], UT[1344:1472, c0:c0 + 256], UT, cu)
                    k.V(lambda h: h.tensor_tensor(out=cb[:, :, 1:288], in0=cu[:, :, 0:287], in1=cu[:, :, 1:288], op=ALU.add), [cu], [cb])
                    k.V(lambda h: h.tensor_tensor(out=ca[64:128, 0, 2:287], in0=cb[64:128, 0, 1:286], in1=cb[64:128, 0, 3:288], op=ALU.add), [cb], [ca])
                    k.V(lambda h: h.tensor_tensor(out=ca[:, 1, 2:287], in0=cb[:, 1, 1:286], in1=cb[:, 1, 3:288], op=ALU.add), [cb], [ca])
                    k.V(lambda h: h.tensor_tensor(out=cb[:, 1, 4:285], in0=ca[:, 1, 2:283], in1=ca[:, 1, 6:287], op=ALU.add), [ca], [cb])
                    k.V(lambda h: h.tensor_tensor(out=ca[64:128, 1, 8:281], in0=cb[64:128, 1, 4:277], in1=cb[64:128, 1, 12:285], op=ALU.add), [cb], [ca])
                    hs = [cb, ca, cb, ca]
                    for gi, (pc, ps) in enumerate(GRP):
                        k.V(lambda h: h.tensor_tensor(out=tmpd[ps, pc, :], in0=hs[gi][ps, pc, 16:272], in1=rc1[ps, pc, :], op=ALU.mult), [ca, cb, rc1], [tmpd])
                        k.V(lambda h: h.tensor_tensor(out=dd[ps, pc, :], in0=tmpd[ps, pc, :], in1=cu[ps, pc, 16:272], op=ALU.subtract), [tmpd, cu], [dd])
                for gi, (pc, ps) in enumerate(GRP):
                    for nn in range(2):
                        bk = k.bank()
                        k.mm(bk[:, 0:256], poolw[ps, pc, nn * 128:(nn + 1) * 128], dd[ps, pc, :], True, True, [poolw, dd], [bk])
                        k.A(lambda h, bk=bk, gi=gi, nn=nn: h.activation(out=mq[:, 2 + gi * 2 + nn, :], in_=bk[:, 0:256], func=AF.Copy,
                                                                       scale=pscale[:, l, gi * 2 + nn:gi * 2 + nn + 1]), [bk, pscale], [mq])
                for n4 in range(4):
                    for nn in range(4):
                        n_ = n4 * 4 + nn
                        bk = k.bank()
                        for kc in range(10):
                            k.mm(bk[:, 0:256], wo[:, kc, n_ * 128:(n_ + 1) * 128], mq[:, kc, :], kc == 0, kc == 9, [wo, mq], [bk])
                        k.cp(ost[:, nn, :], bk[:, 0:256], [bk], [ost])
                    if kind == "lat":
                        r = bi // 16; lc = 64 + (bi % 16) * 256
                        cdma(False, PT, r * D + n4 * 512, 512, lambda off, ln: ost[:, :, off:off + ln], ost, lc, 256)
                    else:
                        for r in range(4):
                            cdma(False, PT, r * D + n4 * 512, 512, lambda off, ln, r=r: ost[:, :, r * 64 + off:r * 64 + off + ln], ost, 0, 64)

            blk3c(CTX0, "ctx", 0)
            for bi in range(64):
                blk3c(LAT0 + 256 * bi, "lat", bi)
            for j in range(NCH):
                coll("ReduceScatter", ALU.add, PT, DT, PT[j], DT[j])
        k.barrier()
        FT = 416
        with ExitStack() as es:
            xt = k.sb(es, "f_x", [128, 16, FT]); h2 = k.sb(es, "f_h", [128, 16, FT], BF16); sq = k.sb(es, "f_sq", [128, 16, FT])
            rs = k.sb(es, "f_rs", [128, FT]); ff = k.sb(es, "f_f", [128, FC, FT], BF16); sl = k.sb(es, "f_sl", [128, FT])
            wgs = [k.sb(es, "f_wg%d" % i, [128, 16, 128], BF16) for i in range(2)]; wus = [k.sb(es, "f_wu%d" % i, [128, 16, 128], BF16) for i in range(2)]
            wds = [k.sb(es, "f_wd%d" % i, [128, FC, 128], BF16) for i in range(2)]
            last = (l == L - 1)
            for i in range(NTL // FT):
                lt0 = i * FT
                k.load(xt[:], xrows(XS, lt0, lt0 + FT), XS, xt)
                cdma(True, DT, 0, D, lambda off, ln: sq[:, :, off:off + ln], sq, lt0, FT)
                modulate(sq, sq, lt0, FT, l, 2, None, [])
                k.V(lambda h: h.tensor_tensor(out=xt[:], in0=xt[:], in1=sq[:], op=ALU.add), [xt, sq], [xt])
                k.A(lambda h: h.activation(out=sq[:], in_=xt[:], func=AF.Square), [xt], [sq])
                bk = k.bank()
                for kc in range(16):
                    k.mm(bk[:, 0:FT], ONES[:], sq[:, kc, :], kc == 0, kc == 15, [ONES, sq], [bk])
                rsq(k, rs[:], bk[:, 0:FT], 1.0 / D, 1e-6, [bk], rs)
                k.V(lambda h: h.tensor_tensor(out=sq[:], in0=xt[:], in1=bc(rs[:].unsqueeze(1), [128, 16, FT]), op=ALU.mult), [xt, rs], [sq])
                modulate(sq, sq, lt0, FT, l, 4, 3, [], out_tile=h2, outT=h2)
                for j in range(FC):
                    wg = wgs[j % 2]; wu = wus[j % 2]
                    k.load(wg[:], FGB[j], FGB, wg)
                    k.load(wu[:], FUB[j], FUB, wu)
                    bk = k.bank(); bk2 = k.bank()
                    for kc in range(16):
                        k.mm(bk[:, 0:FT], wg[:, kc, :], h2[:, kc, :], kc == 0, kc == 15, [wg, h2], [bk])
                    for kc in range(16):
                        k.mm(bk2[:, 0:FT], wu[:, kc, :], h2[:, kc, :], kc == 0, kc == 15, [wu, h2], [bk2])
                    k.A(lambda h, bk=bk: h.activation(out=sl[:], in_=bk[:, 0:FT], func=AF.Silu), [bk], [sl])
                    k.V(lambda h, bk2=bk2, j=j: h.tensor_tensor(out=ff[:, j, :], in0=bk2[:, 0:FT], in1=sl[:], op=ALU.mult), [bk2, sl], [ff])
                for n_ in range(16):
                    wd = wds[n_ % 2]
                    k.load(wd[:], FDB[n_], FDB, wd)
                    bk = k.bank()
                    for fc in range(FC):
                        k.mm(bk[:, 0:FT], wd[:, fc, :], ff[:, fc, :], fc == 0, fc == FC - 1, [wd, ff], [bk])
                    segs = [(0, 64, 1), (64, FT, 0)] if lt0 == 0 else [(0, FT, 0)]
                    for (a, b_, m) in segs:
                        k.V(lambda h, a=a, b_=b_, m=m, n_=n_, bk=bk: h.scalar_tensor_tensor(out=xt[:, n_, a:b_], in0=bk[:, a:b_],
                            scalar=DER[:, l, 5, n_, m:m + 1], in1=xt[:, n_, a:b_], op0=ALU.mult, op1=ALU.add), [bk, DER, xt], [xt])
                if not last:
                    k.store(xrows(XT, lt0, lt0 + FT), xt[:], xt, XT)
                else:
                    k.A(lambda h: h.activation(out=sq[:], in_=xt[:], func=AF.Square), [xt], [sq])
                    bk = k.bank()
                    for kc in range(16):
                        k.mm(bk[:, 0:FT], ONES[:], sq[:, kc, :], kc == 0, kc == 15, [ONES, sq], [bk])
                    rsq(k, rs[:], bk[:, 0:FT], 1.0 / D, 1e-6, [bk], rs)
                    k.V(lambda h: h.tensor_tensor(out=xt[:], in0=xt[:], in1=bc(rs[:].unsqueeze(1), [128, 16, FT]), op=ALU.mult), [xt, rs], [xt])
                    k.V(lambda h: h.tensor_tensor(out=xt[:], in0=xt[:], in1=bc(fin[:].unsqueeze(2), [128, 16, FT]), op=ALU.mult), [xt, fin], [xt])
                    k.store(xrows(out_d, lt0, lt0 + FT), xt[:], xt, out_d)
        k.barrier()
        XS = XT
    return k


def _host_inputs(inp, depth=4):
    f = np.float32
    L = depth
    x = inp["x"]; ctx = inp["ctx"]
    ORK = [0, 1024, 2048]
    maps = []
    rc2 = np.zeros((128, 5, 2, 4, 64), f); rc1 = np.zeros((128, 2, 256), f)
    for pc in range(2):
        for half in range(2):
            gi = pc * 2 + half; W = 2 ** (gi + 1); hf = W // 2
            ps = slice(half * 64, half * 64 + 64)
            cc = np.arange(64); ccnt = np.clip(cc + hf, 0, 64) - np.clip(cc - hf, 0, 64)
            for var, r0 in enumerate([0, 4, 128, 248, 252]):
                rr = np.arange(r0, r0 + 4); rcnt = np.clip(rr + hf, 0, 256) - np.clip(rr - hf, 0, 256)
                rc2[ps, var, pc] = (1.0 / (rcnt[:, None] * ccnt[None, :])).astype(f)[None]
            tt = np.arange(256); tcnt = np.clip(tt + hf, 0, 256) - np.clip(tt - hf, 0, 256)
            rc1[ps, pc] = (1.0 / tcnt).astype(f)[None]
    pm = lambda v: np.ascontiguousarray(v.reshape(-1, 128).T)
    FG = np.ascontiguousarray(inp["ffn_gate"][:L].reshape(L, 16, 128, FC, 128).transpose(0, 3, 2, 1, 4))
    FU = np.ascontiguousarray(inp["ffn_up"][:L].reshape(L, 16, 128, FC, 128).transpose(0, 3, 2, 1, 4))
    FD = np.ascontiguousarray(inp["ffn_down"][:L].reshape(L, FC, 128, 16, 128).transpose(0, 3, 2, 1, 4))
    for c in range(8):
        b, s_ = c // 4, c % 4
        g = s_
        m = {}
        m["xT"] = np.ascontiguousarray(np.concatenate([ctx[b, 64 * s_:64 * s_ + 64], x[b, 4096 * s_:4096 * s_ + 4096]], 0).T)
        cT = np.zeros((128, 16, 2), f); cT[:, :, 0] = pm(inp["c"][b]); cT[:, :, 1] = pm(inp["c_ctx"]); m["cT"] = cT
        m["adaw"] = np.ascontiguousarray(inp["ada_w"][:L, :, 3072 * s_:3072 * s_ + 3072])
        m["adab"] = np.ascontiguousarray(np.stack([pm(inp["ada_b"][l, 3072 * s_:3072 * s_ + 3072]) for l in range(L)], 1))
        nr = np.zeros((128, 2, L, 16), f)
        for l in range(L):
            nr[:, 0, l] = pm(inp["norm_mix"][l]); nr[:, 1, l] = pm(inp["norm_ffn"][l])
        m["nrm"] = nr; m["fin"] = pm(inp["final_norm"])
        cols = []
        for o in ORK:
            cols += list(range(o + g * 256, o + g * 256 + 256))
        cols += list(range(3072, 3072 + 448))
        for gi in range(4):
            cols += list(range(3520 + gi * 256 + s_ * 64, 3520 + gi * 256 + s_ * 64 + 64))
        cols = np.array(cols)
        m["win"] = np.ascontiguousarray(inp["w_in"][:L][:, :, cols].reshape(L, 16, 128, 1472).transpose(0, 2, 1, 3))
        cw = np.zeros((128, L, 6, 3), f)
        for l in range(L):
            for wh in range(3):
                for q in range(2):
                    cw[:, l, wh * 2 + q, :] = inp["conv_rkv"][l][:, wh * 1024 + g * 256 + q * 128: wh * 1024 + g * 256 + q * 128 + 128].T
        m["convw"] = cw
        hsl = lambda a, l, q: a[l][g * 256 + q * 128: g * 256 + q * 128 + 128]
        db = np.zeros((128, L, 2, 2), f); ib = np.zeros((128, L, 2, 2), f); hvv = np.zeros((128, L, 5, 2), f)
        for l in range(L):
            for q in range(2):
                for d in range(2):
                    db[:, l, d, q] = inp["decay_bias"][l, d, g * 256 + q * 128: g * 256 + q * 128 + 128]
                    ib[:, l, d, q] = inp["iclr_bias"][l, d, g * 256 + q * 128: g * 256 + q * 128 + 128]
                for wi, nm in enumerate(["k_k", "k_a", "r_k", "gn_w", "gn_b"]):
                    hvv[:, l, wi, q] = hsl(inp[nm], l, q)
        m["dbias"] = db; m["ibias"] = ib; m["hv"] = hvv
        m["dup"] = np.ascontiguousarray(np.transpose(inp["decay_up"][:L, :, :, g * 256:g * 256 + 256], (2, 0, 1, 3)))
        m["iup"] = np.ascontiguousarray(np.transpose(inp["iclr_up"][:L, :, :, g * 256:g * 256 + 256], (2, 0, 1, 3)))
        gu = inp["gate_up"][:L, :, g * 256:g * 256 + 256].reshape(L, 2, 128, 256)
        m["gup"] = np.ascontiguousarray(np.transpose(gu, (2, 0, 1, 3)))
        pw = np.zeros((128, L, 2, 256), f)
        for l in range(L):
            for pc in range(2):
                for half in range(2):
                    pw[half * 64:half * 64 + 64, l, pc] = inp["pool_w"][l, pc * 2 + half, s_ * 64:s_ * 64 + 64, :]
        m["poolw"] = pw
        m["pscale"] = np.ascontiguousarray(np.stack([pm(inp["pool_scale"][l]) for l in range(L)], 1))
        m["wout"] = np.ascontiguousarray(np.concatenate([inp["w_out"][:L, g * 256:g * 256 + 256], inp["w_out"][:L, 1024:2048]], 1).reshape(L, 10, 128, D).transpose(0, 2, 1, 3))
        m["fg"] = FG; m["fu"] = FU; m["fd"] = FD
        m["rc2"] = rc2; m["rc1"] = rc1
        maps.append({kk_: np.ascontiguousarray(v, dtype=f) for kk_, v in m.items()})
    return maps


def kernel(**inputs):
    inp = {k_: np.asarray(v) for k_, v in inputs.items()}
    k = build(4)
    maps = _host_inputs(inp, 4)
    res = run_bass_kernel_spmd(k.nc, maps, core_ids=list(range(8)))
    out = np.zeros((2, 16384, 2048), np.float32)
    for c in range(8):
        b, s_ = c // 4, c % 4
        o = res.results[c]["out"]
        out[b, 4096 * s_:4096 * s_ + 4096] = o[:, 64:].T
    return out
```
